# Optimizing a Trainium2 kernel written in Bass

```python
import math
import jax, jax.numpy as jnp
from jax import lax
import numpy as np

D_MODEL = 1024
BATCH = 16
SEQ = 4096
DEPTH = 2
DEC_BATCH = 8
DEC_SEQ = 8192
PAST_LEN = 128

BLOCK = 128
EPS = 1e-6
A_HEADS = 4
A_HEAD_DIM = 64
A_VDIM = 2 * A_HEAD_DIM
A_QK = A_HEADS * 2 * A_HEAD_DIM
A_WIDTH = A_HEADS * A_VDIM
B_HEADS = 8
B_KV_HEADS = 2
B_GROUP = B_HEADS // B_KV_HEADS
B_HEAD_DIM = 64
B_WIDTH = B_HEADS * B_HEAD_DIM
B_KV = B_KV_HEADS * B_HEAD_DIM
WINDOW = 128
D_FF = 2816
CONV_WIDTH = 3
SPLIT_SIZES = (A_QK, A_QK, A_WIDTH, B_WIDTH, B_KV, B_KV, D_MODEL, D_MODEL)
SPLIT_IDX = tuple(int(i) for i in np.cumsum(SPLIT_SIZES)[:-1])
IN_COLS = int(sum(SPLIT_SIZES))
NEG = -1e30

kernel_name = "hybrid_diffattn_swa_sink_convffn_encoder"


def rmsnorm(x, g):
    xf = x.astype(jnp.float32)
    y = xf * lax.rsqrt(jnp.mean(xf * xf, axis=-1, keepdims=True) + EPS)
    return (y * g.astype(jnp.float32)).astype(x.dtype)


def alibi_slopes(n):
    return jnp.asarray([2.0 ** (-8.0 * (h + 1) / n) for h in range(n)], dtype=jnp.float32)


def diff_attention(q, k, v, lam, lam_init, sub_g):
    B, S = q.shape[0], q.shape[1]
    nb = S // BLOCK
    scale = A_HEAD_DIM ** -0.5
    slopes = alibi_slopes(A_HEADS)[:, None, None, None]
    kpos = jnp.arange(S, dtype=jnp.float32)
    qb = q.reshape(B, nb, BLOCK, A_HEADS, 2, A_HEAD_DIM).transpose(1, 0, 2, 3, 4, 5)

    def one_block(args):
        qi, i = args
        s = jnp.einsum('bqhcd,bkhcd->bhcqk', qi, k).astype(jnp.float32) * scale
        qpos = (i * BLOCK + jnp.arange(BLOCK)).astype(jnp.float32)
        dist = jnp.abs(qpos[:, None] - kpos[None, :])
        p = jax.nn.softmax(s - slopes * dist, axis=-1)
        a = p[:, :, 0] - lam * p[:, :, 1]
        return jnp.einsum('bhqk,bkhe->bqhe', a.astype(v.dtype), v)

    o = lax.map(one_block, (qb, jnp.arange(nb)))
    o = o.transpose(1, 0, 2, 3, 4).reshape(B, S, A_HEADS, A_VDIM)
    o = rmsnorm(o, sub_g) * (1.0 - lam_init)
    return o.reshape(B, S, A_WIDTH)


def window_attention(q, k, v, sink):
    B, S = q.shape[0], q.shape[1]
    nb = S // BLOCK
    scale = B_HEAD_DIM ** -0.5
    kp = jnp.pad(k, ((0, 0), (BLOCK, BLOCK), (0, 0), (0, 0)))
    vp = jnp.pad(v, ((0, 0), (BLOCK, BLOCK), (0, 0), (0, 0)))
    qb = q.reshape(B, nb, BLOCK, B_KV_HEADS, B_GROUP, B_HEAD_DIM).transpose(1, 0, 2, 3, 4, 5)
    slopes = alibi_slopes(B_HEADS).reshape(B_KV_HEADS, B_GROUP)[:, :, None, None]
    sink_g = sink.astype(jnp.float32).reshape(B_KV_HEADS, B_GROUP)[:, :, None, None]
    qi_idx = jnp.arange(BLOCK)
    kj_idx = jnp.arange(3 * BLOCK)
    rel = kj_idx[None, :] - qi_idx[:, None]
    band = (rel >= BLOCK - WINDOW) & (rel <= BLOCK + WINDOW)
    dist = jnp.abs(rel - BLOCK).astype(jnp.float32)

    def one_block(args):
        qi, i = args
        kw = lax.dynamic_slice_in_dim(kp, i * BLOCK, 3 * BLOCK, axis=1)
        vw = lax.dynamic_slice_in_dim(vp, i * BLOCK, 3 * BLOCK, axis=1)
        kpos = (i - 1) * BLOCK + kj_idx
        valid = band & ((kpos >= 0) & (kpos < S))[None, :]
        s = jnp.einsum('bqgrd,bkgd->bgrqk', qi, kw).astype(jnp.float32) * scale - slopes * dist
        s = jnp.where(valid, s, NEG)
        m = jnp.maximum(jnp.max(s, axis=-1, keepdims=True), sink_g)
        e = jnp.exp(s - m)
        p = e / (jnp.sum(e, axis=-1, keepdims=True) + jnp.exp(sink_g - m))
        return jnp.einsum('bgrqk,bkgd->bqgrd', p.astype(v.dtype), vw)

    o = lax.map(one_block, (qb, jnp.arange(nb)))
    return o.transpose(1, 0, 2, 3, 4, 5).reshape(B, S, B_WIDTH)


def dwconv_centred(a, w, b):
    ap = jnp.pad(a, ((0, 0), (1, 1), (0, 0)))
    return ap[:, :-2] * w[0] + ap[:, 1:-1] * w[1] + ap[:, 2:] * w[2] + b


def trunk_layer(x, l, attn_norm, w_in, gate_bias, lambda_q1, lambda_k1, lambda_q2, lambda_k2,
                subln, sink, w_proj_a, w_proj_b, w_out, ffn_norm, w_up, conv_w, conv_b, w_down):
    B, S = x.shape[0], x.shape[1]
    lam_init = 0.8 - 0.6 * math.exp(-0.3 * l)
    h = rmsnorm(x, attn_norm[l])
    proj = h @ w_in[l]
    qa, ka, va, qb, kb, vb, ga, gb = jnp.split(proj, SPLIT_IDX, axis=-1)
    lam = (jnp.exp(jnp.sum(lambda_q1[l].astype(jnp.float32) * lambda_k1[l].astype(jnp.float32)))
           - jnp.exp(jnp.sum(lambda_q2[l].astype(jnp.float32) * lambda_k2[l].astype(jnp.float32)))
           + lam_init)
    ya = diff_attention(qa.reshape(B, S, A_HEADS, 2, A_HEAD_DIM),
                        ka.reshape(B, S, A_HEADS, 2, A_HEAD_DIM),
                        va.reshape(B, S, A_HEADS, A_VDIM), lam, lam_init, subln[l])
    yb = window_attention(qb.reshape(B, S, B_KV_HEADS, B_GROUP, B_HEAD_DIM),
                          kb.reshape(B, S, B_KV_HEADS, B_HEAD_DIM),
                          vb.reshape(B, S, B_KV_HEADS, B_HEAD_DIM), sink[l])
    bga, bgb = jnp.split(gate_bias[l], 2)
    merged = (jax.nn.sigmoid(ga + bga) * (ya @ w_proj_a[l])
              + jax.nn.sigmoid(gb + bgb) * (yb @ w_proj_b[l]))
    x = x + merged @ w_out[l]
    h = rmsnorm(x, ffn_norm[l])
    a, v = jnp.split(h @ w_up[l], 2, axis=-1)
    a = dwconv_centred(a, conv_w[l], conv_b[l])
    return x + (jax.nn.gelu(a) * v) @ w_down[l]


def setup_inputs(seed: int = 0) -> dict:
    key = jax.random.key(seed)
    ks = jax.random.split(key, 20)
    f32 = jnp.float32
    nrm = lambda k, shp, s: jax.random.normal(k, shp, f32) * s
    return {
        "x_prompt": nrm(ks[0], (BATCH, SEQ, D_MODEL), 1.0),
        "x_sample": nrm(ks[1], (DEC_BATCH, DEC_SEQ, D_MODEL), 1.0),
        "attn_norm": 1.0 + nrm(ks[2], (DEPTH, D_MODEL), 0.02),
        "w_in": nrm(ks[3], (DEPTH, D_MODEL, IN_COLS), D_MODEL ** -0.5),
        "gate_bias": nrm(ks[4], (DEPTH, 2 * D_MODEL), 0.02),
        "lambda_q1": nrm(ks[5], (DEPTH, A_HEAD_DIM), 0.1),
        "lambda_k1": nrm(ks[6], (DEPTH, A_HEAD_DIM), 0.1),
        "lambda_q2": nrm(ks[7], (DEPTH, A_HEAD_DIM), 0.1),
        "lambda_k2": nrm(ks[8], (DEPTH, A_HEAD_DIM), 0.1),
        "subln": 1.0 + nrm(ks[9], (DEPTH, A_VDIM), 0.02),
        "sink": nrm(ks[10], (DEPTH, B_HEADS), 1.0),
        "w_proj_a": nrm(ks[11], (DEPTH, A_WIDTH, D_MODEL), A_WIDTH ** -0.5),
        "w_proj_b": nrm(ks[12], (DEPTH, B_WIDTH, D_MODEL), B_WIDTH ** -0.5),
        "w_out": nrm(ks[13], (DEPTH, D_MODEL, D_MODEL), D_MODEL ** -0.5),
        "ffn_norm": 1.0 + nrm(ks[14], (DEPTH, D_MODEL), 0.02),
        "w_up": nrm(ks[15], (DEPTH, D_MODEL, 2 * D_FF), D_MODEL ** -0.5),
        "conv_w": nrm(ks[16], (DEPTH, CONV_WIDTH, D_FF), 0.5),
        "conv_b": nrm(ks[17], (DEPTH, D_FF), 0.02),
        "w_down": nrm(ks[18], (DEPTH, D_FF, D_MODEL), D_FF ** -0.5),
        "final_norm": 1.0 + nrm(ks[19], (D_MODEL,), 0.02),
    }


def reference(x_prompt, x_sample, attn_norm, w_in, gate_bias, lambda_q1, lambda_k1, lambda_q2,
              lambda_k2, subln, sink, w_proj_a, w_proj_b, w_out, ffn_norm, w_up, conv_w, conv_b,
              w_down, final_norm):
    xp = x_prompt
    xs = x_sample
    for l in range(DEPTH):
        xp = trunk_layer(xp, l, attn_norm, w_in, gate_bias, lambda_q1, lambda_k1, lambda_q2,
                         lambda_k2, subln, sink, w_proj_a, w_proj_b, w_out, ffn_norm, w_up,
                         conv_w, conv_b, w_down)
        xs = trunk_layer(xs, l, attn_norm, w_in, gate_bias, lambda_q1, lambda_k1, lambda_q2,
                         lambda_k2, subln, sink, w_proj_a, w_proj_b, w_out, ffn_norm, w_up,
                         conv_w, conv_b, w_down)
    y_prompt = rmsnorm(xp, final_norm)
    y_sample = rmsnorm(xs, final_norm)
    return (y_prompt, y_sample)
```

```python
import contextlib
import math
import numpy as np
import concourse.bass as bass
import concourse.mybir as mybir
from concourse.bass_utils import run_bass_kernel_spmd

F32 = mybir.dt.float32
BF16 = mybir.dt.bfloat16
AF = mybir.ActivationFunctionType
ALU = mybir.AluOpType

PE, ACT, DVE, POOL, SP = "tensor", "scalar", "vector", "gpsimd", "sync"
ENGS = [PE, ACT, DVE, POOL, SP]

D = 1024
DEPTH = 2
IN_COLS = 4352
DFF = 2816
NFC = 22
EPS = 1e-6
N_CORES = 8
STOP_AFTER = 10 ** 9
SLOPES_A = [2.0 ** (-8.0 * (h + 1) / 4) for h in range(4)]
SLOPES_B = [2.0 ** (-8.0 * (h + 1) / 8) for h in range(8)]
NEGBIG = -30000.0
SKIP_EXP = 88.0 + 92.3


class Slot:
    __slots__ = ("name", "writers", "readers", "prev_readers", "dsem", "waw")

    def __init__(self, name, waw=False):
        self.name = name
        self.waw = waw
        self.writers = []
        self.readers = []
        self.prev_readers = []
        self.dsem = None


class Op:
    __slots__ = ("eng", "fn", "deps", "signal", "count", "dma", "dtok")

    def __init__(self, eng, fn):
        self.eng = eng
        self.fn = fn
        self.deps = []
        self.signal = False
        self.count = None
        self.dma = False
        self.dtok = None


class Prog:
    def __init__(self, nc, stack, n_dma_sems=88):
        self.nc = nc
        self.ops = {e: [] for e in ENGS}
        self.esem = {e: stack.enter_context(nc.semaphore("S_" + e)) for e in ENGS}
        self.dsem = [stack.enter_context(nc.semaphore("D%d" % i)) for i in range(n_dma_sems)]
        self.dcount = [0] * n_dma_sems
        self.next_dsem = 0
        self.free_dsems = []
        self.ecount = {e: 0 for e in ENGS}
        self.waited_e = {e: {x: 0 for x in ENGS} for e in ENGS}
        self.waited_d = {e: [0] * n_dma_sems for e in ENGS}
        self.slots = []
        self.last_sig = {e: None for e in ENGS}

    def slot(self, name, waw=False):
        s = Slot(name, waw)
        self.slots.append(s)
        return s

    def release_slots(self, slots):
        for s in slots:
            if s.dsem is not None:
                self.free_dsems.append(s.dsem)
                s.dsem = None
        ids = set(id(s) for s in slots)
        self.slots = [s for s in self.slots if id(s) not in ids]

    def _mkdeps(self, op, reads, writes):
        deps = []
        for s in reads:
            deps.extend(s.writers)
        for s in writes:
            if s.readers:
                s.prev_readers = s.readers
                deps.extend(s.writers)
                s.readers = []
                s.writers = []
            deps.extend(s.prev_readers)
            if s.waw:
                deps.extend(s.writers)
        for s in reads:
            s.readers.append(op)
        for s in writes:
            s.writers.append(op)
        seen = set()
        out = []
        for d in deps:
            if d is op or id(d) in seen:
                continue
            seen.add(id(d))
            out.append(d)
        return out

    def op(self, eng, fn, reads=(), writes=(), extra=()):
        o = Op(eng, fn)
        o.deps = self._mkdeps(o, reads, writes) + list(extra)
        for d in o.deps:
            if not d.dma:
                d.signal = True
        self.ops[eng].append(o)
        return o

    def dma(self, eng, out, in_, sb, reads=(), writes=(), extra=(), **kw):
        o = Op(eng, lambda e: e.dma_start(out=out, in_=in_, **kw))
        o.dma = True
        o.deps = self._mkdeps(o, reads, writes) + list(extra)
        for d in o.deps:
            if not d.dma:
                d.signal = True
        if sb.dsem is None:
            if self.free_dsems:
                sb.dsem = self.free_dsems.pop()
            else:
                sb.dsem = self.next_dsem
                self.next_dsem += 1
                assert self.next_dsem <= len(self.dsem), "out of DMA semaphores"
        self.dcount[sb.dsem] += 16
        o.dtok = (sb.dsem, self.dcount[sb.dsem])
        self.ops[eng].append(o)
        return o

    def mm(self, out, lhsT, rhs, start=True, stop=True, reads=(), writes=(), skip=False):
        return self.op(PE, lambda e: e.matmul(out, lhsT=lhsT, rhs=rhs, start=start, stop=stop,
                                              skip_group_check=skip), reads, writes)

    def tr(self, out, in_, ident, reads=(), writes=()):
        return self.op(PE, lambda e: e.transpose(out=out, in_=in_, identity=ident), reads, writes)

    def act(self, out, in_, func, bias=None, scale=None, accum=None, reads=(), writes=()):
        kw = {}
        if bias is not None:
            kw["bias"] = bias
        if scale is not None:
            kw["scale"] = scale
        if accum is not None:
            kw["accum_out"] = accum
        return self.op(ACT, lambda e: e.activation(out=out, in_=in_, func=func, **kw), reads, writes)

    def ts(self, eng, out, in0, s1, s2, op0, op1=None, reads=(), writes=()):
        if op1 is None:
            return self.op(eng, lambda e: e.tensor_scalar(out=out, in0=in0, scalar1=s1, scalar2=None,
                                                          op0=op0), reads, writes)
        return self.op(eng, lambda e: e.tensor_scalar(out=out, in0=in0, scalar1=s1, scalar2=s2,
                                                      op0=op0, op1=op1), reads, writes)

    def tt(self, eng, out, in0, in1, op, reads=(), writes=()):
        return self.op(eng, lambda e: e.tensor_tensor(out=out, in0=in0, in1=in1, op=op), reads, writes)

    def stt(self, out, in0, scalar, in1, op0, op1, accum=None, reads=(), writes=()):
        if accum is None:
            return self.op(DVE, lambda e: e.scalar_tensor_tensor(out=out, in0=in0, scalar=scalar, in1=in1,
                                                                 op0=op0, op1=op1), reads, writes)
        return self.op(DVE, lambda e: e.scalar_tensor_tensor(out=out, in0=in0, scalar=scalar, in1=in1,
                                                             op0=op0, op1=op1, accum_out=accum),
                       reads, writes)

    def cp(self, eng, out, in_, reads=(), writes=()):
        if eng == ACT:
            return self.op(ACT, lambda e: e.copy(out=out, in_=in_), reads, writes)
        return self.op(eng, lambda e: e.tensor_copy(out=out, in_=in_), reads, writes)

    def memset(self, eng, ap, val, writes=()):
        return self.op(eng, lambda e: e.memset(ap, val), (), writes)

    def recip(self, out, in_, reads=(), writes=()):
        return self.op(DVE, lambda e: e.reciprocal(out=out, in_=in_), reads, writes)

    def barrier(self):
        lasts = []
        for e in ENGS:
            for o in reversed(self.ops[e]):
                if not o.dma:
                    lasts.append(o)
                    break
            else:
                if self.last_sig[e] is not None:
                    lasts.append(self.last_sig[e])
        dtoks = [(i, c) for i, c in enumerate(self.dcount) if c > 0]
        for e in ENGS:
            o = Op(e, ("barrier", dtoks))
            o.deps = [l for l in lasts if l.eng != e]
            for d in o.deps:
                d.signal = True
            self.ops[e].append(o)
        for s in self.slots:
            s.writers = []
            s.readers = []
            s.prev_readers = []

    def emit(self):
        nc = self.nc
        for e in ENGS:
            for o in self.ops[e]:
                if o.signal and not o.dma and o.count is None:
                    self.ecount[e] += 1
                    o.count = self.ecount[e]
                    self.last_sig[e] = o
        esem, dsem = self.esem, self.dsem

        def run(en, e):
            we = self.waited_e[en]
            wd = self.waited_d[en]
            for o in self.ops[en]:
                dmax = {}
                for d in o.deps:
                    if d.dma:
                        si, v = d.dtok
                        if dmax.get(si, 0) < v:
                            dmax[si] = v
                for si, v in dmax.items():
                    if wd[si] < v:
                        e.wait_ge(dsem[si], v)
                        wd[si] = v
                for d in o.deps:
                    if d.dma:
                        continue
                    else:
                        if d.eng == en and en == PE:
                            continue
                        if we[d.eng] < d.count:
                            e.wait_ge(esem[d.eng], d.count)
                            we[d.eng] = d.count
                if isinstance(o.fn, tuple):
                    for si, v in o.fn[1]:
                        if wd[si] < v:
                            e.wait_ge(dsem[si], v)
                            wd[si] = v
                    if o.signal:
                        e.nop().then_inc(esem[en], 1)
                    continue
                ins = o.fn(e)
                if o.dma:
                    ins.then_inc(dsem[o.dtok[0]], 16)
                elif o.signal:
                    ins.then_inc(esem[en], 1)

        with nc.Block() as block:
            @block.tensor
            def _(e):
                run(PE, e)

            @block.scalar
            def _(e):
                run(ACT, e)

            @block.vector
            def _(e):
                run(DVE, e)

            @block.gpsimd
            def _(e):
                run(POOL, e)

            @block.sync
            def _(e):
                run(SP, e)
        self.ops = {e: [] for e in ENGS}


class Ring:
    def __init__(self, P, name, tiles):
        self.tiles = tiles
        self.slots = [P.slot("%s%d" % (name, i)) for i in range(len(tiles))]
        self.i = -1

    def next(self):
        self.i = (self.i + 1) % len(self.tiles)
        return self.tiles[self.i], self.slots[self.i]

    def cur(self):
        return self.tiles[self.i], self.slots[self.i]


def build(seqs, n_layers=DEPTH):
    T = sum(seqs)
    SMAX = max(seqs)
    NKMAX = SMAX // 128
    nc = bass.Bass("TRN2", target_bir_lowering=False)

    def din(name, shape, dt=F32):
        return nc.dram_tensor(name, list(shape), dt, kind="ExternalInput").ap()

    def dscr(name, shape, dt):
        return nc.dram_tensor(name, list(shape), dt, kind="Internal").ap()

    xin = din("xin", [T, D])
    w_in = din("w_in", [DEPTH, D, IN_COLS])
    w_pa = din("w_proj_a", [DEPTH, 512, D])
    w_pb = din("w_proj_b", [DEPTH, 512, D])
    w_out = din("w_out", [DEPTH, D, D])
    w_up = din("w_up", [DEPTH, D, 2 * DFF])
    w_dn = din("w_down", [DEPTH, DFF, D])
    p_an = din("p_an", [DEPTH, 128, 8])
    p_fn = din("p_fn", [DEPTH, 128, 8])
    p_gb = din("p_gb", [DEPTH, 128, 16])
    p_lam = din("p_lam", [DEPTH, 128, 256])
    p_sub = din("p_sub", [DEPTH, 128, 1])
    p_sink = din("p_sink", [DEPTH, 128, 8])
    p_cw = din("p_cw", [DEPTH, 128, 3 * NFC])
    p_cb = din("p_cb", [DEPTH, 128, NFC])
    p_fin = din("p_fin", [128, D])
    c_ident = din("c_ident", [128, 128])
    c_augq = din("c_augq", [4, 3, 3, 2, 512])
    c_blr = din("c_blr", [128, 4 * 2 * 64])
    c_toep = din("c_toep", [128, 4 * 896])
    c_bb = din("c_bb", [128, 3 * 8 * 128])
    yout = nc.dram_tensor("yout", [T, D], F32, kind="ExternalOutput").ap()

    qkA = dscr("s_qkA", [16, 64, T], BF16)
    qkB = dscr("s_qkB", [10, 64, T], BF16)
    vAB = dscr("s_vAB", [T, 640], BF16)
    gT = dscr("s_gT", [16, 128, T], BF16)
    yAT = dscr("s_yAT", [4, 128, T], BF16)
    yBT = dscr("s_yBT", [4, 128, T], BF16)
    xmid = dscr("s_xmid", [T, D], F32)
    h2T = dscr("s_h2T", [8, 128, T], BF16)
    xres = dscr("s_xres", [T, D], F32)

    seq_off = [sum(seqs[:i]) for i in range(len(seqs))]
    NCH = T // 512
    chunk_seq = []
    for si, S in enumerate(seqs):
        for c in range(S // 512):
            chunk_seq.append((si, c))

    with contextlib.ExitStack() as top:
        P = Prog(nc, top)

        def phase_scope():
            st = contextlib.ExitStack()
            st.slots = []
            return st

        uniq = [0]

        def sb(st, name, shape, dt):
            uniq[0] += 1
            return st.enter_context(nc.sbuf_tensor("%s_%d" % (name, uniq[0]), list(shape), dt))

        def ps(st, name, shape, dt):
            uniq[0] += 1
            return st.enter_context(nc.psum_tensor("%s_%d" % (name, uniq[0]), list(shape), dt))

        def slot(st, name, waw=False):
            s = P.slot(name, waw)
            st.slots.append(s)
            return s

        nphase = [0]

        class StopBuild(Exception):
            pass

        def end_phase(st):
            P.barrier()
            P.emit()
            P.release_slots(st.slots)
            st.close()
            nphase[0] += 1
            if nphase[0] >= STOP_AFTER:
                raise StopBuild()

        dmaq = [SP, POOL]
        dq = [0]

        def dma_eng():
            return SP

        def load_ident(st, pfx):
            idf = sb(st, pfx + "idf", [128, 128], F32)
            idb = sb(st, pfx + "idb", [128, 128], BF16)
            s1, s2 = slot(st, pfx + "idf"), slot(st, pfx + "idb")
            P.dma(SP, idf[:], c_ident[:, :], s1, writes=[s1])
            P.cp(DVE, idb[:], idf[:], reads=[s1], writes=[s2])
            return idb, s2

        cast_rr = [0]

        def load_weight(st, pfx, dst, dst_slot, src, n_k, n_cols, scol=None, scol_slot=None, s2=None,
                        stage=None):
            stg_ring = stage
            cb = 1024
            for kc in range(n_k):
                for c0 in range(0, n_cols, cb):
                    cw = min(cb, n_cols - c0)
                    stg, sslot = stg_ring.next()
                    P.dma(dma_eng(), stg[:, 0:cw], src[kc * 128:(kc + 1) * 128, c0:c0 + cw], sslot,
                          writes=[sslot])
                    o = dst[:, kc, c0:c0 + cw]
                    cast_rr[0] += 1
                    if scol is None:
                        eng = [DVE, ACT, POOL][cast_rr[0] % 3]
                        P.cp(eng, o, stg[:, 0:cw], reads=[sslot], writes=[dst_slot])
                    else:
                        sc = scol[:, kc:kc + 1]
                        if s2 is None:
                            if cast_rr[0] % 2:
                                P.ts(DVE, o, stg[:, 0:cw], sc, None, ALU.mult, reads=[sslot, scol_slot],
                                     writes=[dst_slot])
                            else:
                                P.act(o, stg[:, 0:cw], AF.Copy, scale=sc, reads=[sslot, scol_slot],
                                      writes=[dst_slot])
                        else:
                            P.ts(DVE, o, stg[:, 0:cw], sc, float(s2), ALU.mult, ALU.mult,
                                 reads=[sslot, scol_slot], writes=[dst_slot])

        def mk_stage(st, pfx, n=2):
            tiles = [sb(st, "%sstg%d" % (pfx, i), [128, 1024], F32) for i in range(n)]
            r = Ring(P, pfx + "stg", tiles)
            st.slots.extend(r.slots)
            return r

        def small(st, name, src, shape):
            t = sb(st, name, shape, F32)
            s = slot(st, name)
            P.dma(SP, t[:], src, s, writes=[s])
            return t, s

        def rms_stats(st_tiles, x_ap_list, x_slot, ssq, ssq_slot, rstd, rstd_slot, junk, junk_slot, n, dim):
            for i in range(n):
                P.act(junk[:, 0:dim], x_ap_list[i], AF.Square, accum=ssq[:, i:i + 1], reads=[x_slot],
                      writes=[junk_slot, ssq_slot])
            P.ts(POOL, rstd[:, 0:n], ssq[:, 0:n], 1.0 / dim, EPS, ALU.mult, ALU.add, reads=[ssq_slot],
                 writes=[rstd_slot])
            P.tt(POOL, rstd[:, 0:n], rstd[:, 0:n], st_tiles["mhalf"][:, 0:n], ALU.pow,
                 reads=[rstd_slot, st_tiles["mhalf_s"]], writes=[rstd_slot])

        def mk_mhalf(st, pfx):
            t = sb(st, pfx + "mhalf", [128, 8], F32)
            s = slot(st, pfx + "mhalf")
            P.memset(POOL, t[:], -0.5, writes=[s])
            return {"mhalf": t, "mhalf_s": s}

        top.push(lambda et, ev, tb: et is StopBuild)
        for l in range(n_layers):
            lam_init = 0.8 - 0.6 * math.exp(-0.3 * l)
            xsrc = xin if l == 0 else xres
            last = (l == n_layers - 1)

            st = phase_scope()
            idb, idb_s = load_ident(st, "p1")
            cst = mk_mhalf(st, "p1")
            wi = sb(st, "p1wi", [128, 8, IN_COLS], BF16)
            wi_s = slot(st, "p1wi")
            stage = mk_stage(st, "p1", n=3)
            gcol, gcol_s = small(st, "p1gcol", p_an[l], [128, 8])
            gbias, gbias_s = small(st, "p1gbias", p_gb[l], [128, 16])
            load_weight(st, "p1", wi, wi_s, w_in[l], 8, IN_COLS, gcol, gcol_s, stage=stage)

            xr = Ring(P, "p1x", [sb(st, "p1x%d" % i, [128, 4, D], F32) for i in range(2)])
            st.slots.extend(xr.slots)
            hb = sb(st, "p1h", [128, 4, D], BF16)
            hb_s = slot(st, "p1h")
            hT = sb(st, "p1hT", [128, 8, 512], BF16)
            hT_s = slot(st, "p1hT")
            stA = sb(st, "p1stA", [128, 8, 512], BF16)
            stA_s = slot(st, "p1stA")
            stB = sb(st, "p1stB", [128, 5, 512], BF16)
            stB_s = slot(st, "p1stB")
            gr = Ring(P, "p1g", [sb(st, "p1g%d" % i, [128, 8, 512], BF16) for i in range(2)])
            st.slots.extend(gr.slots)
            vr = Ring(P, "p1v", [sb(st, "p1v%d" % i, [128, 4, 640], BF16) for i in range(2)])
            st.slots.extend(vr.slots)
            junk = sb(st, "p1junk", [128, D], BF16)
            junk_s = slot(st, "p1junk", waw=True)
            ssq = sb(st, "p1ssq", [128, 8], F32)
            ssq_s = slot(st, "p1ssq")
            rstd = sb(st, "p1rstd", [128, 8], F32)
            rstd_s = slot(st, "p1rstd")
            ptr = Ring(P, "p1ptr", [ps(st, "p1ptr%d" % i, [128, 512], BF16)[:] for i in range(2)])
            st.slots.extend(ptr.slots)
            pp = Ring(P, "p1pp", [ps(st, "p1pp%d" % i, [128, 512], F32) for i in range(6)])
            st.slots.extend(pp.slots)
            evac_rr = [0]

            def evac_eng():
                evac_rr[0] += 1
                return DVE if evac_rr[0] % 3 else ACT

            def p1_load(c):
                xt, xs = xr.next()
                t0 = 512 * c
                P.dma(SP, xt[:], xsrc[t0:t0 + 512, :].rearrange("(t p) d -> p t d", p=128), xs, writes=[xs])
                return xt, xs

            def p1_norm(xt, xs):
                rms_stats(cst, [xt[:, t, :] for t in range(4)], xs, ssq, ssq_s, rstd, rstd_s, junk, junk_s,
                          4, D)
                for t in range(4):
                    P.ts(DVE, hb[:, t, :], xt[:, t, :], rstd[:, t:t + 1], None, ALU.mult,
                         reads=[xs, rstd_s], writes=[hb_s])

            def transposes(src, src_s, dstT, dstT_s, ptr_ring, idb, idb_s, nfc=8):
                for fc in range(nfc):
                    pt, pts = ptr_ring.next()
                    for t in range(4):
                        P.tr(pt[:, t * 128:(t + 1) * 128], src[:, t, fc * 128:(fc + 1) * 128], idb[:],
                             reads=[src_s, idb_s], writes=[pts])
                    P.cp(evac_eng(), dstT[:, fc, :], pt, reads=[pts], writes=[dstT_s])

            nxt = p1_load(0)
            p1_norm(*nxt)
            for c in range(NCH):
                t0 = 512 * c
                transposes(hb, hb_s, hT, hT_s, ptr, idb, idb_s)
                if c + 1 < NCH:
                    nxt = p1_load(c + 1)
                    p1_norm(*nxt)
                for i in range(8):
                    pt, pts = pp.next()
                    for kc in range(8):
                        P.mm(pt[:], wi[:, kc, i * 128:(i + 1) * 128], hT[:, kc, :], start=(kc == 0),
                             stop=(kc == 7), reads=[wi_s, hT_s], writes=[pts])
                    P.cp(evac_eng(), stA[:, i, :], pt[:], reads=[pts], writes=[stA_s])
                P.dma(SP, qkA[:, :, t0:t0 + 512].rearrange("(i two) d t -> (two d) i t", two=2), stA[:], stA_s,
                      reads=[stA_s])
                for i in range(5):
                    pt, pts = pp.next()
                    c0 = 1536 + i * 128
                    for kc in range(8):
                        P.mm(pt[:], wi[:, kc, c0:c0 + 128], hT[:, kc, :], start=(kc == 0),
                             stop=(kc == 7), reads=[wi_s, hT_s], writes=[pts])
                    P.cp(evac_eng(), stB[:, i, :], pt[:], reads=[pts], writes=[stB_s])
                P.dma(SP, qkB[:, :, t0:t0 + 512].rearrange("(i two) d t -> (two d) i t", two=2), stB[:], stB_s,
                      reads=[stB_s])
                for half in range(2):
                    gt, gs = gr.next()
                    for i in range(8):
                        gi = half * 8 + i
                        c0 = 2304 + gi * 128
                        pt, pts = pp.next()
                        for kc in range(8):
                            P.mm(pt[:], wi[:, kc, c0:c0 + 128], hT[:, kc, :], start=(kc == 0), stop=(kc == 7),
                                 reads=[wi_s, hT_s], writes=[pts])
                        P.act(gt[:, i, :], pt[:], AF.Sigmoid, bias=gbias[:, gi:gi + 1], reads=[pts, gbias_s],
                              writes=[gs])
                    P.dma(SP, gT[half * 8:half * 8 + 8, :, t0:t0 + 512].rearrange("c p t -> p c t"), gt[:], gs,
                          reads=[gs])
                vt, vs = vr.next()
                for t in range(4):
                    pt, pts = pp.next()
                    for kc in range(8):
                        P.mm(pt[:], hT[:, kc, t * 128:(t + 1) * 128], wi[:, kc, 1024:1536], start=(kc == 0),
                             stop=(kc == 7), reads=[wi_s, hT_s], writes=[pts])
                    P.cp(evac_eng(), vt[:, t, 0:512], pt[:], reads=[pts], writes=[vs])
                    pt, pts = pp.next()
                    for kc in range(8):
                        P.mm(pt[:, 0:128], hT[:, kc, t * 128:(t + 1) * 128], wi[:, kc, 2176:2304],
                             start=(kc == 0), stop=(kc == 7), reads=[wi_s, hT_s], writes=[pts])
                    P.cp(evac_eng(), vt[:, t, 512:640], pt[:, 0:128], reads=[pts], writes=[vs])
                P.dma(SP, vAB[t0:t0 + 512, :].rearrange("(t p) d -> p t d", p=128), vt[:], vs, reads=[vs])
            end_phase(st)

            st = phase_scope()
            idb, idb_s = load_ident(st, "pa")
            cst = mk_mhalf(st, "pa")
            blr, blr_s = small(st, "pablr", c_blr, [128, 4 * 2 * 64])
            toep, toep_s = small(st, "patoep", c_toep, [128, 4 * 896])
            lamv, lamv_s = small(st, "palam", p_lam[l], [128, 256])
            lt = sb(st, "palt", [128, 8], F32)
            lt_s = slot(st, "palt")
            ljunk = sb(st, "paljunk", [128, 64], F32)
            ljunk_s = slot(st, "paljunk", waw=True)
            P.stt(ljunk[:], lamv[:, 0:64], 1.0, lamv[:, 64:128], ALU.mult, ALU.mult, accum=lt[:, 0:1],
                  reads=[lamv_s], writes=[ljunk_s, lt_s])
            P.stt(ljunk[:], lamv[:, 128:192], 1.0, lamv[:, 192:256], ALU.mult, ALU.mult, accum=lt[:, 1:2],
                  reads=[lamv_s], writes=[ljunk_s, lt_s])
            P.act(lt[:, 2:4], lt[:, 0:2], AF.Exp, reads=[lt_s], writes=[lt_s])
            P.tt(DVE, lt[:, 4:5], lt[:, 3:4], lt[:, 2:3], ALU.subtract, reads=[lt_s], writes=[lt_s])
            P.ts(DVE, lt[:, 5:6], lt[:, 4:5], -lam_init, None, ALU.add, reads=[lt_s], writes=[lt_s])
            neglam = lt[:, 5:6]

            ktr = Ring(P, "pakt", [sb(st, "pakt%d" % i, [67, 2, SMAX], BF16) for i in range(2)])
            var = Ring(P, "pava", [sb(st, "pava%d" % i, [128, NKMAX, 130], BF16) for i in range(2)])
            qvr = Ring(P, "paqv", [sb(st, "paqv%d" % i, [67, 3, 2, 512], BF16) for i in range(2)])
            qaug_s = [slot(st, "paqaug%d" % i) for i in range(2)]
            augst = sb(st, "paaugst", [67, 3, 2, 512], F32)
            augst_s = slot(st, "paaugst")
            etr = Ring(P, "paet", [sb(st, "paet%d" % i, [128, 1024], BF16) for i in range(3)])
            orr = Ring(P, "paor", [sb(st, "paor%d" % i, [128, 8, 129], F32) for i in range(2)])
            obr = Ring(P, "paob", [sb(st, "paob%d" % i, [128, 4, 128], F32) for i in range(2)])
            ybr = Ring(P, "payb", [sb(st, "payb%d" % i, [128, 4, 128], BF16) for i in range(2)])
            ysr = Ring(P, "pays", [sb(st, "pays%d" % i, [128, 512], BF16) for i in range(2)])
            for r in (ktr, var, qvr, etr, orr, obr, ybr, ysr):
                st.slots.extend(r.slots)
            pjunk = sb(st, "pajunk", [128, 128], F32)
            pjunk_s = slot(st, "pajunk", waw=True)
            smlr = Ring(P, "pasml", [sb(st, "pasml%d" % i, [128, 32], F32) for i in range(2)])
            st.slots.extend(smlr.slots)
            pscr = Ring(P, "papsc", [ps(st, "papsc%d" % i, [128, 1024], F32) for i in range(2)])
            paccT = [ps(st, "papacc%d" % i, [128, 512], F32) for i in range(3)]
            pacc_s = [slot(st, "papacc%d" % i) for i in range(3)]
            ptr = Ring(P, "paptr", [ps(st, "paptr%d" % i, [128, 512], BF16)[:] for i in range(1)])
            st.slots.extend(pscr.slots)
            st.slots.extend(ptr.slots)
            for i in range(2):
                P.memset(POOL, ktr.tiles[i][64:67, :, :], 1.0, writes=[ktr.slots[i]])
                P.memset(POOL, var.tiles[i][:, :, 128:130], 1.0, writes=[var.slots[i]])

            def acc_ap(a):
                return paccT[a // 3][:, (a % 3) * 129:(a % 3) * 129 + 129], pacc_s[a // 3]

            def pa_load_kv(si, h):
                S = seqs[si]
                t0 = seq_off[si]
                kt, kts = ktr.next()
                va, vas = var.next()
                for m in range(2):
                    P.dma(SP, kt[0:64, m, 0:S], qkA[8 + 2 * h + m, :, t0:t0 + S], kts, writes=[kts])
                nk = S // 128
                for j0 in range(0, nk, 8):
                    nj = min(8, nk - j0)
                    P.dma(dma_eng(), va[:, j0:j0 + nj, 0:128],
                          vAB[t0 + 128 * j0:t0 + 128 * (j0 + nj), 128 * h:128 * h + 128].rearrange(
                              "(j p) e -> p j e", p=128), vas, writes=[vas])
                return kt, kts, va, vas

            def keep(h, c, j):
                jj = j - 4 * c
                if jj < 0:
                    dmin = 128 * (-jj) - 127
                elif jj >= 4:
                    dmin = 128 * jj - 511
                else:
                    return True
                return SLOPES_A[h] * dmin <= SKIP_EXP

            work = [(si, h) for si in range(len(seqs)) for h in range(4)]
            W = []
            flat = []
            for wi_, (si, h) in enumerate(work):
                S = seqs[si]
                w = dict(si=si, h=h, S=S, t0=seq_off[si], nk=S // 128, nq=S // 512, kv=None, first={}, last={})
                for c in range(w["nq"]):
                    for j in range(w["nk"]):
                        if keep(h, c, j):
                            w["first"].setdefault(c, j)
                            w["last"][c] = j
                            flat.append((wi_, c, j))
                W.append(w)
            chunk_order = []
            for (wi_, c, j) in flat:
                if not chunk_order or chunk_order[-1] != (wi_, c):
                    chunk_order.append((wi_, c))
            chunk_pos = {k: i for i, k in enumerate(chunk_order)}
            cur_h = [None, None]
            qbufs = {}

            def load_q(wi_, c):
                w = W[wi_]
                h, t0 = w["h"], w["t0"]
                qv, qs = qvr.next()
                bi = qvr.i
                if cur_h[bi] != h:
                    P.dma(SP, augst[64:67, :, :, :], c_augq[h], augst_s, writes=[augst_s])
                    P.cp(DVE, qv[64:67, :, :, :], augst[64:67, :, :, :], reads=[augst_s],
                         writes=[qaug_s[bi], qs])
                    cur_h[bi] = h
                for v in range(3):
                    P.dma(SP, qv[0:64, v, :, :],
                          qkA[2 * h:2 * h + 2, :, t0 + 512 * c:t0 + 512 * c + 512].rearrange(
                              "m d t -> d m t"), qs, writes=[qs])
                return qv, qs

            def ensure_q(wi_, c, j):
                if (wi_, c) not in qbufs:
                    qbufs[(wi_, c)] = load_q(wi_, c)
                w = W[wi_]
                if j == min(w["first"][c] + 1, w["last"][c]):
                    p = chunk_pos[(wi_, c)] + 1
                    if p < len(chunk_order) and chunk_order[p] not in qbufs:
                        qbufs[chunk_order[p]] = load_q(*chunk_order[p])

            def ensure_kv(wi_):
                if W[wi_]["kv"] is None:
                    W[wi_]["kv"] = pa_load_kv(*work[wi_])

            def qk(wi_, c, j):
                w = W[wi_]
                h = w["h"]
                ensure_kv(wi_)
                kt, kts, va, vas = w["kv"]
                qv, qs = qbufs[(wi_, c)]
                pscT, pscS = pscr.next()
                jj = j - 4 * c
                if jj < 0:
                    kind, v, kk = "L", 0, 67
                elif jj >= 4:
                    kind, v, kk = "R", 1, 67
                else:
                    kind, v, kk = "D", 2, 67
                for m in range(2):
                    P.mm(pscT[:, m * 512:(m + 1) * 512], kt[0:kk, m, j * 128:(j + 1) * 128],
                         qv[0:kk, v, m, :], reads=[kts, qs], writes=[pscS])
                if kind == "D":
                    off = 384 - 128 * jj
                    for m in range(2):
                        P.tt(DVE, pscT[:, m * 512:(m + 1) * 512], pscT[:, m * 512:(m + 1) * 512],
                             toep[:, h * 896 + off:h * 896 + off + 512], ALU.add, reads=[pscS, toep_s],
                             writes=[pscS])
                    bias = None
                elif kind == "L":
                    bias = blr[:, (h * 2 + 0) * 64 + (-jj):(h * 2 + 0) * 64 + (-jj) + 1]
                else:
                    bias = blr[:, (h * 2 + 1) * 64 + jj:(h * 2 + 1) * 64 + jj + 1]
                return pscT, pscS, bias

            def post(wi_, c):
                orw, ors = orr.next()
                sml, sml_s = smlr.next()
                for b_ in range(3):
                    wd_ = 387 if b_ < 2 else 258
                    P.cp(DVE, orw[:, 3 * b_:3 * b_ + wd_ // 129, :], paccT[b_][:, 0:wd_].rearrange(
                        "p (a e) -> p a e", e=129), reads=[pacc_s[b_]], writes=[ors])
                yield
                P.recip(sml[:, 0:8], orw[:, :, 128], reads=[ors], writes=[sml_s])
                P.ts(DVE, sml[:, 8:12], sml[:, 4:8], neglam, None, ALU.mult, reads=[sml_s, lt_s],
                     writes=[sml_s])
                ob, obs = obr.next()
                for qs_ in range(4):
                    yield
                    P.ts(DVE, ob[:, qs_, :], orw[:, qs_, 0:128], sml[:, qs_:qs_ + 1], None, ALU.mult,
                         reads=[ors, sml_s], writes=[obs])
                    P.stt(ob[:, qs_, :], orw[:, 4 + qs_, 0:128], sml[:, 8 + qs_:9 + qs_], ob[:, qs_, :],
                          ALU.mult, ALU.add, reads=[ors, sml_s, obs], writes=[obs])
                    P.stt(pjunk[:], ob[:, qs_, :], 1.0, ob[:, qs_, :], ALU.mult, ALU.mult,
                          accum=sml[:, 12 + qs_:13 + qs_], reads=[obs], writes=[pjunk_s, sml_s])
                P.ts(POOL, sml[:, 16:20], sml[:, 12:16], 1.0 / 128, EPS, ALU.mult, ALU.add, reads=[sml_s],
                     writes=[sml_s])
                P.tt(POOL, sml[:, 16:20], sml[:, 16:20], cst["mhalf"][:, 0:4], ALU.pow,
                     reads=[sml_s, cst["mhalf_s"]], writes=[sml_s])
                yield
                yield
                yb, ybs = ybr.next()
                for qs_ in range(4):
                    if qs_ == 2:
                        yield
                    P.ts(DVE, yb[:, qs_, :], ob[:, qs_, :], sml[:, 16 + qs_:17 + qs_], None, ALU.mult,
                         reads=[obs, sml_s], writes=[ybs])
                yield
                post_b(yb, ybs, wi_, c)

            def post_b(yb, ybs, wi_, c):
                w = W[wi_]
                pt, pts = ptr.next()
                for qs_ in range(4):
                    P.tr(pt[:, qs_ * 128:(qs_ + 1) * 128], yb[:, qs_, :], idb[:], reads=[ybs, idb_s],
                         writes=[pts])
                ys, yss = ysr.next()
                P.cp(DVE, ys[:], pt, reads=[pts], writes=[yss])
                tt0 = w["t0"] + 512 * c
                P.dma(SP, yAT[w["h"], :, tt0:tt0 + 512], ys[:], yss, reads=[yss])

            ensure_kv(0)
            pend_post = None
            since_post = 0
            pendq = []
            for it in flat[0:2]:
                ensure_q(*it)
                pendq.append(qk(*it))
            for ii, (wi_, c, j) in enumerate(flat):
                w = W[wi_]
                kt, kts, va, vas = w["kv"]
                pscT, pscS, bias = pendq.pop(0)
                et, ets = etr.next()
                if bias is None:
                    P.act(et[:], pscT[:], AF.Exp, scale=0.125, reads=[pscS], writes=[ets])
                else:
                    P.act(et[:], pscT[:], AF.Exp, bias=bias, scale=0.125, reads=[pscS, blr_s], writes=[ets])
                if ii + 2 < len(flat):
                    ensure_q(*flat[ii + 2])
                    pendq.append(qk(*flat[ii + 2]))
                first, last_ = w["first"][c], w["last"][c]
                for a_ in range(8):
                    m, qs_ = a_ // 4, a_ % 4
                    ap_, as_ = acc_ap(a_)
                    P.mm(ap_, et[:, m * 512 + qs_ * 128:m * 512 + (qs_ + 1) * 128], va[:, j, 0:129],
                         start=(j == first and a_ % 3 == 0), stop=(j == last_), reads=[ets, vas],
                         writes=[as_], skip=True)
                if pend_post is not None:
                    if next(pend_post, "done") == "done":
                        pend_post = None
                if j == last_:
                    if pend_post is not None:
                        for _ in pend_post:
                            pass
                    pend_post = post(wi_, c)
                    next(pend_post)
                    qbufs.pop((wi_, c), None)
                if wi_ + 1 < len(work) and W[wi_ + 1]["kv"] is None and c == min(1, w["nq"] - 1) \
                        and j == min(first + 6, last_):
                    ensure_kv(wi_ + 1)
            if pend_post is not None:
                for _ in pend_post:
                    pass
            end_phase(st)

            st = phase_scope()
            idb, idb_s = load_ident(st, "pb")
            bb, bb_s = small(st, "pbbb", c_bb, [128, 3 * 8 * 128])
            skt, skt_s = small(st, "pbsink", p_sink[l], [128, 8])
            esink = sb(st, "pbesink", [128, 8], F32)
            esink_s = slot(st, "pbesink")
            P.act(esink[:], skt[:], AF.Exp, reads=[skt_s], writes=[esink_s])
            ktbs = [sb(st, "pbkt%d" % i, [128, 2, SMAX], BF16) for i in range(2)]
            ktb_ss = [slot(st, "pbkt%d" % i) for i in range(2)]
            ktz_s = slot(st, "pbktz")
            vbs = [sb(st, "pbvb%d" % i, [128, NKMAX, 2, 66], BF16) for i in range(2)]
            vb_ss = [slot(st, "pbvb%d" % i) for i in range(2)]
            vb1_s = slot(st, "pbvb1")
            for i in range(2):
                P.memset(POOL, ktbs[i][64:128, :, :], 0.0, writes=[ktz_s])
                P.memset(POOL, vbs[i][:, :, :, 64:66], 1.0, writes=[vb1_s])
            qbr = Ring(P, "pbqb", [sb(st, "pbqb%d" % i, [128, 4, 8, 128], BF16) for i in range(2)])
            qbz_s = slot(st, "pbqbz")
            for i in range(2):
                P.memset(POOL, qbr.tiles[i][64:128, :, :, :], 0.0, writes=[qbz_s])
            etbr = Ring(P, "pbet", [sb(st, "pbet%d" % i, [128, 512], BF16) for i in range(4)])
            ybtr = Ring(P, "pbyb", [sb(st, "pbyb%d" % i, [128, 512], BF16) for i in range(2)])
            ysbr = Ring(P, "pbys", [sb(st, "pbys%d" % i, [128, 4, 512], BF16) for i in range(2)])
            smbr = Ring(P, "pbsm", [sb(st, "pbsm%d" % i, [128, 16], F32) for i in range(2)])
            for r in (qbr, etbr, ybtr, ysbr, smbr):
                st.slots.extend(r.slots)
            psbr = Ring(P, "pbpsb", [ps(st, "pbpsb%d" % i, [128, 512], F32) for i in range(3)])
            paccB = [[ps(st, "pbpacc%d_%d" % (k, i), [128, 4, 65], F32) for i in range(2)] for k in range(2)]
            paccB_s = [[slot(st, "pbpacc%d_%d" % (k, i)) for i in range(2)] for k in range(2)]
            ptr = Ring(P, "pbptr", [ps(st, "pbptr0", [128, 512], BF16)[:]])
            st.slots.extend(psbr.slots)
            st.slots.extend(ptr.slots)

            def pb_post(k, ysb, ysbs, t, flush):
                smb, smb_s = smbr.next()
                for g in range(2):
                    P.tt(DVE, smb[:, 4 * g:4 * g + 4], paccB[k][g][:, :, 64], esink[:, 4 * g:4 * g + 4],
                         ALU.add, reads=[paccB_s[k][g], esink_s], writes=[smb_s])
                P.recip(smb[:, 8:16], smb[:, 0:8], reads=[smb_s], writes=[smb_s])
                ybt, ybts = ybtr.next()
                for hd in range(8):
                    P.ts(DVE, ybt[:, hd * 64:(hd + 1) * 64], paccB[k][hd // 4][:, hd % 4, 0:64],
                         smb[:, 8 + hd:9 + hd], None, ALU.mult, reads=[paccB_s[k][hd // 4], smb_s],
                         writes=[ybts])
                pt, pts = ptr.next()
                for fc in range(4):
                    P.tr(pt[:, fc * 128:(fc + 1) * 128], ybt[:, fc * 128:(fc + 1) * 128], idb[:],
                         reads=[ybts, idb_s], writes=[pts])
                P.cp(ACT, ysb[:, :, t * 128:(t + 1) * 128], pt.rearrange("p (f q) -> p f q", q=128),
                     reads=[pts], writes=[ysbs])
                if flush is not None:
                    P.dma(SP, flush, ysb[:], ysbs, reads=[ysbs])

            units = []
            tile_no = 0
            for si, S in enumerate(seqs):
                nk = S // 128
                for c in range(S // 512):
                    for t in range(4):
                        jq = 4 * c + t
                        rels = [r for r in range(3) if 0 <= jq + r - 1 < nk]
                        for g in range(2):
                            for ri, r in enumerate(rels):
                                units.append(dict(si=si, c=c, t=t, g=g, ri=ri, r=r, jk=jq + r - 1,
                                                  nrel=len(rels), k=tile_no % 2,
                                                  last=(g == 1 and ri == len(rels) - 1)))
                        tile_no += 1
            state = dict(si=None, c=None, qb=None, qbs=None, ysb=None, ysbs=None)

            def pb_qk(u):
                si, c = u["si"], u["c"]
                S = seqs[si]
                t0 = seq_off[si]
                nk = S // 128
                ktb, ktb_s, vb, vb_s = ktbs[si % 2], ktb_ss[si % 2], vbs[si % 2], vb_ss[si % 2]
                if state["si"] != si:
                    for g in range(2):
                        P.dma(SP, ktb[0:64, g, 0:S], qkB[8 + g, :, t0:t0 + S], ktb_s, writes=[ktb_s])
                        for j0 in range(0, nk, 8):
                            nj = min(8, nk - j0)
                            P.dma(SP, vb[:, j0:j0 + nj, g, 0:64],
                                  vAB[t0 + 128 * j0:t0 + 128 * (j0 + nj), 512 + 64 * g:576 + 64 * g].rearrange(
                                      "(j p) e -> p j e", p=128), vb_s, writes=[vb_s])
                    state["si"] = si
                    state["c"] = None
                if state["c"] != c:
                    qb, qbs = qbr.next()
                    for t in range(4):
                        P.dma(SP, qb[0:64, t, :, :],
                              qkB[0:8, :, t0 + 512 * c + 128 * t:t0 + 512 * c + 128 * (t + 1)].rearrange(
                                  "h d q -> d h q"), qbs, writes=[qbs])
                    state["qb"], state["qbs"] = qb, qbs
                    state["c"] = c
                qb, qbs = state["qb"], state["qbs"]
                g, r, jk, t = u["g"], u["r"], u["jk"], u["t"]
                pt, pts = psbr.next()
                P.mm(pt[:], ktb[:, g, jk * 128:(jk + 1) * 128], qb[:, t, 4 * g:4 * g + 4, :],
                     reads=[ktb_s, ktz_s, qbs, qbz_s], writes=[pts])
                P.tt(DVE, pt[:], pt[:], bb[:, (r * 8 + 4 * g) * 128:(r * 8 + 4 * g + 4) * 128],
                     ALU.add, reads=[pts, bb_s], writes=[pts])
                return pt, pts

            pending = None
            pq = [pb_qk(u) for u in units[0:2]]
            cur_ysb = {}
            for i, u in enumerate(units):
                pt, pts = pq.pop(0)
                et, ets = etbr.next()
                P.act(et[:], pt[:], AF.Exp, scale=0.125, reads=[pts], writes=[ets])
                if i + 2 < len(units):
                    pq.append(pb_qk(units[i + 2]))
                k, g, ri, jk = u["k"], u["g"], u["ri"], u["jk"]
                vb, vb_s = vbs[u["si"] % 2], vb_ss[u["si"] % 2]
                for hh in range(4):
                    P.mm(paccB[k][g][:, hh, :], et[:, hh * 128:(hh + 1) * 128], vb[:, jk, g, 0:65],
                         start=(ri == 0 and hh == 0), stop=(ri == u["nrel"] - 1), reads=[ets, vb_s, vb1_s],
                         writes=[paccB_s[k][g]], skip=True)
                if u["last"]:
                    key = (u["si"], u["c"])
                    if key not in cur_ysb:
                        cur_ysb.clear()
                        cur_ysb[key] = ysbr.next()
                    ysb, ysbs = cur_ysb[key]
                    if pending is not None:
                        pb_post(*pending)
                    flush = None
                    if u["t"] == 3:
                        tt0 = seq_off[u["si"]] + 512 * u["c"]
                        flush = yBT[:, :, tt0:tt0 + 512].rearrange("f p t -> p f t")
                    pending = (k, ysb, ysbs, u["t"], flush)
            pb_post(*pending)
            end_phase(st)

            st = phase_scope()
            idb, idb_s = load_ident(st, "pc")
            cst = mk_mhalf(st, "pc")
            stage = mk_stage(st, "pc", n=3)
            subc, subc_s = small(st, "pcsub", p_sub[l], [128, 1])
            wpa = sb(st, "pcwpa", [128, 4, D], BF16)
            wpa_s = slot(st, "pcwpa")
            wpb = sb(st, "pcwpb", [128, 4, D], BF16)
            wpb_s = slot(st, "pcwpb")
            wo = sb(st, "pcwo", [128, 8, D], BF16)
            wo_s = slot(st, "pcwo")
            subc4 = sb(st, "pcsub4", [128, 4], F32)
            subc4_s = slot(st, "pcsub4")
            for i in range(4):
                P.cp(DVE, subc4[:, i:i + 1], subc[:], reads=[subc_s], writes=[subc4_s])
            load_weight(st, "pc", wpa, wpa_s, w_pa[l], 4, D, subc4, subc4_s, s2=(1.0 - lam_init), stage=stage)
            load_weight(st, "pc", wpb, wpb_s, w_pb[l], 4, D, stage=stage)
            load_weight(st, "pc", wo, wo_s, w_out[l], 8, D, stage=stage)
            fcol, fcol_s = small(st, "pcfcol", p_fn[l], [128, 8])
            yar = Ring(P, "pcya", [sb(st, "pcya%d" % i, [128, 4, 512], BF16) for i in range(2)])
            ybr2 = Ring(P, "pcyb", [sb(st, "pcyb%d" % i, [128, 4, 512], BF16) for i in range(2)])
            ggr = Ring(P, "pcgg", [sb(st, "pcgg%d" % i, [128, 16, 512], BF16) for i in range(2)])
            xr = Ring(P, "pcx", [sb(st, "pcx%d" % i, [128, 4, D], F32) for i in range(2)])
            xts_all = [[slot(st, "pcx%d_%d" % (i, t)) for t in range(4)] for i in range(2)]
            t1r = Ring(P, "pct1", [sb(st, "pct1%d" % i, [128, 512], F32) for i in range(2)])
            t2r = Ring(P, "pct2", [sb(st, "pct2%d" % i, [128, 512], F32) for i in range(2)])
            for r in (yar, ybr2, ggr, xr, t1r, t2r):
                st.slots.extend(r.slots)
            mT = sb(st, "pcmT", [128, 8, 512], BF16)
            mT_s = slot(st, "pcmT")
            h2b = sb(st, "pch2", [128, 4, D], BF16)
            h2b_s = slot(st, "pch2")
            h2Ts = sb(st, "pch2T", [128, 8, 512], BF16)
            h2Ts_s = slot(st, "pch2T")
            junk = sb(st, "pcjunk", [128, D], BF16)
            junk_s = slot(st, "pcjunk", waw=True)
            ssq = sb(st, "pcssq", [128, 8], F32)
            ssq_s = slot(st, "pcssq")
            rstd = sb(st, "pcrstd", [128, 8], F32)
            rstd_s = slot(st, "pcrstd")
            ptr = Ring(P, "pcptr", [ps(st, "pcptr%d" % i, [128, 512], BF16)[:] for i in range(2)])
            pp = Ring(P, "pcpp", [ps(st, "pcpp%d" % i, [128, 512], F32) for i in range(6)])
            st.slots.extend(ptr.slots)
            st.slots.extend(pp.slots)

            def pc_load(c):
                t0 = 512 * c
                ya, yas = yar.next()
                yb_, ybs_ = ybr2.next()
                gg, ggs = ggr.next()
                xt, xs = xr.next()
                xs = xts_all[xr.i]
                P.dma(SP, ya[:], yAT[:, :, t0:t0 + 512].rearrange("f p t -> p f t"), yas, writes=[yas])
                P.dma(SP, yb_[:], yBT[:, :, t0:t0 + 512].rearrange("f p t -> p f t"), ybs_, writes=[ybs_])
                P.dma(SP, gg[:], gT[:, :, t0:t0 + 512].rearrange("c p t -> p c t"), ggs, writes=[ggs])
                P.dma(SP, xt[:], xsrc[t0:t0 + 512, :].rearrange("(t p) d -> p t d", p=128), xs[0], writes=xs)
                return ya, yas, yb_, ybs_, gg, ggs, xt, xs

            pc_pending = [None]

            def pc_flush():
                if pc_pending[0] is None:
                    return
                tp0 = pc_pending[0]
                transposes(h2b, h2b_s, h2Ts, h2Ts_s, ptr, idb, idb_s)
                P.dma(SP, h2T[:, :, tp0:tp0 + 512].rearrange("f p t -> p f t"), h2Ts[:], h2Ts_s, reads=[h2Ts_s])
                pc_pending[0] = None

            nxt = pc_load(0)
            for c in range(NCH):
                t0 = 512 * c
                ya, yas, yb_, ybs_, gg, ggs, xt, xs = nxt
                if c + 1 < NCH:
                    nxt = pc_load(c + 1)
                for cc in range(8):
                    pa_, pas_ = pp.next()
                    for fc in range(4):
                        P.mm(pa_[:], wpa[:, fc, cc * 128:(cc + 1) * 128], ya[:, fc, :], start=(fc == 0),
                             stop=(fc == 3), reads=[wpa_s, yas], writes=[pas_])
                    pb_, pbs_ = pp.next()
                    for fc in range(4):
                        P.mm(pb_[:], wpb[:, fc, cc * 128:(cc + 1) * 128], yb_[:, fc, :], start=(fc == 0),
                             stop=(fc == 3), reads=[wpb_s, ybs_], writes=[pbs_])
                    t1, t1s = t1r.next()
                    t2, t2s = t2r.next()
                    P.tt(DVE, t1[:], pa_[:], gg[:, cc, :], ALU.mult, reads=[pas_, ggs], writes=[t1s])
                    P.tt(DVE, t2[:], pb_[:], gg[:, 8 + cc, :], ALU.mult, reads=[pbs_, ggs], writes=[t2s])
                    P.tt(POOL, mT[:, cc, :], t1[:], t2[:], ALU.add, reads=[t1s, t2s], writes=[mT_s])
                    if cc == 7:
                        pc_flush()
                for t in range(4):
                    for hh in range(2):
                        pt, pts = pp.next()
                        for fc in range(8):
                            P.mm(pt[:], mT[:, fc, t * 128:(t + 1) * 128], wo[:, fc, hh * 512:(hh + 1) * 512],
                                 start=(fc == 0), stop=(fc == 7), reads=[mT_s, wo_s], writes=[pts])
                        P.tt(DVE, xt[:, t, hh * 512:(hh + 1) * 512], pt[:], xt[:, t, hh * 512:(hh + 1) * 512],
                             ALU.add, reads=[pts, xs[t]], writes=[xs[t]])
                    P.dma(SP, xmid[t0 + 128 * t:t0 + 128 * (t + 1), :], xt[:, t, :], xs[t], reads=[xs[t]])
                    rms_stats(cst, [xt[:, t, :]], xs[t], ssq[:, t:t + 1], ssq_s, rstd[:, t:t + 1], rstd_s, junk,
                              junk_s, 1, D)
                    P.act(h2b[:, t, :], xt[:, t, :], AF.Copy, scale=rstd[:, t:t + 1], reads=[xs[t], rstd_s],
                          writes=[h2b_s])
                pc_pending[0] = t0
            pc_flush()
            end_phase(st)

            st = phase_scope()
            cst = mk_mhalf(st, "pf")
            stage = mk_stage(st, "pf", n=3)
            fcol, fcol_s = small(st, "pffcol", p_fn[l], [128, 8])
            cw, cw_s = small(st, "pfcw", p_cw[l], [128, 3 * NFC])
            cb, cb_s = small(st, "pfcb", p_cb[l], [128, NFC])
            wu = sb(st, "pfwu", [128, 8, 2 * DFF], BF16)
            wu_s = slot(st, "pfwu")
            wd = sb(st, "pfwd", [128, NFC, D], BF16)
            wd_s = slot(st, "pfwd")
            load_weight(st, "pf", wu, wu_s, w_up[l], 8, 2 * DFF, fcol, fcol_s, stage=stage)
            load_weight(st, "pf", wd, wd_s, w_dn[l], NFC, D, stage=stage)
            if last:
                gfin, gfin_s = small(st, "pfgfin", p_fin, [128, D])
            hx = sb(st, "pfhx", [128, 8, 514], BF16)
            hx_s = slot(st, "pfhx")
            xq = [sb(st, "pfx%d" % i, [128, D], F32) for i in range(4)]
            xq_s = [slot(st, "pfx%d" % i) for i in range(4)]
            uT = sb(st, "pfuT", [128, NFC, 512], BF16)
            uT_s = slot(st, "pfuT")
            cr = Ring(P, "pfc", [sb(st, "pfc%d" % i, [128, 512], F32) for i in range(2)])
            grr = Ring(P, "pfg", [sb(st, "pfg%d" % i, [128, 512], F32) for i in range(2)])
            st.slots.extend(cr.slots)
            st.slots.extend(grr.slots)
            junk = sb(st, "pfjunk", [128, D], BF16)
            junk_s = slot(st, "pfjunk", waw=True)
            ssq = sb(st, "pfssq", [128, 8], F32)
            ssq_s = slot(st, "pfssq")
            rstd = sb(st, "pfrstd", [128, 8], F32)
            rstd_s = slot(st, "pfrstd")
            pp = Ring(P, "pfpp", [ps(st, "pfpp%d" % i, [128, 512], F32) for i in range(7)])
            st.slots.extend(pp.slots)
            phal = ps(st, "pfhal", [128, 512], F32)
            phr = Ring(P, "pfhal", [phal[:, 0:2]])
            st.slots.extend(phr.slots)
            def load_hx(c):
                t0 = 512 * c
                si, cs = chunk_seq[c]
                first_c = (cs == 0)
                last_c = (cs == seqs[si] // 512 - 1)
                lo = 0 if first_c else 1
                hi = 0 if last_c else 1
                if first_c:
                    P.memset(POOL, hx[:, :, 0:1], 0.0, writes=[hx_s])
                if last_c:
                    P.memset(POOL, hx[:, :, 513:514], 0.0, writes=[hx_s])
                P.dma(SP, hx[:, :, 1 - lo:513 + hi], h2T[:, :, t0 - lo:t0 + 512 + hi].rearrange("f p t -> p f t"),
                      hx_s, writes=[hx_s])

            load_hx(0)
            for c in range(NCH):
                t0 = 512 * c
                for t in range(4):
                    P.dma(dma_eng(), xq[t][:], xmid[t0 + 128 * t:t0 + 128 * (t + 1), :], xq_s[t], writes=[xq_s[t]])
                for cc in range(NFC):
                    pa_, pas_ = pp.next()
                    ph_, phs_ = phr.next()
                    for kc in range(8):
                        P.mm(pa_[:], wu[:, kc, cc * 128:(cc + 1) * 128], hx[:, kc, 1:513], start=(kc == 0),
                             stop=(kc == 7), reads=[wu_s, hx_s], writes=[pas_])
                    pv_, pvs_ = pp.next()
                    for kc in range(8):
                        P.mm(pv_[:], wu[:, kc, DFF + cc * 128:DFF + (cc + 1) * 128], hx[:, kc, 1:513],
                             start=(kc == 0), stop=(kc == 7), reads=[wu_s, hx_s], writes=[pvs_])
                    for kc in range(8):
                        P.mm(ph_, wu[:, kc, cc * 128:(cc + 1) * 128], hx[:, kc, 0:514:513], start=(kc == 0),
                             stop=(kc == 7), reads=[wu_s, hx_s], writes=[phs_])
                    ct, cs_ = cr.next()
                    w0 = cw[:, 0 * NFC + cc:0 * NFC + cc + 1]
                    w1 = cw[:, 1 * NFC + cc:1 * NFC + cc + 1]
                    w2 = cw[:, 2 * NFC + cc:2 * NFC + cc + 1]
                    P.act(ct[:], pa_[:], AF.Identity, bias=cb[:, cc:cc + 1], scale=w1, reads=[pas_, cb_s, cw_s],
                          writes=[cs_])
                    P.stt(ct[:, 1:512], pa_[:, 0:511], w0, ct[:, 1:512], ALU.mult, ALU.add,
                          reads=[pas_, cw_s, cs_], writes=[cs_])
                    P.stt(ct[:, 0:511], pa_[:, 1:512], w2, ct[:, 0:511], ALU.mult, ALU.add,
                          reads=[pas_, cw_s, cs_], writes=[cs_])
                    P.stt(ct[:, 0:1], ph_[:, 0:1], w0, ct[:, 0:1], ALU.mult, ALU.add, reads=[phs_, cw_s, cs_],
                          writes=[cs_])
                    P.stt(ct[:, 511:512], ph_[:, 1:2], w2, ct[:, 511:512], ALU.mult, ALU.add,
                          reads=[phs_, cw_s, cs_], writes=[cs_])
                    gt_, gs_ = grr.next()
                    P.act(gt_[:], ct[:], AF.Gelu_apprx_tanh, reads=[cs_], writes=[gs_])
                    P.tt(DVE, uT[:, cc, :], pv_[:], gt_[:], ALU.mult, reads=[pvs_, gs_], writes=[uT_s])
                if c + 1 < NCH:
                    load_hx(c + 1)
                for t in range(4):
                    for hh in range(2):
                        pt, pts = pp.next()
                        for fc in range(NFC):
                            P.mm(pt[:], uT[:, fc, t * 128:(t + 1) * 128], wd[:, fc, hh * 512:(hh + 1) * 512],
                                 start=(fc == 0), stop=(fc == NFC - 1), reads=[uT_s, wd_s], writes=[pts])
                        P.tt(DVE, xq[t][:, hh * 512:(hh + 1) * 512], pt[:], xq[t][:, hh * 512:(hh + 1) * 512],
                             ALU.add, reads=[pts, xq_s[t]], writes=[xq_s[t]])
                    if not last:
                        P.dma(dma_eng(), xres[t0 + 128 * t:t0 + 128 * (t + 1), :], xq[t][:], xq_s[t],
                              reads=[xq_s[t]])
                    else:
                        rms_stats(cst, [xq[t][:]], xq_s[t], ssq[:, t:t + 1], ssq_s, rstd[:, t:t + 1], rstd_s,
                                  junk, junk_s, 1, D)
                        P.stt(xq[t][:], xq[t][:], rstd[:, t:t + 1], gfin[:], ALU.mult, ALU.mult,
                              reads=[xq_s[t], rstd_s, gfin_s], writes=[xq_s[t]])
                        P.dma(dma_eng(), yout[t0 + 128 * t:t0 + 128 * (t + 1), :], xq[t][:], xq_s[t],
                              reads=[xq_s[t]])
            end_phase(st)
    return nc


def _bf16_round(x):
    x = np.asarray(x, np.float32)
    u = x.view(np.uint32).astype(np.uint64)
    r = ((u + 0x7FFF + ((u >> 16) & 1)) & 0xFFFF0000).astype(np.uint32)
    return r.view(np.float32)


def make_consts():
    c = {}
    c["c_ident"] = np.eye(128, dtype=np.float32)
    qi = np.arange(512, dtype=np.float64)
    aug = np.zeros((4, 3, 3, 2, 512), np.float32)
    for h in range(4):
        for v, sgn in enumerate((-1.0, 1.0)):
            val = (sgn * 8.0 * SLOPES_A[h] * qi).astype(np.float32)
            hi = _bf16_round(val)
            mid = _bf16_round(val - hi)
            lo = _bf16_round(val - hi - mid)
            for r, part in enumerate((hi, mid, lo)):
                aug[h, r, v, :, :] = part[None, :]
    c["c_augq"] = aug
    ki = np.arange(128, dtype=np.float64)[:, None]
    m = np.arange(64, dtype=np.float64)[None, :]
    blr = np.zeros((128, 4, 2, 64), np.float32)
    for h in range(4):
        blr[:, h, 0, :] = SLOPES_A[h] * (ki - 128.0 * m)
        blr[:, h, 1, :] = -SLOPES_A[h] * (128.0 * m + ki)
    c["c_blr"] = blr.reshape(128, -1)
    xx = np.arange(896, dtype=np.float64)[None, :]
    toep = np.zeros((128, 4, 896), np.float32)
    for h in range(4):
        toep[:, h, :] = -8.0 * SLOPES_A[h] * np.abs(xx - 384.0 - ki)
    c["c_toep"] = toep.reshape(128, -1)
    qq = np.arange(128, dtype=np.float64)[None, :]
    bb = np.zeros((128, 3, 8, 128), np.float32)
    for r in range(3):
        dist = np.abs(128.0 * (r - 1) + ki - qq)
        for h in range(8):
            bb[:, r, h, :] = np.where(dist <= 128.0, -8.0 * SLOPES_B[h] * dist, 8.0 * NEGBIG)
    c["c_bb"] = bb.reshape(128, -1)
    return c


def layout_params(inp):
    f = lambda a: np.ascontiguousarray(np.asarray(a, np.float32))
    p = {}
    p["p_an"] = f(np.asarray(inp["attn_norm"]).reshape(DEPTH, 8, 128).transpose(0, 2, 1))
    p["p_fn"] = f(np.asarray(inp["ffn_norm"]).reshape(DEPTH, 8, 128).transpose(0, 2, 1))
    p["p_gb"] = f(np.asarray(inp["gate_bias"]).reshape(DEPTH, 16, 128).transpose(0, 2, 1))
    lam = np.concatenate([np.asarray(inp[k]) for k in ("lambda_q1", "lambda_k1", "lambda_q2", "lambda_k2")],
                         axis=1)
    p["p_lam"] = f(np.broadcast_to(lam[:, None, :], (DEPTH, 128, 256)))
    p["p_sub"] = f(np.asarray(inp["subln"]).reshape(DEPTH, 128, 1))
    p["p_sink"] = f(np.broadcast_to(np.asarray(inp["sink"])[:, None, :], (DEPTH, 128, 8)))
    p["p_cw"] = f(np.asarray(inp["conv_w"]).reshape(DEPTH, 3, NFC, 128).transpose(0, 3, 1, 2).reshape(
        DEPTH, 128, 3 * NFC))
    p["p_cb"] = f(np.asarray(inp["conv_b"]).reshape(DEPTH, NFC, 128).transpose(0, 2, 1))
    p["p_fin"] = f(np.broadcast_to(np.asarray(inp["final_norm"])[None, :], (128, D)))
    for k in ("w_in", "w_proj_a", "w_proj_b", "w_out", "w_up", "w_down"):
        p[k] = f(inp[k])
    return p


_NC_CACHE = {}


def run(core_x, inp, seqs, n_layers=DEPTH):
    key = (tuple(seqs), n_layers)
    if key not in _NC_CACHE:
        _NC_CACHE[key] = build(list(seqs), n_layers)
    nc = _NC_CACHE[key]
    shared = dict(make_consts())
    shared.update(layout_params(inp))
    in_maps = []
    for x in core_x:
        m = dict(shared)
        m["xin"] = np.ascontiguousarray(x, dtype=np.float32)
        in_maps.append(m)
    res = run_bass_kernel_spmd(nc, in_maps, core_ids=list(range(len(core_x))))
    return [r["yout"] for r in res.results]


def kernel(**inputs):
    xp = np.asarray(inputs["x_prompt"], np.float32)
    xs = np.asarray(inputs["x_sample"], np.float32)
    nb_p = xp.shape[0] // N_CORES
    nb_s = xs.shape[0] // N_CORES
    seqs = [xp.shape[1]] * nb_p + [xs.shape[1]] * nb_s
    core_x = []
    for i in range(N_CORES):
        parts = [xp[i * nb_p + b] for b in range(nb_p)] + [xs[i * nb_s + b] for b in range(nb_s)]
        core_x.append(np.concatenate(parts, axis=0))
    outs = run(core_x, inputs, seqs)
    yp = np.empty_like(xp)
    ys = np.empty_like(xs)
    for i in range(N_CORES):
        o = outs[i]
        off = 0
        for b in range(nb_p):
            yp[i * nb_p + b] = o[off:off + xp.shape[1]]
            off += xp.shape[1]
        for b in range(nb_s):
            ys[i * nb_s + b] = o[off:off + xs.shape[1]]
            off += xs.shape[1]
    return (yp, ys)
```

```python
import contextlib
import math
import numpy as np
import concourse.bass as bass
import concourse.mybir as mybir
from concourse.bass_utils import run_bass_kernel_spmd

F32 = mybir.dt.float32
BF16 = mybir.dt.bfloat16
AF = mybir.ActivationFunctionType
ALU = mybir.AluOpType

PE, ACT, DVE, POOL, SP = "tensor", "scalar", "vector", "gpsimd", "sync"
ENGS = [PE, ACT, DVE, POOL, SP]

D = 1024
DEPTH = 2
IN_COLS = 4352
DFF = 2816
NFC = 22
EPS = 1e-6
N_CORES = 8
STOP_AFTER = 10 ** 9
SLOPES_A = [2.0 ** (-8.0 * (h + 1) / 4) for h in range(4)]
SLOPES_B = [2.0 ** (-8.0 * (h + 1) / 8) for h in range(8)]
NEGBIG = -30000.0
SKIP_EXP = 88.0 + 92.3


class Slot:
    __slots__ = ("name", "writers", "readers", "prev_readers", "dsem", "waw")

    def __init__(self, name, waw=False):
        self.name = name
        self.waw = waw
        self.writers = []
        self.readers = []
        self.prev_readers = []
        self.dsem = None


class Op:
    __slots__ = ("eng", "fn", "deps", "signal", "count", "dma", "dtok")

    def __init__(self, eng, fn):
        self.eng = eng
        self.fn = fn
        self.deps = []
        self.signal = False
        self.count = None
        self.dma = False
        self.dtok = None


class Prog:
    def __init__(self, nc, stack, n_dma_sems=88):
        self.nc = nc
        self.ops = {e: [] for e in ENGS}
        self.esem = {e: stack.enter_context(nc.semaphore("S_" + e)) for e in ENGS}
        self.dsem = [stack.enter_context(nc.semaphore("D%d" % i)) for i in range(n_dma_sems)]
        self.dcount = [0] * n_dma_sems
        self.next_dsem = 0
        self.free_dsems = []
        self.ecount = {e: 0 for e in ENGS}
        self.waited_e = {e: {x: 0 for x in ENGS} for e in ENGS}
        self.waited_d = {e: [0] * n_dma_sems for e in ENGS}
        self.slots = []
        self.last_sig = {e: None for e in ENGS}

    def slot(self, name, waw=False):
        s = Slot(name, waw)
        self.slots.append(s)
        return s

    def release_slots(self, slots):
        for s in slots:
            if s.dsem is not None:
                self.free_dsems.append(s.dsem)
                s.dsem = None
        ids = set(id(s) for s in slots)
        self.slots = [s for s in self.slots if id(s) not in ids]

    def _mkdeps(self, op, reads, writes):
        deps = []
        for s in reads:
            deps.extend(s.writers)
        for s in writes:
            if s.readers:
                s.prev_readers = s.readers
                deps.extend(s.writers)
                s.readers = []
                s.writers = []
            deps.extend(s.prev_readers)
            if s.waw:
                deps.extend(s.writers)
        for s in reads:
            s.readers.append(op)
        for s in writes:
            s.writers.append(op)
        seen = set()
        out = []
        for d in deps:
            if d is op or id(d) in seen:
                continue
            seen.add(id(d))
            out.append(d)
        return out

    def op(self, eng, fn, reads=(), writes=(), extra=()):
        o = Op(eng, fn)
        o.deps = self._mkdeps(o, reads, writes) + list(extra)
        for d in o.deps:
            if not d.dma:
                d.signal = True
        self.ops[eng].append(o)
        return o

    def dma(self, eng, out, in_, sb, reads=(), writes=(), extra=(), **kw):
        o = Op(eng, lambda e: e.dma_start(out=out, in_=in_, **kw))
        o.dma = True
        o.deps = self._mkdeps(o, reads, writes) + list(extra)
        for d in o.deps:
            if not d.dma:
                d.signal = True
        if sb.dsem is None:
            if self.free_dsems:
                sb.dsem = self.free_dsems.pop()
            else:
                sb.dsem = self.next_dsem
                self.next_dsem += 1
                assert self.next_dsem <= len(self.dsem), "out of DMA semaphores"
        self.dcount[sb.dsem] += 16
        o.dtok = (sb.dsem, self.dcount[sb.dsem])
        self.ops[eng].append(o)
        return o

    def mm(self, out, lhsT, rhs, start=True, stop=True, reads=(), writes=(), skip=False):
        return self.op(PE, lambda e: e.matmul(out, lhsT=lhsT, rhs=rhs, start=start, stop=stop,
                                              skip_group_check=skip), reads, writes)

    def tr(self, out, in_, ident, reads=(), writes=()):
        return self.op(PE, lambda e: e.transpose(out=out, in_=in_, identity=ident), reads, writes)

    def act(self, out, in_, func, bias=None, scale=None, accum=None, reads=(), writes=()):
        kw = {}
        if bias is not None:
            kw["bias"] = bias
        if scale is not None:
            kw["scale"] = scale
        if accum is not None:
            kw["accum_out"] = accum
        return self.op(ACT, lambda e: e.activation(out=out, in_=in_, func=func, **kw), reads, writes)

    def ts(self, eng, out, in0, s1, s2, op0, op1=None, reads=(), writes=()):
        if op1 is None:
            return self.op(eng, lambda e: e.tensor_scalar(out=out, in0=in0, scalar1=s1, scalar2=None,
                                                          op0=op0), reads, writes)
        return self.op(eng, lambda e: e.tensor_scalar(out=out, in0=in0, scalar1=s1, scalar2=s2,
                                                      op0=op0, op1=op1), reads, writes)

    def tt(self, eng, out, in0, in1, op, reads=(), writes=()):
        return self.op(eng, lambda e: e.tensor_tensor(out=out, in0=in0, in1=in1, op=op), reads, writes)

    def stt(self, out, in0, scalar, in1, op0, op1, accum=None, reads=(), writes=()):
        if accum is None:
            return self.op(DVE, lambda e: e.scalar_tensor_tensor(out=out, in0=in0, scalar=scalar, in1=in1,
                                                                 op0=op0, op1=op1), reads, writes)
        return self.op(DVE, lambda e: e.scalar_tensor_tensor(out=out, in0=in0, scalar=scalar, in1=in1,
                                                             op0=op0, op1=op1, accum_out=accum),
                       reads, writes)

    def cp(self, eng, out, in_, reads=(), writes=()):
        if eng == ACT:
            return self.op(ACT, lambda e: e.copy(out=out, in_=in_), reads, writes)
        return self.op(eng, lambda e: e.tensor_copy(out=out, in_=in_), reads, writes)

    def memset(self, eng, ap, val, writes=()):
        return self.op(eng, lambda e: e.memset(ap, val), (), writes)

    def recip(self, out, in_, reads=(), writes=()):
        return self.op(DVE, lambda e: e.reciprocal(out=out, in_=in_), reads, writes)

    def barrier(self):
        lasts = []
        for e in ENGS:
            for o in reversed(self.ops[e]):
                if not o.dma:
                    lasts.append(o)
                    break
            else:
                if self.last_sig[e] is not None:
                    lasts.append(self.last_sig[e])
        dtoks = [(i, c) for i, c in enumerate(self.dcount) if c > 0]
        for e in ENGS:
            o = Op(e, ("barrier", dtoks))
            o.deps = [l for l in lasts if l.eng != e]
            for d in o.deps:
                d.signal = True
            self.ops[e].append(o)
        for s in self.slots:
            s.writers = []
            s.readers = []
            s.prev_readers = []

    def emit(self):
        nc = self.nc
        for e in ENGS:
            for o in self.ops[e]:
                if o.signal and not o.dma and o.count is None:
                    self.ecount[e] += 1
                    o.count = self.ecount[e]
                    self.last_sig[e] = o
        esem, dsem = self.esem, self.dsem

        def run(en, e):
            we = self.waited_e[en]
            wd = self.waited_d[en]
            for o in self.ops[en]:
                dmax = {}
                for d in o.deps:
                    if d.dma:
                        si, v = d.dtok
                        if dmax.get(si, 0) < v:
                            dmax[si] = v
                for si, v in dmax.items():
                    if wd[si] < v:
                        e.wait_ge(dsem[si], v)
                        wd[si] = v
                for d in o.deps:
                    if d.dma:
                        continue
                    else:
                        if d.eng == en and en == PE:
                            continue
                        if we[d.eng] < d.count:
                            e.wait_ge(esem[d.eng], d.count)
                            we[d.eng] = d.count
                if isinstance(o.fn, tuple):
                    for si, v in o.fn[1]:
                        if wd[si] < v:
                            e.wait_ge(dsem[si], v)
                            wd[si] = v
                    if o.signal:
                        e.nop().then_inc(esem[en], 1)
                    continue
                ins = o.fn(e)
                if o.dma:
                    ins.then_inc(dsem[o.dtok[0]], 16)
                elif o.signal:
                    ins.then_inc(esem[en], 1)

        with nc.Block() as block:
            @block.tensor
            def _(e):
                run(PE, e)

            @block.scalar
            def _(e):
                run(ACT, e)

            @block.vector
            def _(e):
                run(DVE, e)

            @block.gpsimd
            def _(e):
                run(POOL, e)

            @block.sync
            def _(e):
                run(SP, e)
        self.ops = {e: [] for e in ENGS}


class Ring:
    def __init__(self, P, name, tiles):
        self.tiles = tiles
        self.slots = [P.slot("%s%d" % (name, i)) for i in range(len(tiles))]
        self.i = -1

    def next(self):
        self.i = (self.i + 1) % len(self.tiles)
        return self.tiles[self.i], self.slots[self.i]

    def cur(self):
        return self.tiles[self.i], self.slots[self.i]


def build(seqs, n_layers=DEPTH):
    T = sum(seqs)
    SMAX = max(seqs)
    NKMAX = SMAX // 128
    nc = bass.Bass("TRN2", target_bir_lowering=False)

    def din(name, shape, dt=F32):
        return nc.dram_tensor(name, list(shape), dt, kind="ExternalInput").ap()

    def dscr(name, shape, dt):
        return nc.dram_tensor(name, list(shape), dt, kind="Internal").ap()

    xin = din("xin", [T, D])
    w_in = din("w_in", [DEPTH, D, IN_COLS])
    w_pa = din("w_proj_a", [DEPTH, 512, D])
    w_pb = din("w_proj_b", [DEPTH, 512, D])
    w_out = din("w_out", [DEPTH, D, D])
    w_up = din("w_up", [DEPTH, D, 2 * DFF])
    w_dn = din("w_down", [DEPTH, DFF, D])
    p_an = din("p_an", [DEPTH, 128, 8])
    p_fn = din("p_fn", [DEPTH, 128, 8])
    p_gb = din("p_gb", [DEPTH, 128, 16])
    p_lam = din("p_lam", [DEPTH, 128, 256])
    p_sub = din("p_sub", [DEPTH, 128, 1])
    p_sink = din("p_sink", [DEPTH, 128, 8])
    p_cw = din("p_cw", [DEPTH, 128, 3 * NFC])
    p_cb = din("p_cb", [DEPTH, 128, NFC])
    p_fin = din("p_fin", [128, D])
    c_ident = din("c_ident", [128, 128])
    c_augq = din("c_augq", [4, 3, 3, 2, 512])
    c_blr = din("c_blr", [128, 4 * 2 * 64])
    c_toep = din("c_toep", [128, 4 * 896])
    c_bb = din("c_bb", [128, 3 * 8 * 128])
    yout = nc.dram_tensor("yout", [T, D], F32, kind="ExternalOutput").ap()

    qkA = dscr("s_qkA", [16, 64, T], BF16)
    qkB = dscr("s_qkB", [10, 64, T], BF16)
    vAB = dscr("s_vAB", [T, 640], BF16)
    gT = dscr("s_gT", [16, 128, T], BF16)
    yAT = dscr("s_yAT", [4, 128, T], BF16)
    yBT = dscr("s_yBT", [4, 128, T], BF16)
    xmid = dscr("s_xmid", [T, D], F32)
    h2T = dscr("s_h2T", [8, 128, T], BF16)
    xres = dscr("s_xres", [T, D], F32)

    seq_off = [sum(seqs[:i]) for i in range(len(seqs))]
    NCH = T // 512
    chunk_seq = []
    for si, S in enumerate(seqs):
        for c in range(S // 512):
            chunk_seq.append((si, c))

    with contextlib.ExitStack() as top:
        P = Prog(nc, top)

        def phase_scope():
            st = contextlib.ExitStack()
            st.slots = []
            return st

        uniq = [0]

        def sb(st, name, shape, dt):
            uniq[0] += 1
            return st.enter_context(nc.sbuf_tensor("%s_%d" % (name, uniq[0]), list(shape), dt))

        def ps(st, name, shape, dt):
            uniq[0] += 1
            return st.enter_context(nc.psum_tensor("%s_%d" % (name, uniq[0]), list(shape), dt))

        def slot(st, name, waw=False):
            s = P.slot(name, waw)
            st.slots.append(s)
            return s

        nphase = [0]

        class StopBuild(Exception):
            pass

        def end_phase(st):
            P.barrier()
            P.emit()
            P.release_slots(st.slots)
            st.close()
            nphase[0] += 1
            if nphase[0] >= STOP_AFTER:
                raise StopBuild()

        dmaq = [SP, POOL]
        dq = [0]

        def dma_eng():
            return SP

        def load_ident(st, pfx):
            idf = sb(st, pfx + "idf", [128, 128], F32)
            idb = sb(st, pfx + "idb", [128, 128], BF16)
            s1, s2 = slot(st, pfx + "idf"), slot(st, pfx + "idb")
            P.dma(SP, idf[:], c_ident[:, :], s1, writes=[s1])
            P.cp(DVE, idb[:], idf[:], reads=[s1], writes=[s2])
            return idb, s2

        cast_rr = [0]

        class WSlots:
            def __init__(self, st, name, ranges):
                self.ranges = [(a_, b_, slot(st, "%s_%d" % (name, a_))) for (a_, b_) in ranges]

            def s(self, col):
                for a_, b_, sl in self.ranges:
                    if a_ <= col < b_:
                        return sl
                raise KeyError(col)

        def load_weight(st, pfx, dst, dst_slot, src, n_k, n_cols, scol=None, scol_slot=None, s2=None,
                        stage=None):
            stg_ring = stage
            cb = 1024
            if isinstance(dst_slot, WSlots):
                wsl = dst_slot
                order = [(kc, a_, b_ - a_, sl) for (a_, b_, sl) in wsl.ranges for kc in range(n_k)]
            else:
                order = [(kc, c0, min(cb, n_cols - c0), dst_slot) for kc in range(n_k)
                         for c0 in range(0, n_cols, cb)]
            if True:
                for (kc, c0, cw, dst_slot) in order:
                    stg, sslot = stg_ring.next()
                    P.dma(dma_eng(), stg[:, 0:cw], src[kc * 128:(kc + 1) * 128, c0:c0 + cw], sslot,
                          writes=[sslot])
                    o = dst[:, kc, c0:c0 + cw]
                    if scol is None:
                        P.cp(POOL, o, stg[:, 0:cw], reads=[sslot], writes=[dst_slot])
                    else:
                        sc = scol[:, kc:kc + 1]
                        P.ts(POOL, o, stg[:, 0:cw], sc, float(1.0 if s2 is None else s2), ALU.mult, ALU.mult,
                             reads=[sslot, scol_slot], writes=[dst_slot])

        def mk_stage(st, pfx, n=2):
            tiles = [sb(st, "%sstg%d" % (pfx, i), [128, 1024], F32) for i in range(n)]
            r = Ring(P, pfx + "stg", tiles)
            st.slots.extend(r.slots)
            return r

        def small(st, name, src, shape):
            t = sb(st, name, shape, F32)
            s = slot(st, name)
            P.dma(SP, t[:], src, s, writes=[s])
            return t, s

        def rms_stats(st_tiles, x_ap_list, x_slot, ssq, ssq_slot, rstd, rstd_slot, junk, junk_slot, n, dim):
            for i in range(n):
                P.act(junk[:, 0:dim], x_ap_list[i], AF.Square, accum=ssq[:, i:i + 1], reads=[x_slot],
                      writes=[junk_slot, ssq_slot])
            P.ts(POOL, rstd[:, 0:n], ssq[:, 0:n], 1.0 / dim, EPS, ALU.mult, ALU.add, reads=[ssq_slot],
                 writes=[rstd_slot])
            P.tt(POOL, rstd[:, 0:n], rstd[:, 0:n], st_tiles["mhalf"][:, 0:n], ALU.pow,
                 reads=[rstd_slot, st_tiles["mhalf_s"]], writes=[rstd_slot])

        def mk_mhalf(st, pfx):
            t = sb(st, pfx + "mhalf", [128, 8], F32)
            s = slot(st, pfx + "mhalf")
            P.memset(POOL, t[:], -0.5, writes=[s])
            return {"mhalf": t, "mhalf_s": s}

        top.push(lambda et, ev, tb: et is StopBuild)
        for l in range(n_layers):
            lam_init = 0.8 - 0.6 * math.exp(-0.3 * l)
            xsrc = xin if l == 0 else xres
            last = (l == n_layers - 1)

            st = phase_scope()
            idb, idb_s = load_ident(st, "p1")
            cst = mk_mhalf(st, "p1")
            wi = sb(st, "p1wi", [128, 8, IN_COLS], BF16)
            wi_w = WSlots(st, "p1wi", [(0, 512), (512, 1024), (1536, 2176), (2304, 2816), (2816, 3328),
                                       (3328, 3840), (3840, 4352), (1024, 1536), (2176, 2304)])
            stage = mk_stage(st, "p1", n=3)
            gcol, gcol_s = small(st, "p1gcol", p_an[l], [128, 8])
            gbias, gbias_s = small(st, "p1gbias", p_gb[l], [128, 16])

            xr = Ring(P, "p1x", [sb(st, "p1x%d" % i, [128, 4, D], F32) for i in range(2)])
            st.slots.extend(xr.slots)
            hb = sb(st, "p1h", [128, 4, D], BF16)
            hb_s = slot(st, "p1h")
            hT = sb(st, "p1hT", [128, 8, 512], BF16)
            hT_s = slot(st, "p1hT")
            stA = sb(st, "p1stA", [128, 8, 512], BF16)
            stA_s = slot(st, "p1stA")
            stB = sb(st, "p1stB", [128, 5, 512], BF16)
            stB_s = slot(st, "p1stB")
            gr = Ring(P, "p1g", [sb(st, "p1g%d" % i, [128, 8, 512], BF16) for i in range(2)])
            st.slots.extend(gr.slots)
            vr = Ring(P, "p1v", [sb(st, "p1v%d" % i, [128, 4, 640], BF16) for i in range(2)])
            st.slots.extend(vr.slots)
            junk = sb(st, "p1junk", [128, D], BF16)
            junk_s = slot(st, "p1junk", waw=True)
            ssq = sb(st, "p1ssq", [128, 8], F32)
            ssq_s = slot(st, "p1ssq")
            rstd = sb(st, "p1rstd", [128, 8], F32)
            rstd_s = slot(st, "p1rstd")
            ptr = Ring(P, "p1ptr", [ps(st, "p1ptr%d" % i, [128, 512], BF16)[:] for i in range(2)])
            st.slots.extend(ptr.slots)
            pp = Ring(P, "p1pp", [ps(st, "p1pp%d" % i, [128, 512], F32) for i in range(6)])
            st.slots.extend(pp.slots)
            evac_rr = [0]

            def evac_eng():
                evac_rr[0] += 1
                return DVE if evac_rr[0] % 3 else ACT

            def p1_load(c):
                xt, xs = xr.next()
                t0 = 512 * c
                P.dma(SP, xt[:], xsrc[t0:t0 + 512, :].rearrange("(t p) d -> p t d", p=128), xs, writes=[xs])
                return xt, xs

            def p1_norm(xt, xs):
                rms_stats(cst, [xt[:, t, :] for t in range(4)], xs, ssq, ssq_s, rstd, rstd_s, junk, junk_s,
                          4, D)
                for t in range(4):
                    P.ts(DVE, hb[:, t, :], xt[:, t, :], rstd[:, t:t + 1], None, ALU.mult,
                         reads=[xs, rstd_s], writes=[hb_s])

            def transposes(src, src_s, dstT, dstT_s, ptr_ring, idb, idb_s, nfc=8):
                for fc in range(nfc):
                    pt, pts = ptr_ring.next()
                    for t in range(4):
                        P.tr(pt[:, t * 128:(t + 1) * 128], src[:, t, fc * 128:(fc + 1) * 128], idb[:],
                             reads=[src_s, idb_s], writes=[pts])
                    P.cp(evac_eng(), dstT[:, fc, :], pt, reads=[pts], writes=[dstT_s])

            nxt = p1_load(0)
            p1_norm(*nxt)
            load_weight(st, "p1", wi, wi_w, w_in[l], 8, IN_COLS, gcol, gcol_s, stage=stage)
            for c in range(NCH):
                t0 = 512 * c
                transposes(hb, hb_s, hT, hT_s, ptr, idb, idb_s)
                if c + 1 < NCH:
                    nxt = p1_load(c + 1)
                    p1_norm(*nxt)
                for i in range(8):
                    pt, pts = pp.next()
                    for kc in range(8):
                        P.mm(pt[:], wi[:, kc, i * 128:(i + 1) * 128], hT[:, kc, :], start=(kc == 0),
                             stop=(kc == 7), reads=[wi_w.s(i * 128), hT_s], writes=[pts])
                    P.cp(evac_eng(), stA[:, i, :], pt[:], reads=[pts], writes=[stA_s])
                P.dma(SP, qkA[:, :, t0:t0 + 512].rearrange("(i two) d t -> (two d) i t", two=2), stA[:], stA_s,
                      reads=[stA_s])
                for i in range(5):
                    pt, pts = pp.next()
                    c0 = 1536 + i * 128
                    for kc in range(8):
                        P.mm(pt[:], wi[:, kc, c0:c0 + 128], hT[:, kc, :], start=(kc == 0),
                             stop=(kc == 7), reads=[wi_w.s(c0), hT_s], writes=[pts])
                    P.cp(evac_eng(), stB[:, i, :], pt[:], reads=[pts], writes=[stB_s])
                P.dma(SP, qkB[:, :, t0:t0 + 512].rearrange("(i two) d t -> (two d) i t", two=2), stB[:], stB_s,
                      reads=[stB_s])
                for half in range(2):
                    gt, gs = gr.next()
                    for i in range(8):
                        gi = half * 8 + i
                        c0 = 2304 + gi * 128
                        pt, pts = pp.next()
                        for kc in range(8):
                            P.mm(pt[:], wi[:, kc, c0:c0 + 128], hT[:, kc, :], start=(kc == 0), stop=(kc == 7),
                                 reads=[wi_w.s(c0), hT_s], writes=[pts])
                        P.act(gt[:, i, :], pt[:], AF.Sigmoid, bias=gbias[:, gi:gi + 1], reads=[pts, gbias_s],
                              writes=[gs])
                    P.dma(SP, gT[half * 8:half * 8 + 8, :, t0:t0 + 512].rearrange("c p t -> p c t"), gt[:], gs,
                          reads=[gs])
                vt, vs = vr.next()
                for t in range(4):
                    pt, pts = pp.next()
                    for kc in range(8):
                        P.mm(pt[:], hT[:, kc, t * 128:(t + 1) * 128], wi[:, kc, 1024:1536], start=(kc == 0),
                             stop=(kc == 7), reads=[wi_w.s(1024), hT_s], writes=[pts])
                    P.cp(evac_eng(), vt[:, t, 0:512], pt[:], reads=[pts], writes=[vs])
                    pt, pts = pp.next()
                    for kc in range(8):
                        P.mm(pt[:, 0:128], hT[:, kc, t * 128:(t + 1) * 128], wi[:, kc, 2176:2304],
                             start=(kc == 0), stop=(kc == 7), reads=[wi_w.s(2176), hT_s], writes=[pts])
                    P.cp(evac_eng(), vt[:, t, 512:640], pt[:, 0:128], reads=[pts], writes=[vs])
                P.dma(SP, vAB[t0:t0 + 512, :].rearrange("(t p) d -> p t d", p=128), vt[:], vs, reads=[vs])
            end_phase(st)

            st = phase_scope()
            idb, idb_s = load_ident(st, "pa")
            cst = mk_mhalf(st, "pa")
            blr, blr_s = small(st, "pablr", c_blr, [128, 4 * 2 * 64])
            toep, toep_s = small(st, "patoep", c_toep, [128, 4 * 896])
            lamv, lamv_s = small(st, "palam", p_lam[l], [128, 256])
            lt = sb(st, "palt", [128, 8], F32)
            lt_s = slot(st, "palt")
            ljunk = sb(st, "paljunk", [128, 64], F32)
            ljunk_s = slot(st, "paljunk", waw=True)
            P.stt(ljunk[:], lamv[:, 0:64], 1.0, lamv[:, 64:128], ALU.mult, ALU.mult, accum=lt[:, 0:1],
                  reads=[lamv_s], writes=[ljunk_s, lt_s])
            P.stt(ljunk[:], lamv[:, 128:192], 1.0, lamv[:, 192:256], ALU.mult, ALU.mult, accum=lt[:, 1:2],
                  reads=[lamv_s], writes=[ljunk_s, lt_s])
            P.act(lt[:, 2:4], lt[:, 0:2], AF.Exp, reads=[lt_s], writes=[lt_s])
            P.tt(DVE, lt[:, 4:5], lt[:, 3:4], lt[:, 2:3], ALU.subtract, reads=[lt_s], writes=[lt_s])
            P.ts(DVE, lt[:, 5:6], lt[:, 4:5], -lam_init, None, ALU.add, reads=[lt_s], writes=[lt_s])
            neglam = lt[:, 5:6]

            ktr = Ring(P, "pakt", [sb(st, "pakt%d" % i, [67, 2, SMAX], BF16) for i in range(2)])
            var = Ring(P, "pava", [sb(st, "pava%d" % i, [128, NKMAX, 130], BF16) for i in range(2)])
            qvr = Ring(P, "paqv", [sb(st, "paqv%d" % i, [67, 3, 2, 512], BF16) for i in range(2)])
            qaug_s = [slot(st, "paqaug%d" % i) for i in range(2)]
            augst = sb(st, "paaugst", [67, 3, 2, 512], F32)
            augst_s = slot(st, "paaugst")
            etr = Ring(P, "paet", [sb(st, "paet%d" % i, [128, 1024], BF16) for i in range(3)])
            orr = Ring(P, "paor", [sb(st, "paor%d" % i, [128, 8, 129], F32) for i in range(2)])
            obr = Ring(P, "paob", [sb(st, "paob%d" % i, [128, 4, 128], F32) for i in range(2)])
            ybr = Ring(P, "payb", [sb(st, "payb%d" % i, [128, 4, 128], BF16) for i in range(2)])
            ysr = Ring(P, "pays", [sb(st, "pays%d" % i, [128, 512], BF16) for i in range(2)])
            for r in (ktr, var, qvr, etr, orr, obr, ybr, ysr):
                st.slots.extend(r.slots)
            pjunk = sb(st, "pajunk", [128, 128], F32)
            pjunk_s = slot(st, "pajunk", waw=True)
            smlr = Ring(P, "pasml", [sb(st, "pasml%d" % i, [128, 32], F32) for i in range(2)])
            st.slots.extend(smlr.slots)
            pscr = Ring(P, "papsc", [ps(st, "papsc%d" % i, [128, 1024], F32) for i in range(2)])
            paccT = [ps(st, "papacc%d" % i, [128, 512], F32) for i in range(3)]
            pacc_s = [slot(st, "papacc%d" % i) for i in range(3)]
            ptr = Ring(P, "paptr", [ps(st, "paptr%d" % i, [128, 512], BF16)[:] for i in range(1)])
            st.slots.extend(pscr.slots)
            st.slots.extend(ptr.slots)
            for i in range(2):
                P.memset(POOL, ktr.tiles[i][64:67, :, :], 1.0, writes=[ktr.slots[i]])
                P.memset(POOL, var.tiles[i][:, :, 128:130], 1.0, writes=[var.slots[i]])

            def acc_ap(a):
                return paccT[a // 3][:, (a % 3) * 129:(a % 3) * 129 + 129], pacc_s[a // 3]

            def pa_load_kv(si, h):
                S = seqs[si]
                t0 = seq_off[si]
                kt, kts = ktr.next()
                va, vas = var.next()
                for m in range(2):
                    P.dma(SP, kt[0:64, m, 0:S], qkA[8 + 2 * h + m, :, t0:t0 + S], kts, writes=[kts])
                nk = S // 128
                for j0 in range(0, nk, 8):
                    nj = min(8, nk - j0)
                    P.dma(dma_eng(), va[:, j0:j0 + nj, 0:128],
                          vAB[t0 + 128 * j0:t0 + 128 * (j0 + nj), 128 * h:128 * h + 128].rearrange(
                              "(j p) e -> p j e", p=128), vas, writes=[vas])
                return kt, kts, va, vas

            def keep(h, c, j):
                jj = j - 4 * c
                if jj < 0:
                    dmin = 128 * (-jj) - 127
                elif jj >= 4:
                    dmin = 128 * jj - 511
                else:
                    return True
                return SLOPES_A[h] * dmin <= SKIP_EXP

            work = [(si, h) for si in range(len(seqs)) for h in range(4)]
            W = []
            flat = []
            for wi_, (si, h) in enumerate(work):
                S = seqs[si]
                w = dict(si=si, h=h, S=S, t0=seq_off[si], nk=S // 128, nq=S // 512, kv=None, first={}, last={})
                for c in range(w["nq"]):
                    for j in range(w["nk"]):
                        if keep(h, c, j):
                            w["first"].setdefault(c, j)
                            w["last"][c] = j
                            flat.append((wi_, c, j))
                W.append(w)
            chunk_order = []
            for (wi_, c, j) in flat:
                if not chunk_order or chunk_order[-1] != (wi_, c):
                    chunk_order.append((wi_, c))
            chunk_pos = {k: i for i, k in enumerate(chunk_order)}
            cur_h = [None, None]
            qbufs = {}

            def load_q(wi_, c):
                w = W[wi_]
                h, t0 = w["h"], w["t0"]
                qv, qs = qvr.next()
                bi = qvr.i
                if cur_h[bi] != h:
                    P.dma(SP, augst[64:67, :, :, :], c_augq[h], augst_s, writes=[augst_s])
                    P.cp(DVE, qv[64:67, :, :, :], augst[64:67, :, :, :], reads=[augst_s],
                         writes=[qaug_s[bi], qs])
                    cur_h[bi] = h
                for v in range(3):
                    P.dma(SP, qv[0:64, v, :, :],
                          qkA[2 * h:2 * h + 2, :, t0 + 512 * c:t0 + 512 * c + 512].rearrange(
                              "m d t -> d m t"), qs, writes=[qs])
                return qv, qs

            def ensure_q(wi_, c, j):
                if (wi_, c) not in qbufs:
                    qbufs[(wi_, c)] = load_q(wi_, c)
                w = W[wi_]
                if j == min(w["first"][c] + 1, w["last"][c]):
                    p = chunk_pos[(wi_, c)] + 1
                    if p < len(chunk_order) and chunk_order[p] not in qbufs:
                        qbufs[chunk_order[p]] = load_q(*chunk_order[p])

            def ensure_kv(wi_):
                if W[wi_]["kv"] is None:
                    W[wi_]["kv"] = pa_load_kv(*work[wi_])

            def qk(wi_, c, j):
                w = W[wi_]
                h = w["h"]
                ensure_kv(wi_)
                kt, kts, va, vas = w["kv"]
                qv, qs = qbufs[(wi_, c)]
                pscT, pscS = pscr.next()
                jj = j - 4 * c
                if jj < 0:
                    kind, v, kk = "L", 0, 67
                elif jj >= 4:
                    kind, v, kk = "R", 1, 67
                else:
                    kind, v, kk = "D", 2, 67
                for m in range(2):
                    P.mm(pscT[:, m * 512:(m + 1) * 512], kt[0:kk, m, j * 128:(j + 1) * 128],
                         qv[0:kk, v, m, :], reads=[kts, qs], writes=[pscS])
                if kind == "D":
                    off = 384 - 128 * jj
                    for m in range(2):
                        P.tt(DVE, pscT[:, m * 512:(m + 1) * 512], pscT[:, m * 512:(m + 1) * 512],
                             toep[:, h * 896 + off:h * 896 + off + 512], ALU.add, reads=[pscS, toep_s],
                             writes=[pscS])
                    bias = None
                elif kind == "L":
                    bias = blr[:, (h * 2 + 0) * 64 + (-jj):(h * 2 + 0) * 64 + (-jj) + 1]
                else:
                    bias = blr[:, (h * 2 + 1) * 64 + jj:(h * 2 + 1) * 64 + jj + 1]
                return pscT, pscS, bias

            def post(wi_, c):
                orw, ors = orr.next()
                sml, sml_s = smlr.next()
                for b_ in range(3):
                    wd_ = 387 if b_ < 2 else 258
                    P.cp(DVE, orw[:, 3 * b_:3 * b_ + wd_ // 129, :], paccT[b_][:, 0:wd_].rearrange(
                        "p (a e) -> p a e", e=129), reads=[pacc_s[b_]], writes=[ors])
                yield
                P.recip(sml[:, 0:8], orw[:, :, 128], reads=[ors], writes=[sml_s])
                P.ts(DVE, sml[:, 8:12], sml[:, 4:8], neglam, None, ALU.mult, reads=[sml_s, lt_s],
                     writes=[sml_s])
                ob, obs = obr.next()
                for qs_ in range(4):
                    yield
                    P.ts(DVE, ob[:, qs_, :], orw[:, qs_, 0:128], sml[:, qs_:qs_ + 1], None, ALU.mult,
                         reads=[ors, sml_s], writes=[obs])
                    P.stt(ob[:, qs_, :], orw[:, 4 + qs_, 0:128], sml[:, 8 + qs_:9 + qs_], ob[:, qs_, :],
                          ALU.mult, ALU.add, reads=[ors, sml_s, obs], writes=[obs])
                    P.stt(pjunk[:], ob[:, qs_, :], 1.0, ob[:, qs_, :], ALU.mult, ALU.mult,
                          accum=sml[:, 12 + qs_:13 + qs_], reads=[obs], writes=[pjunk_s, sml_s])
                P.ts(POOL, sml[:, 16:20], sml[:, 12:16], 1.0 / 128, EPS, ALU.mult, ALU.add, reads=[sml_s],
                     writes=[sml_s])
                P.tt(POOL, sml[:, 16:20], sml[:, 16:20], cst["mhalf"][:, 0:4], ALU.pow,
                     reads=[sml_s, cst["mhalf_s"]], writes=[sml_s])
                yield
                yield
                yb, ybs = ybr.next()
                for qs_ in range(4):
                    if qs_ == 2:
                        yield
                    P.ts(DVE, yb[:, qs_, :], ob[:, qs_, :], sml[:, 16 + qs_:17 + qs_], None, ALU.mult,
                         reads=[obs, sml_s], writes=[ybs])
                yield
                post_b(yb, ybs, wi_, c)

            def post_b(yb, ybs, wi_, c):
                w = W[wi_]
                pt, pts = ptr.next()
                for qs_ in range(4):
                    P.tr(pt[:, qs_ * 128:(qs_ + 1) * 128], yb[:, qs_, :], idb[:], reads=[ybs, idb_s],
                         writes=[pts])
                ys, yss = ysr.next()
                P.cp(DVE, ys[:], pt, reads=[pts], writes=[yss])
                tt0 = w["t0"] + 512 * c
                P.dma(SP, yAT[w["h"], :, tt0:tt0 + 512], ys[:], yss, reads=[yss])

            ensure_kv(0)
            pend_post = None
            since_post = 0
            pendq = []
            for it in flat[0:2]:
                ensure_q(*it)
                pendq.append(qk(*it))
            for ii, (wi_, c, j) in enumerate(flat):
                w = W[wi_]
                kt, kts, va, vas = w["kv"]
                pscT, pscS, bias = pendq.pop(0)
                et, ets = etr.next()
                if bias is None:
                    P.act(et[:], pscT[:], AF.Exp, scale=0.125, reads=[pscS], writes=[ets])
                else:
                    P.act(et[:], pscT[:], AF.Exp, bias=bias, scale=0.125, reads=[pscS, blr_s], writes=[ets])
                if ii + 2 < len(flat):
                    ensure_q(*flat[ii + 2])
                    pendq.append(qk(*flat[ii + 2]))
                first, last_ = w["first"][c], w["last"][c]
                for a_ in range(8):
                    m, qs_ = a_ // 4, a_ % 4
                    ap_, as_ = acc_ap(a_)
                    P.mm(ap_, et[:, m * 512 + qs_ * 128:m * 512 + (qs_ + 1) * 128], va[:, j, 0:129],
                         start=(j == first and a_ % 3 == 0), stop=(j == last_), reads=[ets, vas],
                         writes=[as_], skip=True)
                if pend_post is not None:
                    if next(pend_post, "done") == "done":
                        pend_post = None
                if j == last_:
                    if pend_post is not None:
                        for _ in pend_post:
                            pass
                    pend_post = post(wi_, c)
                    next(pend_post)
                    qbufs.pop((wi_, c), None)
                if wi_ + 1 < len(work) and W[wi_ + 1]["kv"] is None and c == min(1, w["nq"] - 1) \
                        and j == min(first + 6, last_):
                    ensure_kv(wi_ + 1)
            if pend_post is not None:
                for _ in pend_post:
                    pass
            end_phase(st)

            st = phase_scope()
            idb, idb_s = load_ident(st, "pb")
            bb, bb_s = small(st, "pbbb", c_bb, [128, 3 * 8 * 128])
            skt, skt_s = small(st, "pbsink", p_sink[l], [128, 8])
            esink = sb(st, "pbesink", [128, 8], F32)
            esink_s = slot(st, "pbesink")
            P.act(esink[:], skt[:], AF.Exp, reads=[skt_s], writes=[esink_s])
            ktbs = [sb(st, "pbkt%d" % i, [128, 2, SMAX], BF16) for i in range(2)]
            ktb_ss = [slot(st, "pbkt%d" % i) for i in range(2)]
            ktz_s = slot(st, "pbktz")
            vbs = [sb(st, "pbvb%d" % i, [128, NKMAX, 2, 66], BF16) for i in range(2)]
            vb_ss = [slot(st, "pbvb%d" % i) for i in range(2)]
            vb1_s = slot(st, "pbvb1")
            for i in range(2):
                P.memset(POOL, ktbs[i][64:128, :, :], 0.0, writes=[ktz_s])
                P.memset(POOL, vbs[i][:, :, :, 64:66], 1.0, writes=[vb1_s])
            qbr = Ring(P, "pbqb", [sb(st, "pbqb%d" % i, [128, 4, 8, 128], BF16) for i in range(2)])
            qbz_s = slot(st, "pbqbz")
            for i in range(2):
                P.memset(POOL, qbr.tiles[i][64:128, :, :, :], 0.0, writes=[qbz_s])
            etbr = Ring(P, "pbet", [sb(st, "pbet%d" % i, [128, 512], BF16) for i in range(4)])
            ybtr = Ring(P, "pbyb", [sb(st, "pbyb%d" % i, [128, 512], BF16) for i in range(2)])
            ysbr = Ring(P, "pbys", [sb(st, "pbys%d" % i, [128, 4, 512], BF16) for i in range(2)])
            smbr = Ring(P, "pbsm", [sb(st, "pbsm%d" % i, [128, 16], F32) for i in range(2)])
            for r in (qbr, etbr, ybtr, ysbr, smbr):
                st.slots.extend(r.slots)
            psbr = Ring(P, "pbpsb", [ps(st, "pbpsb%d" % i, [128, 512], F32) for i in range(3)])
            paccB = [[ps(st, "pbpacc%d_%d" % (k, i), [128, 4, 65], F32) for i in range(2)] for k in range(2)]
            paccB_s = [[slot(st, "pbpacc%d_%d" % (k, i)) for i in range(2)] for k in range(2)]
            ptr = Ring(P, "pbptr", [ps(st, "pbptr0", [128, 512], BF16)[:]])
            st.slots.extend(psbr.slots)
            st.slots.extend(ptr.slots)

            def pb_post(k, ysb, ysbs, t, flush):
                smb, smb_s = smbr.next()
                for g in range(2):
                    P.tt(DVE, smb[:, 4 * g:4 * g + 4], paccB[k][g][:, :, 64], esink[:, 4 * g:4 * g + 4],
                         ALU.add, reads=[paccB_s[k][g], esink_s], writes=[smb_s])
                P.recip(smb[:, 8:16], smb[:, 0:8], reads=[smb_s], writes=[smb_s])
                ybt, ybts = ybtr.next()
                for g in range(2):
                    P.tt(DVE, ybt[:, g * 256:(g + 1) * 256].rearrange("p (h d) -> p h d", d=64),
                         paccB[k][g][:, :, 0:64],
                         smb[:, 8 + 4 * g:12 + 4 * g].unsqueeze(2).to_broadcast([128, 4, 64]), ALU.mult,
                         reads=[paccB_s[k][g], smb_s], writes=[ybts])
                pt, pts = ptr.next()
                for fc in range(4):
                    P.tr(pt[:, fc * 128:(fc + 1) * 128], ybt[:, fc * 128:(fc + 1) * 128], idb[:],
                         reads=[ybts, idb_s], writes=[pts])
                P.cp(ACT, ysb[:, :, t * 128:(t + 1) * 128], pt.rearrange("p (f q) -> p f q", q=128),
                     reads=[pts], writes=[ysbs])
                if flush is not None:
                    P.dma(SP, flush, ysb[:], ysbs, reads=[ysbs])

            units = []
            tile_no = 0
            for si, S in enumerate(seqs):
                nk = S // 128
                for c in range(S // 512):
                    for t in range(4):
                        jq = 4 * c + t
                        rels = [r for r in range(3) if 0 <= jq + r - 1 < nk]
                        for g in range(2):
                            for ri, r in enumerate(rels):
                                units.append(dict(si=si, c=c, t=t, g=g, ri=ri, r=r, jk=jq + r - 1,
                                                  nrel=len(rels), k=tile_no % 2,
                                                  last=(g == 1 and ri == len(rels) - 1)))
                        tile_no += 1
            state = dict(si=None, c=None, qb=None, qbs=None, ysb=None, ysbs=None)

            def pb_qk(u):
                si, c = u["si"], u["c"]
                S = seqs[si]
                t0 = seq_off[si]
                nk = S // 128
                ktb, ktb_s, vb, vb_s = ktbs[si % 2], ktb_ss[si % 2], vbs[si % 2], vb_ss[si % 2]
                if state["si"] != si:
                    for g in range(2):
                        P.dma(SP, ktb[0:64, g, 0:S], qkB[8 + g, :, t0:t0 + S], ktb_s, writes=[ktb_s])
                        for j0 in range(0, nk, 8):
                            nj = min(8, nk - j0)
                            P.dma(SP, vb[:, j0:j0 + nj, g, 0:64],
                                  vAB[t0 + 128 * j0:t0 + 128 * (j0 + nj), 512 + 64 * g:576 + 64 * g].rearrange(
                                      "(j p) e -> p j e", p=128), vb_s, writes=[vb_s])
                    state["si"] = si
                    state["c"] = None
                if state["c"] != c:
                    qb, qbs = qbr.next()
                    for t in range(4):
                        P.dma(SP, qb[0:64, t, :, :],
                              qkB[0:8, :, t0 + 512 * c + 128 * t:t0 + 512 * c + 128 * (t + 1)].rearrange(
                                  "h d q -> d h q"), qbs, writes=[qbs])
                    state["qb"], state["qbs"] = qb, qbs
                    state["c"] = c
                qb, qbs = state["qb"], state["qbs"]
                g, r, jk, t = u["g"], u["r"], u["jk"], u["t"]
                pt, pts = psbr.next()
                P.mm(pt[:], ktb[:, g, jk * 128:(jk + 1) * 128], qb[:, t, 4 * g:4 * g + 4, :],
                     reads=[ktb_s, ktz_s, qbs, qbz_s], writes=[pts])
                P.tt(DVE, pt[:], pt[:], bb[:, (r * 8 + 4 * g) * 128:(r * 8 + 4 * g + 4) * 128],
                     ALU.add, reads=[pts, bb_s], writes=[pts])
                return pt, pts

            pending = None
            pq = [pb_qk(u) for u in units[0:2]]
            cur_ysb = {}
            for i, u in enumerate(units):
                pt, pts = pq.pop(0)
                et, ets = etbr.next()
                P.act(et[:], pt[:], AF.Exp, scale=0.125, reads=[pts], writes=[ets])
                if i + 2 < len(units):
                    pq.append(pb_qk(units[i + 2]))
                k, g, ri, jk = u["k"], u["g"], u["ri"], u["jk"]
                vb, vb_s = vbs[u["si"] % 2], vb_ss[u["si"] % 2]
                for hh in range(4):
                    P.mm(paccB[k][g][:, hh, :], et[:, hh * 128:(hh + 1) * 128], vb[:, jk, g, 0:65],
                         start=(ri == 0 and hh == 0), stop=(ri == u["nrel"] - 1), reads=[ets, vb_s, vb1_s],
                         writes=[paccB_s[k][g]], skip=True)
                if u["last"]:
                    key = (u["si"], u["c"])
                    if key not in cur_ysb:
                        cur_ysb.clear()
                        cur_ysb[key] = ysbr.next()
                    ysb, ysbs = cur_ysb[key]
                    if pending is not None:
                        pb_post(*pending)
                    flush = None
                    if u["t"] == 3:
                        tt0 = seq_off[u["si"]] + 512 * u["c"]
                        flush = yBT[:, :, tt0:tt0 + 512].rearrange("f p t -> p f t")
                    pending = (k, ysb, ysbs, u["t"], flush)
            pb_post(*pending)
            end_phase(st)

            st = phase_scope()
            idb, idb_s = load_ident(st, "pc")
            cst = mk_mhalf(st, "pc")
            stage = mk_stage(st, "pc", n=3)
            subc, subc_s = small(st, "pcsub", p_sub[l], [128, 1])
            wpa = sb(st, "pcwpa", [128, 4, D], BF16)
            wpa_s = slot(st, "pcwpa")
            wpb = sb(st, "pcwpb", [128, 4, D], BF16)
            wpb_s = slot(st, "pcwpb")
            wo = sb(st, "pcwo", [128, 8, D], BF16)
            wo_s = slot(st, "pcwo")
            subc4 = sb(st, "pcsub4", [128, 4], F32)
            subc4_s = slot(st, "pcsub4")
            for i in range(4):
                P.cp(DVE, subc4[:, i:i + 1], subc[:], reads=[subc_s], writes=[subc4_s])
            load_weight(st, "pc", wpa, wpa_s, w_pa[l], 4, D, subc4, subc4_s, s2=(1.0 - lam_init), stage=stage)
            load_weight(st, "pc", wpb, wpb_s, w_pb[l], 4, D, stage=stage)
            load_weight(st, "pc", wo, wo_s, w_out[l], 8, D, stage=stage)
            fcol, fcol_s = small(st, "pcfcol", p_fn[l], [128, 8])
            yar = Ring(P, "pcya", [sb(st, "pcya%d" % i, [128, 4, 512], BF16) for i in range(2)])
            ybr2 = Ring(P, "pcyb", [sb(st, "pcyb%d" % i, [128, 4, 512], BF16) for i in range(2)])
            ggr = Ring(P, "pcgg", [sb(st, "pcgg%d" % i, [128, 16, 512], BF16) for i in range(2)])
            xr = Ring(P, "pcx", [sb(st, "pcx%d" % i, [128, 4, D], F32) for i in range(2)])
            xts_all = [[slot(st, "pcx%d_%d" % (i, t)) for t in range(4)] for i in range(2)]
            t1r = Ring(P, "pct1", [sb(st, "pct1%d" % i, [128, 512], F32) for i in range(2)])
            t2r = Ring(P, "pct2", [sb(st, "pct2%d" % i, [128, 512], F32) for i in range(2)])
            for r in (yar, ybr2, ggr, xr, t1r, t2r):
                st.slots.extend(r.slots)
            mT = sb(st, "pcmT", [128, 8, 512], BF16)
            mT_s = slot(st, "pcmT")
            h2b = sb(st, "pch2", [128, 4, D], BF16)
            h2b_s = slot(st, "pch2")
            h2Ts = sb(st, "pch2T", [128, 8, 512], BF16)
            h2Ts_s = slot(st, "pch2T")
            junk = sb(st, "pcjunk", [128, D], BF16)
            junk_s = slot(st, "pcjunk", waw=True)
            ssq = sb(st, "pcssq", [128, 8], F32)
            ssq_s = slot(st, "pcssq")
            rstd = sb(st, "pcrstd", [128, 8], F32)
            rstd_s = slot(st, "pcrstd")
            ptr = Ring(P, "pcptr", [ps(st, "pcptr%d" % i, [128, 512], BF16)[:] for i in range(2)])
            pp = Ring(P, "pcpp", [ps(st, "pcpp%d" % i, [128, 512], F32) for i in range(6)])
            st.slots.extend(ptr.slots)
            st.slots.extend(pp.slots)

            def pc_load(c):
                t0 = 512 * c
                ya, yas = yar.next()
                yb_, ybs_ = ybr2.next()
                gg, ggs = ggr.next()
                xt, xs = xr.next()
                xs = xts_all[xr.i]
                P.dma(SP, ya[:], yAT[:, :, t0:t0 + 512].rearrange("f p t -> p f t"), yas, writes=[yas])
                P.dma(SP, yb_[:], yBT[:, :, t0:t0 + 512].rearrange("f p t -> p f t"), ybs_, writes=[ybs_])
                P.dma(SP, gg[:], gT[:, :, t0:t0 + 512].rearrange("c p t -> p c t"), ggs, writes=[ggs])
                P.dma(SP, xt[:], xsrc[t0:t0 + 512, :].rearrange("(t p) d -> p t d", p=128), xs[0], writes=xs)
                return ya, yas, yb_, ybs_, gg, ggs, xt, xs

            pc_pending = [None]

            def pc_flush():
                if pc_pending[0] is None:
                    return
                tp0 = pc_pending[0]
                transposes(h2b, h2b_s, h2Ts, h2Ts_s, ptr, idb, idb_s)
                P.dma(SP, h2T[:, :, tp0:tp0 + 512].rearrange("f p t -> p f t"), h2Ts[:], h2Ts_s, reads=[h2Ts_s])
                pc_pending[0] = None

            nxt = pc_load(0)
            for c in range(NCH):
                t0 = 512 * c
                ya, yas, yb_, ybs_, gg, ggs, xt, xs = nxt
                if c + 1 < NCH:
                    nxt = pc_load(c + 1)
                for cc in range(8):
                    pa_, pas_ = pp.next()
                    for fc in range(4):
                        P.mm(pa_[:], wpa[:, fc, cc * 128:(cc + 1) * 128], ya[:, fc, :], start=(fc == 0),
                             stop=(fc == 3), reads=[wpa_s, yas], writes=[pas_])
                    pb_, pbs_ = pp.next()
                    for fc in range(4):
                        P.mm(pb_[:], wpb[:, fc, cc * 128:(cc + 1) * 128], yb_[:, fc, :], start=(fc == 0),
                             stop=(fc == 3), reads=[wpb_s, ybs_], writes=[pbs_])
                    t1, t1s = t1r.next()
                    t2, t2s = t2r.next()
                    P.tt(DVE, t1[:], pa_[:], gg[:, cc, :], ALU.mult, reads=[pas_, ggs], writes=[t1s])
                    P.tt(DVE, t2[:], pb_[:], gg[:, 8 + cc, :], ALU.mult, reads=[pbs_, ggs], writes=[t2s])
                    P.tt(POOL, mT[:, cc, :], t1[:], t2[:], ALU.add, reads=[t1s, t2s], writes=[mT_s])
                    if cc == 7:
                        pc_flush()
                for t in range(4):
                    for hh in range(2):
                        pt, pts = pp.next()
                        for fc in range(8):
                            P.mm(pt[:], mT[:, fc, t * 128:(t + 1) * 128], wo[:, fc, hh * 512:(hh + 1) * 512],
                                 start=(fc == 0), stop=(fc == 7), reads=[mT_s, wo_s], writes=[pts])
                        P.tt(DVE, xt[:, t, hh * 512:(hh + 1) * 512], pt[:], xt[:, t, hh * 512:(hh + 1) * 512],
                             ALU.add, reads=[pts, xs[t]], writes=[xs[t]])
                    P.dma(SP, xmid[t0 + 128 * t:t0 + 128 * (t + 1), :], xt[:, t, :], xs[t], reads=[xs[t]])
                    rms_stats(cst, [xt[:, t, :]], xs[t], ssq[:, t:t + 1], ssq_s, rstd[:, t:t + 1], rstd_s, junk,
                              junk_s, 1, D)
                    P.act(h2b[:, t, :], xt[:, t, :], AF.Copy, scale=rstd[:, t:t + 1], reads=[xs[t], rstd_s],
                          writes=[h2b_s])
                pc_pending[0] = t0
            pc_flush()
            end_phase(st)

            st = phase_scope()
            cst = mk_mhalf(st, "pf")
            stage = mk_stage(st, "pf", n=3)
            fcol, fcol_s = small(st, "pffcol", p_fn[l], [128, 8])
            cw, cw_s = small(st, "pfcw", p_cw[l], [128, 3 * NFC])
            cb, cb_s = small(st, "pfcb", p_cb[l], [128, NFC])
            wu = sb(st, "pfwu", [128, 8, 2 * DFF], BF16)
            _r = []
            for b_ in range(6):
                _r.append((512 * b_, min(512 * b_ + 512, DFF)))
                _r.append((DFF + 512 * b_, min(DFF + 512 * b_ + 512, 2 * DFF)))
            wu_w = WSlots(st, "pfwu", _r)
            wd = sb(st, "pfwd", [128, NFC, D], BF16)
            wd_s = slot(st, "pfwd")
            if last:
                gfin, gfin_s = small(st, "pfgfin", p_fin, [128, D])
            hx = sb(st, "pfhx", [128, 8, 514], BF16)
            hx_s = slot(st, "pfhx")
            xq = [sb(st, "pfx%d" % i, [128, D], F32) for i in range(4)]
            xq_s = [slot(st, "pfx%d" % i) for i in range(4)]
            uT = sb(st, "pfuT", [128, NFC, 512], BF16)
            uT_s = slot(st, "pfuT")
            cr = Ring(P, "pfc", [sb(st, "pfc%d" % i, [128, 512], F32) for i in range(2)])
            grr = Ring(P, "pfg", [sb(st, "pfg%d" % i, [128, 512], F32) for i in range(2)])
            st.slots.extend(cr.slots)
            st.slots.extend(grr.slots)
            junk = sb(st, "pfjunk", [128, D], BF16)
            junk_s = slot(st, "pfjunk", waw=True)
            ssq = sb(st, "pfssq", [128, 8], F32)
            ssq_s = slot(st, "pfssq")
            rstd = sb(st, "pfrstd", [128, 8], F32)
            rstd_s = slot(st, "pfrstd")
            pp = Ring(P, "pfpp", [ps(st, "pfpp%d" % i, [128, 512], F32) for i in range(7)])
            st.slots.extend(pp.slots)
            phal = ps(st, "pfhal", [128, 512], F32)
            phr = Ring(P, "pfhal", [phal[:, 0:2]])
            st.slots.extend(phr.slots)
            def load_hx(c):
                t0 = 512 * c
                si, cs = chunk_seq[c]
                first_c = (cs == 0)
                last_c = (cs == seqs[si] // 512 - 1)
                lo = 0 if first_c else 1
                hi = 0 if last_c else 1
                if first_c:
                    P.memset(POOL, hx[:, :, 0:1], 0.0, writes=[hx_s])
                if last_c:
                    P.memset(POOL, hx[:, :, 513:514], 0.0, writes=[hx_s])
                P.dma(SP, hx[:, :, 1 - lo:513 + hi], h2T[:, :, t0 - lo:t0 + 512 + hi].rearrange("f p t -> p f t"),
                      hx_s, writes=[hx_s])

            load_hx(0)
            load_weight(st, "pf", wu, wu_w, w_up[l], 8, 2 * DFF, fcol, fcol_s, stage=stage)
            load_weight(st, "pf", wd, wd_s, w_dn[l], NFC, D, stage=stage)
            for c in range(NCH):
                t0 = 512 * c
                for t in range(4):
                    P.dma(dma_eng(), xq[t][:], xmid[t0 + 128 * t:t0 + 128 * (t + 1), :], xq_s[t], writes=[xq_s[t]])
                for cc in range(NFC):
                    pa_, pas_ = pp.next()
                    ph_, phs_ = phr.next()
                    for kc in range(8):
                        P.mm(pa_[:], wu[:, kc, cc * 128:(cc + 1) * 128], hx[:, kc, 1:513], start=(kc == 0),
                             stop=(kc == 7), reads=[wu_w.s(cc * 128), hx_s], writes=[pas_])
                    pv_, pvs_ = pp.next()
                    for kc in range(8):
                        P.mm(pv_[:], wu[:, kc, DFF + cc * 128:DFF + (cc + 1) * 128], hx[:, kc, 1:513],
                             start=(kc == 0), stop=(kc == 7), reads=[wu_w.s(DFF + cc * 128), hx_s], writes=[pvs_])
                    for kc in range(8):
                        P.mm(ph_, wu[:, kc, cc * 128:(cc + 1) * 128], hx[:, kc, 0:514:513], start=(kc == 0),
                             stop=(kc == 7), reads=[wu_w.s(cc * 128), hx_s], writes=[phs_])
                    ct, cs_ = cr.next()
                    w0 = cw[:, 0 * NFC + cc:0 * NFC + cc + 1]
                    w1 = cw[:, 1 * NFC + cc:1 * NFC + cc + 1]
                    w2 = cw[:, 2 * NFC + cc:2 * NFC + cc + 1]
                    P.act(ct[:], pa_[:], AF.Identity, bias=cb[:, cc:cc + 1], scale=w1, reads=[pas_, cb_s, cw_s],
                          writes=[cs_])
                    P.stt(ct[:, 1:512], pa_[:, 0:511], w0, ct[:, 1:512], ALU.mult, ALU.add,
                          reads=[pas_, cw_s, cs_], writes=[cs_])
                    P.stt(ct[:, 0:511], pa_[:, 1:512], w2, ct[:, 0:511], ALU.mult, ALU.add,
                          reads=[pas_, cw_s, cs_], writes=[cs_])
                    P.stt(ct[:, 0:1], ph_[:, 0:1], w0, ct[:, 0:1], ALU.mult, ALU.add, reads=[phs_, cw_s, cs_],
                          writes=[cs_])
                    P.stt(ct[:, 511:512], ph_[:, 1:2], w2, ct[:, 511:512], ALU.mult, ALU.add,
                          reads=[phs_, cw_s, cs_], writes=[cs_])
                    gt_, gs_ = grr.next()
                    P.act(gt_[:], ct[:], AF.Gelu_apprx_tanh, reads=[cs_], writes=[gs_])
                    P.tt(DVE, uT[:, cc, :], pv_[:], gt_[:], ALU.mult, reads=[pvs_, gs_], writes=[uT_s])
                if c + 1 < NCH:
                    load_hx(c + 1)
                for t in range(4):
                    for hh in range(2):
                        pt, pts = pp.next()
                        for fc in range(NFC):
                            P.mm(pt[:], uT[:, fc, t * 128:(t + 1) * 128], wd[:, fc, hh * 512:(hh + 1) * 512],
                                 start=(fc == 0), stop=(fc == NFC - 1), reads=[uT_s, wd_s], writes=[pts])
                        P.tt(DVE, xq[t][:, hh * 512:(hh + 1) * 512], pt[:], xq[t][:, hh * 512:(hh + 1) * 512],
                             ALU.add, reads=[pts, xq_s[t]], writes=[xq_s[t]])
                    if not last:
                        P.dma(dma_eng(), xres[t0 + 128 * t:t0 + 128 * (t + 1), :], xq[t][:], xq_s[t],
                              reads=[xq_s[t]])
                    else:
                        rms_stats(cst, [xq[t][:]], xq_s[t], ssq[:, t:t + 1], ssq_s, rstd[:, t:t + 1], rstd_s,
                                  junk, junk_s, 1, D)
                        P.stt(xq[t][:], xq[t][:], rstd[:, t:t + 1], gfin[:], ALU.mult, ALU.mult,
                              reads=[xq_s[t], rstd_s, gfin_s], writes=[xq_s[t]])
                        P.dma(dma_eng(), yout[t0 + 128 * t:t0 + 128 * (t + 1), :], xq[t][:], xq_s[t],
                              reads=[xq_s[t]])
            end_phase(st)
    return nc


def _bf16_round(x):
    x = np.asarray(x, np.float32)
    u = x.view(np.uint32).astype(np.uint64)
    r = ((u + 0x7FFF + ((u >> 16) & 1)) & 0xFFFF0000).astype(np.uint32)
    return r.view(np.float32)


def make_consts():
    c = {}
    c["c_ident"] = np.eye(128, dtype=np.float32)
    qi = np.arange(512, dtype=np.float64)
    aug = np.zeros((4, 3, 3, 2, 512), np.float32)
    for h in range(4):
        for v, sgn in enumerate((-1.0, 1.0)):
            val = (sgn * 8.0 * SLOPES_A[h] * qi).astype(np.float32)
            hi = _bf16_round(val)
            mid = _bf16_round(val - hi)
            lo = _bf16_round(val - hi - mid)
            for r, part in enumerate((hi, mid, lo)):
                aug[h, r, v, :, :] = part[None, :]
    c["c_augq"] = aug
    ki = np.arange(128, dtype=np.float64)[:, None]
    m = np.arange(64, dtype=np.float64)[None, :]
    blr = np.zeros((128, 4, 2, 64), np.float32)
    for h in range(4):
        blr[:, h, 0, :] = SLOPES_A[h] * (ki - 128.0 * m)
        blr[:, h, 1, :] = -SLOPES_A[h] * (128.0 * m + ki)
    c["c_blr"] = blr.reshape(128, -1)
    xx = np.arange(896, dtype=np.float64)[None, :]
    toep = np.zeros((128, 4, 896), np.float32)
    for h in range(4):
        toep[:, h, :] = -8.0 * SLOPES_A[h] * np.abs(xx - 384.0 - ki)
    c["c_toep"] = toep.reshape(128, -1)
    qq = np.arange(128, dtype=np.float64)[None, :]
    bb = np.zeros((128, 3, 8, 128), np.float32)
    for r in range(3):
        dist = np.abs(128.0 * (r - 1) + ki - qq)
        for h in range(8):
            bb[:, r, h, :] = np.where(dist <= 128.0, -8.0 * SLOPES_B[h] * dist, 8.0 * NEGBIG)
    c["c_bb"] = bb.reshape(128, -1)
    return c


def layout_params(inp):
    f = lambda a: np.ascontiguousarray(np.asarray(a, np.float32))
    p = {}
    p["p_an"] = f(np.asarray(inp["attn_norm"]).reshape(DEPTH, 8, 128).transpose(0, 2, 1))
    p["p_fn"] = f(np.asarray(inp["ffn_norm"]).reshape(DEPTH, 8, 128).transpose(0, 2, 1))
    p["p_gb"] = f(np.asarray(inp["gate_bias"]).reshape(DEPTH, 16, 128).transpose(0, 2, 1))
    lam = np.concatenate([np.asarray(inp[k]) for k in ("lambda_q1", "lambda_k1", "lambda_q2", "lambda_k2")],
                         axis=1)
    p["p_lam"] = f(np.broadcast_to(lam[:, None, :], (DEPTH, 128, 256)))
    p["p_sub"] = f(np.asarray(inp["subln"]).reshape(DEPTH, 128, 1))
    p["p_sink"] = f(np.broadcast_to(np.asarray(inp["sink"])[:, None, :], (DEPTH, 128, 8)))
    p["p_cw"] = f(np.asarray(inp["conv_w"]).reshape(DEPTH, 3, NFC, 128).transpose(0, 3, 1, 2).reshape(
        DEPTH, 128, 3 * NFC))
    p["p_cb"] = f(np.asarray(inp["conv_b"]).reshape(DEPTH, NFC, 128).transpose(0, 2, 1))
    p["p_fin"] = f(np.broadcast_to(np.asarray(inp["final_norm"])[None, :], (128, D)))
    for k in ("w_in", "w_proj_a", "w_proj_b", "w_out", "w_up", "w_down"):
        p[k] = f(inp[k])
    return p


_NC_CACHE = {}


def run(core_x, inp, seqs, n_layers=DEPTH):
    key = (tuple(seqs), n_layers)
    if key not in _NC_CACHE:
        _NC_CACHE[key] = build(list(seqs), n_layers)
    nc = _NC_CACHE[key]
    shared = dict(make_consts())
    shared.update(layout_params(inp))
    in_maps = []
    for x in core_x:
        m = dict(shared)
        m["xin"] = np.ascontiguousarray(x, dtype=np.float32)
        in_maps.append(m)
    res = run_bass_kernel_spmd(nc, in_maps, core_ids=list(range(len(core_x))))
    return [r["yout"] for r in res.results]


def kernel(**inputs):
    xp = np.asarray(inputs["x_prompt"], np.float32)
    xs = np.asarray(inputs["x_sample"], np.float32)
    nb_p = xp.shape[0] // N_CORES
    nb_s = xs.shape[0] // N_CORES
    seqs = [xp.shape[1]] * nb_p + [xs.shape[1]] * nb_s
    core_x = []
    for i in range(N_CORES):
        parts = [xp[i * nb_p + b] for b in range(nb_p)] + [xs[i * nb_s + b] for b in range(nb_s)]
        core_x.append(np.concatenate(parts, axis=0))
    outs = run(core_x, inputs, seqs)
    yp = np.empty_like(xp)
    ys = np.empty_like(xs)
    for i in range(N_CORES):
        o = outs[i]
        off = 0
        for b in range(nb_p):
            yp[i * nb_p + b] = o[off:off + xp.shape[1]]
            off += xp.shape[1]
        for b in range(nb_s):
            ys[i * nb_s + b] = o[off:off + xs.shape[1]]
            off += xs.shape[1]
    return (yp, ys)
```

```python
import contextlib
import math
import numpy as np
import concourse.bass as bass
import concourse.mybir as mybir
from concourse.bass_utils import run_bass_kernel_spmd

F32 = mybir.dt.float32
BF16 = mybir.dt.bfloat16
AF = mybir.ActivationFunctionType
ALU = mybir.AluOpType

PE, ACT, DVE, POOL, SP = "tensor", "scalar", "vector", "gpsimd", "sync"
ENGS = [PE, ACT, DVE, POOL, SP]

D = 1024
DEPTH = 2
IN_COLS = 4352
DFF = 2816
NFC = 22
EPS = 1e-6
N_CORES = 8
STOP_AFTER = 10 ** 9
SLOPES_A = [2.0 ** (-8.0 * (h + 1) / 4) for h in range(4)]
SLOPES_B = [2.0 ** (-8.0 * (h + 1) / 8) for h in range(8)]
NEGBIG = -30000.0
SKIP_EXP = 88.0 + 92.3


class Slot:
    __slots__ = ("name", "writers", "readers", "prev_readers", "dsem", "waw")

    def __init__(self, name, waw=False):
        self.name = name
        self.waw = waw
        self.writers = []
        self.readers = []
        self.prev_readers = []
        self.dsem = None


class Op:
    __slots__ = ("eng", "fn", "deps", "signal", "count", "dma", "dtok")

    def __init__(self, eng, fn):
        self.eng = eng
        self.fn = fn
        self.deps = []
        self.signal = False
        self.count = None
        self.dma = False
        self.dtok = None


class Prog:
    def __init__(self, nc, stack, n_dma_sems=88):
        self.nc = nc
        self.ops = {e: [] for e in ENGS}
        self.esem = {e: stack.enter_context(nc.semaphore("S_" + e)) for e in ENGS}
        self.dsem = [stack.enter_context(nc.semaphore("D%d" % i)) for i in range(n_dma_sems)]
        self.dcount = [0] * n_dma_sems
        self.next_dsem = 0
        self.free_dsems = []
        self.ecount = {e: 0 for e in ENGS}
        self.waited_e = {e: {x: 0 for x in ENGS} for e in ENGS}
        self.waited_d = {e: [0] * n_dma_sems for e in ENGS}
        self.slots = []
        self.last_sig = {e: None for e in ENGS}

    def slot(self, name, waw=False):
        s = Slot(name, waw)
        self.slots.append(s)
        return s

    def release_slots(self, slots):
        for s in slots:
            if s.dsem is not None:
                self.free_dsems.append(s.dsem)
                s.dsem = None
        ids = set(id(s) for s in slots)
        self.slots = [s for s in self.slots if id(s) not in ids]

    def _mkdeps(self, op, reads, writes):
        deps = []
        for s in reads:
            deps.extend(s.writers)
        for s in writes:
            if s.readers:
                s.prev_readers = s.readers
                deps.extend(s.writers)
                s.readers = []
                s.writers = []
            deps.extend(s.prev_readers)
            if s.waw:
                deps.extend(s.writers)
        for s in reads:
            s.readers.append(op)
        for s in writes:
            s.writers.append(op)
        seen = set()
        out = []
        for d in deps:
            if d is op or id(d) in seen:
                continue
            seen.add(id(d))
            out.append(d)
        return out

    def op(self, eng, fn, reads=(), writes=(), extra=()):
        o = Op(eng, fn)
        o.deps = self._mkdeps(o, reads, writes) + list(extra)
        for d in o.deps:
            if not d.dma:
                d.signal = True
        self.ops[eng].append(o)
        return o

    def dma(self, eng, out, in_, sb, reads=(), writes=(), extra=(), **kw):
        o = Op(eng, lambda e: e.dma_start(out=out, in_=in_, **kw))
        o.dma = True
        o.deps = self._mkdeps(o, reads, writes) + list(extra)
        for d in o.deps:
            if not d.dma:
                d.signal = True
        if sb.dsem is None:
            if self.free_dsems:
                sb.dsem = self.free_dsems.pop()
            else:
                sb.dsem = self.next_dsem
                self.next_dsem += 1
                assert self.next_dsem <= len(self.dsem), "out of DMA semaphores"
        self.dcount[sb.dsem] += 16
        o.dtok = (sb.dsem, self.dcount[sb.dsem])
        self.ops[eng].append(o)
        return o

    def mm(self, out, lhsT, rhs, start=True, stop=True, reads=(), writes=(), skip=False):
        return self.op(PE, lambda e: e.matmul(out, lhsT=lhsT, rhs=rhs, start=start, stop=stop,
                                              skip_group_check=skip), reads, writes)

    def tr(self, out, in_, ident, reads=(), writes=()):
        return self.op(PE, lambda e: e.transpose(out=out, in_=in_, identity=ident), reads, writes)

    def act(self, out, in_, func, bias=None, scale=None, accum=None, reads=(), writes=()):
        kw = {}
        if bias is not None:
            kw["bias"] = bias
        if scale is not None:
            kw["scale"] = scale
        if accum is not None:
            kw["accum_out"] = accum
        return self.op(ACT, lambda e: e.activation(out=out, in_=in_, func=func, **kw), reads, writes)

    def ts(self, eng, out, in0, s1, s2, op0, op1=None, reads=(), writes=()):
        if op1 is None:
            return self.op(eng, lambda e: e.tensor_scalar(out=out, in0=in0, scalar1=s1, scalar2=None,
                                                          op0=op0), reads, writes)
        return self.op(eng, lambda e: e.tensor_scalar(out=out, in0=in0, scalar1=s1, scalar2=s2,
                                                      op0=op0, op1=op1), reads, writes)

    def tt(self, eng, out, in0, in1, op, reads=(), writes=()):
        return self.op(eng, lambda e: e.tensor_tensor(out=out, in0=in0, in1=in1, op=op), reads, writes)

    def stt(self, out, in0, scalar, in1, op0, op1, accum=None, reads=(), writes=()):
        if accum is None:
            return self.op(DVE, lambda e: e.scalar_tensor_tensor(out=out, in0=in0, scalar=scalar, in1=in1,
                                                                 op0=op0, op1=op1), reads, writes)
        return self.op(DVE, lambda e: e.scalar_tensor_tensor(out=out, in0=in0, scalar=scalar, in1=in1,
                                                             op0=op0, op1=op1, accum_out=accum),
                       reads, writes)

    def cp(self, eng, out, in_, reads=(), writes=()):
        if eng == ACT:
            return self.op(ACT, lambda e: e.copy(out=out, in_=in_), reads, writes)
        return self.op(eng, lambda e: e.tensor_copy(out=out, in_=in_), reads, writes)

    def memset(self, eng, ap, val, writes=()):
        return self.op(eng, lambda e: e.memset(ap, val), (), writes)

    def recip(self, out, in_, reads=(), writes=()):
        return self.op(DVE, lambda e: e.reciprocal(out=out, in_=in_), reads, writes)

    def barrier(self):
        lasts = []
        for e in ENGS:
            for o in reversed(self.ops[e]):
                if not o.dma:
                    lasts.append(o)
                    break
            else:
                if self.last_sig[e] is not None:
                    lasts.append(self.last_sig[e])
        dtoks = [(i, c) for i, c in enumerate(self.dcount) if c > 0]
        for e in ENGS:
            o = Op(e, ("barrier", dtoks))
            o.deps = [l for l in lasts if l.eng != e]
            for d in o.deps:
                d.signal = True
            self.ops[e].append(o)
        for s in self.slots:
            s.writers = []
            s.readers = []
            s.prev_readers = []

    def emit(self):
        nc = self.nc
        for e in ENGS:
            for o in self.ops[e]:
                if o.signal and not o.dma and o.count is None:
                    self.ecount[e] += 1
                    o.count = self.ecount[e]
                    self.last_sig[e] = o
        esem, dsem = self.esem, self.dsem

        def run(en, e):
            we = self.waited_e[en]
            wd = self.waited_d[en]
            for o in self.ops[en]:
                dmax = {}
                for d in o.deps:
                    if d.dma:
                        si, v = d.dtok
                        if dmax.get(si, 0) < v:
                            dmax[si] = v
                for si, v in dmax.items():
                    if wd[si] < v:
                        e.wait_ge(dsem[si], v)
                        wd[si] = v
                for d in o.deps:
                    if d.dma:
                        continue
                    else:
                        if d.eng == en and en == PE:
                            continue
                        if we[d.eng] < d.count:
                            e.wait_ge(esem[d.eng], d.count)
                            we[d.eng] = d.count
                if isinstance(o.fn, tuple):
                    for si, v in o.fn[1]:
                        if wd[si] < v:
                            e.wait_ge(dsem[si], v)
                            wd[si] = v
                    if o.signal:
                        e.nop().then_inc(esem[en], 1)
                    continue
                ins = o.fn(e)
                if o.dma:
                    ins.then_inc(dsem[o.dtok[0]], 16)
                elif o.signal:
                    ins.then_inc(esem[en], 1)

        with nc.Block() as block:
            @block.tensor
            def _(e):
                run(PE, e)

            @block.scalar
            def _(e):
                run(ACT, e)

            @block.vector
            def _(e):
                run(DVE, e)

            @block.gpsimd
            def _(e):
                run(POOL, e)

            @block.sync
            def _(e):
                run(SP, e)
        self.ops = {e: [] for e in ENGS}


class Ring:
    def __init__(self, P, name, tiles):
        self.tiles = tiles
        self.slots = [P.slot("%s%d" % (name, i)) for i in range(len(tiles))]
        self.i = -1

    def next(self):
        self.i = (self.i + 1) % len(self.tiles)
        return self.tiles[self.i], self.slots[self.i]

    def cur(self):
        return self.tiles[self.i], self.slots[self.i]


def build(seqs, n_layers=DEPTH):
    T = sum(seqs)
    SMAX = max(seqs)
    NKMAX = SMAX // 128
    nc = bass.Bass("TRN2", target_bir_lowering=False)

    def din(name, shape, dt=F32):
        return nc.dram_tensor(name, list(shape), dt, kind="ExternalInput").ap()

    def dscr(name, shape, dt):
        return nc.dram_tensor(name, list(shape), dt, kind="Internal").ap()

    xin = din("xin", [T, D])
    w_in = din("w_in", [DEPTH, D, IN_COLS])
    w_pa = din("w_proj_a", [DEPTH, 512, D])
    w_pb = din("w_proj_b", [DEPTH, 512, D])
    w_out = din("w_out", [DEPTH, D, D])
    w_up = din("w_up", [DEPTH, D, 2 * DFF])
    w_dn = din("w_down", [DEPTH, DFF, D])
    p_an = din("p_an", [DEPTH, 128, 8])
    p_fn = din("p_fn", [DEPTH, 128, 8])
    p_gb = din("p_gb", [DEPTH, 128, 16])
    p_lam = din("p_lam", [DEPTH, 128, 256])
    p_sub = din("p_sub", [DEPTH, 128, 1])
    p_sink = din("p_sink", [DEPTH, 128, 8])
    p_cw = din("p_cw", [DEPTH, 128, 3 * NFC])
    p_cb = din("p_cb", [DEPTH, 128, NFC])
    p_fin = din("p_fin", [128, D])
    c_ident = din("c_ident", [128, 128])
    c_augq = din("c_augq", [4, 3, 3, 2, 512])
    c_blr = din("c_blr", [128, 4 * 2 * 64])
    c_toep = din("c_toep", [128, 4 * 896])
    c_bb = din("c_bb", [128, 3 * 8 * 128])
    yout = nc.dram_tensor("yout", [T, D], F32, kind="ExternalOutput").ap()

    qkA = dscr("s_qkA", [16, 64, T], BF16)
    qkB = dscr("s_qkB", [10, 64, T], BF16)
    vAB = dscr("s_vAB", [T, 640], BF16)
    gT = dscr("s_gT", [16, 128, T], BF16)
    yAT = dscr("s_yAT", [4, 128, T], BF16)
    yBT = dscr("s_yBT", [4, 128, T], BF16)
    xmid = dscr("s_xmid", [T, D], F32)
    h2T = dscr("s_h2T", [8, 128, T], BF16)
    xres = dscr("s_xres", [T, D], F32)
    hbnd = dscr("s_hbnd", [8, 128, 2 * (T // 512)], BF16)

    seq_off = [sum(seqs[:i]) for i in range(len(seqs))]
    NCH = T // 512
    chunk_seq = []
    for si, S in enumerate(seqs):
        for c in range(S // 512):
            chunk_seq.append((si, c))

    with contextlib.ExitStack() as top:
        P = Prog(nc, top)

        def phase_scope():
            st = contextlib.ExitStack()
            st.slots = []
            return st

        uniq = [0]

        def sb(st, name, shape, dt):
            uniq[0] += 1
            return st.enter_context(nc.sbuf_tensor("%s_%d" % (name, uniq[0]), list(shape), dt))

        def ps(st, name, shape, dt):
            uniq[0] += 1
            return st.enter_context(nc.psum_tensor("%s_%d" % (name, uniq[0]), list(shape), dt))

        def slot(st, name, waw=False):
            s = P.slot(name, waw)
            st.slots.append(s)
            return s

        nphase = [0]

        class StopBuild(Exception):
            pass

        def end_phase(st):
            P.barrier()
            P.emit()
            P.release_slots(st.slots)
            st.close()
            nphase[0] += 1
            if nphase[0] >= STOP_AFTER:
                raise StopBuild()

        dmaq = [SP, POOL]
        dq = [0]

        def dma_eng():
            return SP

        def load_ident(st, pfx):
            idf = sb(st, pfx + "idf", [128, 128], F32)
            idb = sb(st, pfx + "idb", [128, 128], BF16)
            s1, s2 = slot(st, pfx + "idf"), slot(st, pfx + "idb")
            P.dma(SP, idf[:], c_ident[:, :], s1, writes=[s1])
            P.cp(DVE, idb[:], idf[:], reads=[s1], writes=[s2])
            return idb, s2

        cast_rr = [0]

        class WSlots:
            def __init__(self, st, name, ranges):
                self.ranges = [(a_, b_, slot(st, "%s_%d" % (name, a_))) for (a_, b_) in ranges]

            def s(self, col):
                for a_, b_, sl in self.ranges:
                    if a_ <= col < b_:
                        return sl
                raise KeyError(col)

        def load_weight(st, pfx, dst, dst_slot, src, n_k, n_cols, scol=None, scol_slot=None, s2=None,
                        stage=None):
            stg_ring = stage
            cb = stage.tiles[0].shape[-1]
            if isinstance(dst_slot, WSlots):
                wsl = dst_slot
                order = [(kc, a_, b_ - a_, sl) for (a_, b_, sl) in wsl.ranges for kc in range(n_k)]
            else:
                order = [(kc, c0, min(cb, n_cols - c0), dst_slot) for kc in range(n_k)
                         for c0 in range(0, n_cols, cb)]
            if True:
                for (kc, c0, cw, dst_slot) in order:
                    stg, sslot = stg_ring.next()
                    P.dma(dma_eng(), stg[:, 0:cw], src[kc * 128:(kc + 1) * 128, c0:c0 + cw], sslot,
                          writes=[sslot])
                    o = dst[:, kc, c0:c0 + cw]
                    if scol is None:
                        P.cp(POOL, o, stg[:, 0:cw], reads=[sslot], writes=[dst_slot])
                    else:
                        sc = scol[:, kc:kc + 1]
                        P.ts(POOL, o, stg[:, 0:cw], sc, float(1.0 if s2 is None else s2), ALU.mult, ALU.mult,
                             reads=[sslot, scol_slot], writes=[dst_slot])

        def mk_stage(st, pfx, n=2, width=1024):
            tiles = [sb(st, "%sstg%d" % (pfx, i), [128, width], F32) for i in range(n)]
            r = Ring(P, pfx + "stg", tiles)
            st.slots.extend(r.slots)
            return r

        def small(st, name, src, shape):
            t = sb(st, name, shape, F32)
            s = slot(st, name)
            P.dma(SP, t[:], src, s, writes=[s])
            return t, s

        def rms_stats(st_tiles, x_ap_list, x_slot, ssq, ssq_slot, rstd, rstd_slot, junk, junk_slot, n, dim):
            for i in range(n):
                P.act(junk[:, 0:dim], x_ap_list[i], AF.Square, accum=ssq[:, i:i + 1], reads=[x_slot],
                      writes=[junk_slot, ssq_slot])
            P.ts(POOL, rstd[:, 0:n], ssq[:, 0:n], 1.0 / dim, EPS, ALU.mult, ALU.add, reads=[ssq_slot],
                 writes=[rstd_slot])
            P.tt(POOL, rstd[:, 0:n], rstd[:, 0:n], st_tiles["mhalf"][:, 0:n], ALU.pow,
                 reads=[rstd_slot, st_tiles["mhalf_s"]], writes=[rstd_slot])

        def mk_mhalf(st, pfx):
            t = sb(st, pfx + "mhalf", [128, 8], F32)
            s = slot(st, pfx + "mhalf")
            P.memset(POOL, t[:], -0.5, writes=[s])
            return {"mhalf": t, "mhalf_s": s}

        top.push(lambda et, ev, tb: et is StopBuild)
        for l in range(n_layers):
            lam_init = 0.8 - 0.6 * math.exp(-0.3 * l)
            xsrc = xin if l == 0 else xres
            last = (l == n_layers - 1)

            st = phase_scope()
            idb, idb_s = load_ident(st, "p1")
            cst = mk_mhalf(st, "p1")
            wi = sb(st, "p1wi", [128, 8, IN_COLS], BF16)
            wi_w = WSlots(st, "p1wi", [(0, 512), (512, 1024), (1536, 2176), (2304, 2816), (2816, 3328),
                                       (3328, 3840), (3840, 4352), (1024, 1536), (2176, 2304)])
            stage = mk_stage(st, "p1", n=3)
            gcol, gcol_s = small(st, "p1gcol", p_an[l], [128, 8])
            gbias, gbias_s = small(st, "p1gbias", p_gb[l], [128, 16])

            xr = Ring(P, "p1x", [sb(st, "p1x%d" % i, [128, 4, D], F32) for i in range(2)])
            st.slots.extend(xr.slots)
            hb = sb(st, "p1h", [128, 4, D], BF16)
            hb_s = slot(st, "p1h")
            hT = sb(st, "p1hT", [128, 8, 512], BF16)
            hT_s = slot(st, "p1hT")
            stA = sb(st, "p1stA", [128, 8, 512], BF16)
            stA_s = slot(st, "p1stA")
            stB = sb(st, "p1stB", [128, 5, 512], BF16)
            stB_s = slot(st, "p1stB")
            gr = Ring(P, "p1g", [sb(st, "p1g%d" % i, [128, 8, 512], BF16) for i in range(2)])
            st.slots.extend(gr.slots)
            vr = Ring(P, "p1v", [sb(st, "p1v%d" % i, [128, 4, 640], BF16) for i in range(2)])
            st.slots.extend(vr.slots)
            junk = sb(st, "p1junk", [128, D], BF16)
            junk_s = slot(st, "p1junk", waw=True)
            ssq = sb(st, "p1ssq", [128, 8], F32)
            ssq_s = slot(st, "p1ssq")
            rstd = sb(st, "p1rstd", [128, 8], F32)
            rstd_s = slot(st, "p1rstd")
            ptr = Ring(P, "p1ptr", [ps(st, "p1ptr%d" % i, [128, 512], BF16)[:] for i in range(2)])
            st.slots.extend(ptr.slots)
            pp = Ring(P, "p1pp", [ps(st, "p1pp%d" % i, [128, 512], F32) for i in range(6)])
            st.slots.extend(pp.slots)
            evac_rr = [0]

            def evac_eng():
                evac_rr[0] += 1
                return DVE if evac_rr[0] % 3 else ACT

            def p1_load(c):
                xt, xs = xr.next()
                t0 = 512 * c
                P.dma(SP, xt[:], xsrc[t0:t0 + 512, :].rearrange("(t p) d -> p t d", p=128), xs, writes=[xs])
                return xt, xs

            def p1_norm(xt, xs):
                rms_stats(cst, [xt[:, t, :] for t in range(4)], xs, ssq, ssq_s, rstd, rstd_s, junk, junk_s,
                          4, D)
                for t in range(4):
                    P.ts(DVE, hb[:, t, :], xt[:, t, :], rstd[:, t:t + 1], None, ALU.mult,
                         reads=[xs, rstd_s], writes=[hb_s])

            def transposes(src, src_s, dstT, dstT_s, ptr_ring, idb, idb_s, nfc=8):
                for fc in range(nfc):
                    pt, pts = ptr_ring.next()
                    for t in range(4):
                        P.tr(pt[:, t * 128:(t + 1) * 128], src[:, t, fc * 128:(fc + 1) * 128], idb[:],
                             reads=[src_s, idb_s], writes=[pts])
                    P.cp(evac_eng(), dstT[:, fc, :], pt, reads=[pts], writes=[dstT_s])

            nxt = p1_load(0)
            p1_norm(*nxt)
            load_weight(st, "p1", wi, wi_w, w_in[l], 8, IN_COLS, gcol, gcol_s, stage=stage)
            for c in range(NCH):
                t0 = 512 * c
                transposes(hb, hb_s, hT, hT_s, ptr, idb, idb_s)
                if c + 1 < NCH:
                    nxt = p1_load(c + 1)
                    p1_norm(*nxt)
                for i in range(8):
                    pt, pts = pp.next()
                    for kc in range(8):
                        P.mm(pt[:], wi[:, kc, i * 128:(i + 1) * 128], hT[:, kc, :], start=(kc == 0),
                             stop=(kc == 7), reads=[wi_w.s(i * 128), hT_s], writes=[pts])
                    P.cp(evac_eng(), stA[:, i, :], pt[:], reads=[pts], writes=[stA_s])
                P.dma(SP, qkA[:, :, t0:t0 + 512].rearrange("(i two) d t -> (two d) i t", two=2), stA[:], stA_s,
                      reads=[stA_s])
                for i in range(5):
                    pt, pts = pp.next()
                    c0 = 1536 + i * 128
                    for kc in range(8):
                        P.mm(pt[:], wi[:, kc, c0:c0 + 128], hT[:, kc, :], start=(kc == 0),
                             stop=(kc == 7), reads=[wi_w.s(c0), hT_s], writes=[pts])
                    P.cp(evac_eng(), stB[:, i, :], pt[:], reads=[pts], writes=[stB_s])
                P.dma(SP, qkB[:, :, t0:t0 + 512].rearrange("(i two) d t -> (two d) i t", two=2), stB[:], stB_s,
                      reads=[stB_s])
                for half in range(2):
                    gt, gs = gr.next()
                    for i in range(8):
                        gi = half * 8 + i
                        c0 = 2304 + gi * 128
                        pt, pts = pp.next()
                        for kc in range(8):
                            P.mm(pt[:], wi[:, kc, c0:c0 + 128], hT[:, kc, :], start=(kc == 0), stop=(kc == 7),
                                 reads=[wi_w.s(c0), hT_s], writes=[pts])
                        P.act(gt[:, i, :], pt[:], AF.Sigmoid, bias=gbias[:, gi:gi + 1], reads=[pts, gbias_s],
                              writes=[gs])
                    P.dma(SP, gT[half * 8:half * 8 + 8, :, t0:t0 + 512].rearrange("c p t -> p c t"), gt[:], gs,
                          reads=[gs])
                vt, vs = vr.next()
                for t in range(4):
                    pt, pts = pp.next()
                    for kc in range(8):
                        P.mm(pt[:], hT[:, kc, t * 128:(t + 1) * 128], wi[:, kc, 1024:1536], start=(kc == 0),
                             stop=(kc == 7), reads=[wi_w.s(1024), hT_s], writes=[pts])
                    P.cp(evac_eng(), vt[:, t, 0:512], pt[:], reads=[pts], writes=[vs])
                    pt, pts = pp.next()
                    for kc in range(8):
                        P.mm(pt[:, 0:128], hT[:, kc, t * 128:(t + 1) * 128], wi[:, kc, 2176:2304],
                             start=(kc == 0), stop=(kc == 7), reads=[wi_w.s(2176), hT_s], writes=[pts])
                    P.cp(evac_eng(), vt[:, t, 512:640], pt[:, 0:128], reads=[pts], writes=[vs])
                P.dma(SP, vAB[t0:t0 + 512, :].rearrange("(t p) d -> p t d", p=128), vt[:], vs, reads=[vs])
            end_phase(st)

            st = phase_scope()
            idb, idb_s = load_ident(st, "pa")
            cst = mk_mhalf(st, "pa")
            blr, blr_s = small(st, "pablr", c_blr, [128, 4 * 2 * 64])
            toep, toep_s = small(st, "patoep", c_toep, [128, 4 * 896])
            lamv, lamv_s = small(st, "palam", p_lam[l], [128, 256])
            lt = sb(st, "palt", [128, 8], F32)
            lt_s = slot(st, "palt")
            ljunk = sb(st, "paljunk", [128, 64], F32)
            ljunk_s = slot(st, "paljunk", waw=True)
            P.stt(ljunk[:], lamv[:, 0:64], 1.0, lamv[:, 64:128], ALU.mult, ALU.mult, accum=lt[:, 0:1],
                  reads=[lamv_s], writes=[ljunk_s, lt_s])
            P.stt(ljunk[:], lamv[:, 128:192], 1.0, lamv[:, 192:256], ALU.mult, ALU.mult, accum=lt[:, 1:2],
                  reads=[lamv_s], writes=[ljunk_s, lt_s])
            P.act(lt[:, 2:4], lt[:, 0:2], AF.Exp, reads=[lt_s], writes=[lt_s])
            P.tt(DVE, lt[:, 4:5], lt[:, 3:4], lt[:, 2:3], ALU.subtract, reads=[lt_s], writes=[lt_s])
            P.ts(DVE, lt[:, 5:6], lt[:, 4:5], -lam_init, None, ALU.add, reads=[lt_s], writes=[lt_s])
            neglam = lt[:, 5:6]

            ktr = Ring(P, "pakt", [sb(st, "pakt%d" % i, [67, 2, SMAX], BF16) for i in range(2)])
            var = Ring(P, "pava", [sb(st, "pava%d" % i, [128, NKMAX, 130], BF16) for i in range(2)])
            qvr = Ring(P, "paqv", [sb(st, "paqv%d" % i, [67, 3, 2, 512], BF16) for i in range(2)])
            qaug_s = [slot(st, "paqaug%d" % i) for i in range(2)]
            augst = sb(st, "paaugst", [67, 3, 2, 512], F32)
            augst_s = slot(st, "paaugst")
            etr = Ring(P, "paet", [sb(st, "paet%d" % i, [128, 1024], BF16) for i in range(3)])
            orr = Ring(P, "paor", [sb(st, "paor%d" % i, [128, 8, 129], F32) for i in range(2)])
            obr = Ring(P, "paob", [sb(st, "paob%d" % i, [128, 4, 128], F32) for i in range(2)])
            ybr = Ring(P, "payb", [sb(st, "payb%d" % i, [128, 4, 128], BF16) for i in range(2)])
            ysr = Ring(P, "pays", [sb(st, "pays%d" % i, [128, 512], BF16) for i in range(2)])
            for r in (ktr, var, qvr, etr, orr, obr, ybr, ysr):
                st.slots.extend(r.slots)
            pjunk = sb(st, "pajunk", [128, 128], F32)
            pjunk_s = slot(st, "pajunk", waw=True)
            smlr = Ring(P, "pasml", [sb(st, "pasml%d" % i, [128, 32], F32) for i in range(2)])
            st.slots.extend(smlr.slots)
            pscr = Ring(P, "papsc", [ps(st, "papsc%d" % i, [128, 1024], F32) for i in range(2)])
            paccT = [ps(st, "papacc%d" % i, [128, 512], F32) for i in range(3)]
            pacc_s = [slot(st, "papacc%d" % i) for i in range(3)]
            ptr = Ring(P, "paptr", [ps(st, "paptr%d" % i, [128, 512], BF16)[:] for i in range(1)])
            st.slots.extend(pscr.slots)
            st.slots.extend(ptr.slots)
            for i in range(2):
                P.memset(POOL, ktr.tiles[i][64:67, :, :], 1.0, writes=[ktr.slots[i]])
                P.memset(POOL, var.tiles[i][:, :, 128:130], 1.0, writes=[var.slots[i]])

            def acc_ap(a):
                return paccT[a // 3][:, (a % 3) * 129:(a % 3) * 129 + 129], pacc_s[a // 3]

            def pa_load_kv(si, h):
                S = seqs[si]
                t0 = seq_off[si]
                kt, kts = ktr.next()
                va, vas = var.next()
                for m in range(2):
                    P.dma(SP, kt[0:64, m, 0:S], qkA[8 + 2 * h + m, :, t0:t0 + S], kts, writes=[kts])
                nk = S // 128
                for j0 in range(0, nk, 8):
                    nj = min(8, nk - j0)
                    P.dma(dma_eng(), va[:, j0:j0 + nj, 0:128],
                          vAB[t0 + 128 * j0:t0 + 128 * (j0 + nj), 128 * h:128 * h + 128].rearrange(
                              "(j p) e -> p j e", p=128), vas, writes=[vas])
                return kt, kts, va, vas

            def keep(h, c, j):
                jj = j - 4 * c
                if jj < 0:
                    dmin = 128 * (-jj) - 127
                elif jj >= 4:
                    dmin = 128 * jj - 511
                else:
                    return True
                return SLOPES_A[h] * dmin <= SKIP_EXP

            work = [(si, h) for si in range(len(seqs)) for h in range(4)]
            W = []
            flat = []
            for wi_, (si, h) in enumerate(work):
                S = seqs[si]
                w = dict(si=si, h=h, S=S, t0=seq_off[si], nk=S // 128, nq=S // 512, kv=None, first={}, last={})
                for c in range(w["nq"]):
                    for j in range(w["nk"]):
                        if keep(h, c, j):
                            w["first"].setdefault(c, j)
                            w["last"][c] = j
                            flat.append((wi_, c, j))
                W.append(w)
            chunk_order = []
            for (wi_, c, j) in flat:
                if not chunk_order or chunk_order[-1] != (wi_, c):
                    chunk_order.append((wi_, c))
            chunk_pos = {k: i for i, k in enumerate(chunk_order)}
            cur_h = [None, None]
            qbufs = {}

            def load_q(wi_, c):
                w = W[wi_]
                h, t0 = w["h"], w["t0"]
                qv, qs = qvr.next()
                bi = qvr.i
                if cur_h[bi] != h:
                    P.dma(SP, augst[64:67, :, :, :], c_augq[h], augst_s, writes=[augst_s])
                    P.cp(DVE, qv[64:67, :, :, :], augst[64:67, :, :, :], reads=[augst_s],
                         writes=[qaug_s[bi], qs])
                    cur_h[bi] = h
                for v in range(3):
                    P.dma(SP, qv[0:64, v, :, :],
                          qkA[2 * h:2 * h + 2, :, t0 + 512 * c:t0 + 512 * c + 512].rearrange(
                              "m d t -> d m t"), qs, writes=[qs])
                return qv, qs

            def ensure_q(wi_, c, j):
                if (wi_, c) not in qbufs:
                    qbufs[(wi_, c)] = load_q(wi_, c)
                w = W[wi_]
                if j == min(w["first"][c] + 1, w["last"][c]):
                    p = chunk_pos[(wi_, c)] + 1
                    if p < len(chunk_order) and chunk_order[p] not in qbufs:
                        qbufs[chunk_order[p]] = load_q(*chunk_order[p])

            def ensure_kv(wi_):
                if W[wi_]["kv"] is None:
                    W[wi_]["kv"] = pa_load_kv(*work[wi_])

            def qk(wi_, c, j):
                w = W[wi_]
                h = w["h"]
                ensure_kv(wi_)
                kt, kts, va, vas = w["kv"]
                qv, qs = qbufs[(wi_, c)]
                pscT, pscS = pscr.next()
                jj = j - 4 * c
                if jj < 0:
                    kind, v, kk = "L", 0, 67
                elif jj >= 4:
                    kind, v, kk = "R", 1, 67
                else:
                    kind, v, kk = "D", 2, 67
                for m in range(2):
                    P.mm(pscT[:, m * 512:(m + 1) * 512], kt[0:kk, m, j * 128:(j + 1) * 128],
                         qv[0:kk, v, m, :], reads=[kts, qs], writes=[pscS])
                if kind == "D":
                    off = 384 - 128 * jj
                    for m in range(2):
                        P.tt(DVE, pscT[:, m * 512:(m + 1) * 512], pscT[:, m * 512:(m + 1) * 512],
                             toep[:, h * 896 + off:h * 896 + off + 512], ALU.add, reads=[pscS, toep_s],
                             writes=[pscS])
                    bias = None
                elif kind == "L":
                    bias = blr[:, (h * 2 + 0) * 64 + (-jj):(h * 2 + 0) * 64 + (-jj) + 1]
                else:
                    bias = blr[:, (h * 2 + 1) * 64 + jj:(h * 2 + 1) * 64 + jj + 1]
                return pscT, pscS, bias

            def post(wi_, c):
                orw, ors = orr.next()
                sml, sml_s = smlr.next()
                for b_ in range(3):
                    wd_ = 387 if b_ < 2 else 258
                    P.cp(DVE, orw[:, 3 * b_:3 * b_ + wd_ // 129, :], paccT[b_][:, 0:wd_].rearrange(
                        "p (a e) -> p a e", e=129), reads=[pacc_s[b_]], writes=[ors])
                yield
                P.recip(sml[:, 0:8], orw[:, :, 128], reads=[ors], writes=[sml_s])
                P.ts(DVE, sml[:, 8:12], sml[:, 4:8], neglam, None, ALU.mult, reads=[sml_s, lt_s],
                     writes=[sml_s])
                ob, obs = obr.next()
                for qs_ in range(4):
                    yield
                    P.ts(DVE, ob[:, qs_, :], orw[:, qs_, 0:128], sml[:, qs_:qs_ + 1], None, ALU.mult,
                         reads=[ors, sml_s], writes=[obs])
                    P.stt(ob[:, qs_, :], orw[:, 4 + qs_, 0:128], sml[:, 8 + qs_:9 + qs_], ob[:, qs_, :],
                          ALU.mult, ALU.add, reads=[ors, sml_s, obs], writes=[obs])
                    P.stt(pjunk[:], ob[:, qs_, :], 1.0, ob[:, qs_, :], ALU.mult, ALU.mult,
                          accum=sml[:, 12 + qs_:13 + qs_], reads=[obs], writes=[pjunk_s, sml_s])
                P.ts(POOL, sml[:, 16:20], sml[:, 12:16], 1.0 / 128, EPS, ALU.mult, ALU.add, reads=[sml_s],
                     writes=[sml_s])
                P.tt(POOL, sml[:, 16:20], sml[:, 16:20], cst["mhalf"][:, 0:4], ALU.pow,
                     reads=[sml_s, cst["mhalf_s"]], writes=[sml_s])
                yield
                yield
                yb, ybs = ybr.next()
                for qs_ in range(4):
                    if qs_ == 2:
                        yield
                    P.ts(DVE, yb[:, qs_, :], ob[:, qs_, :], sml[:, 16 + qs_:17 + qs_], None, ALU.mult,
                         reads=[obs, sml_s], writes=[ybs])
                yield
                post_b(yb, ybs, wi_, c)

            def post_b(yb, ybs, wi_, c):
                w = W[wi_]
                pt, pts = ptr.next()
                for qs_ in range(4):
                    P.tr(pt[:, qs_ * 128:(qs_ + 1) * 128], yb[:, qs_, :], idb[:], reads=[ybs, idb_s],
                         writes=[pts])
                ys, yss = ysr.next()
                P.cp(DVE, ys[:], pt, reads=[pts], writes=[yss])
                tt0 = w["t0"] + 512 * c
                P.dma(SP, yAT[w["h"], :, tt0:tt0 + 512], ys[:], yss, reads=[yss])

            ensure_kv(0)
            pend_post = None
            since_post = 0
            pendq = []
            for it in flat[0:2]:
                ensure_q(*it)
                pendq.append(qk(*it))
            for ii, (wi_, c, j) in enumerate(flat):
                w = W[wi_]
                kt, kts, va, vas = w["kv"]
                pscT, pscS, bias = pendq.pop(0)
                et, ets = etr.next()
                if bias is None:
                    P.act(et[:], pscT[:], AF.Exp, scale=0.125, reads=[pscS], writes=[ets])
                else:
                    P.act(et[:], pscT[:], AF.Exp, bias=bias, scale=0.125, reads=[pscS, blr_s], writes=[ets])
                if ii + 2 < len(flat):
                    ensure_q(*flat[ii + 2])
                    pendq.append(qk(*flat[ii + 2]))
                first, last_ = w["first"][c], w["last"][c]
                for a_ in range(8):
                    m, qs_ = a_ // 4, a_ % 4
                    ap_, as_ = acc_ap(a_)
                    P.mm(ap_, et[:, m * 512 + qs_ * 128:m * 512 + (qs_ + 1) * 128], va[:, j, 0:129],
                         start=(j == first and a_ % 3 == 0), stop=(j == last_), reads=[ets, vas],
                         writes=[as_], skip=True)
                if pend_post is not None:
                    if next(pend_post, "done") == "done":
                        pend_post = None
                if j == last_:
                    if pend_post is not None:
                        for _ in pend_post:
                            pass
                    pend_post = post(wi_, c)
                    next(pend_post)
                    qbufs.pop((wi_, c), None)
                if wi_ + 1 < len(work) and W[wi_ + 1]["kv"] is None and c == min(1, w["nq"] - 1) \
                        and j == min(first + 6, last_):
                    ensure_kv(wi_ + 1)
            if pend_post is not None:
                for _ in pend_post:
                    pass
            end_phase(st)

            st = phase_scope()
            idb, idb_s = load_ident(st, "pb")
            bb, bb_s = small(st, "pbbb", c_bb, [128, 3 * 8 * 128])
            skt, skt_s = small(st, "pbsink", p_sink[l], [128, 8])
            esink = sb(st, "pbesink", [128, 8], F32)
            esink_s = slot(st, "pbesink")
            P.act(esink[:], skt[:], AF.Exp, reads=[skt_s], writes=[esink_s])
            ktbs = [sb(st, "pbkt%d" % i, [128, 2, SMAX], BF16) for i in range(2)]
            ktb_ss = [slot(st, "pbkt%d" % i) for i in range(2)]
            ktz_s = slot(st, "pbktz")
            vbs = [sb(st, "pbvb%d" % i, [128, NKMAX, 2, 66], BF16) for i in range(2)]
            vb_ss = [slot(st, "pbvb%d" % i) for i in range(2)]
            vb1_s = slot(st, "pbvb1")
            for i in range(2):
                P.memset(POOL, ktbs[i][64:128, :, :], 0.0, writes=[ktz_s])
                P.memset(POOL, vbs[i][:, :, :, 64:66], 1.0, writes=[vb1_s])
            qbr = Ring(P, "pbqb", [sb(st, "pbqb%d" % i, [128, 4, 8, 128], BF16) for i in range(2)])
            qbz_s = slot(st, "pbqbz")
            for i in range(2):
                P.memset(POOL, qbr.tiles[i][64:128, :, :, :], 0.0, writes=[qbz_s])
            etbr = Ring(P, "pbet", [sb(st, "pbet%d" % i, [128, 512], BF16) for i in range(4)])
            ybtr = Ring(P, "pbyb", [sb(st, "pbyb%d" % i, [128, 512], BF16) for i in range(2)])
            ysbr = Ring(P, "pbys", [sb(st, "pbys%d" % i, [128, 4, 512], BF16) for i in range(2)])
            smbr = Ring(P, "pbsm", [sb(st, "pbsm%d" % i, [128, 16], F32) for i in range(2)])
            for r in (qbr, etbr, ybtr, ysbr, smbr):
                st.slots.extend(r.slots)
            psbr = Ring(P, "pbpsb", [ps(st, "pbpsb%d" % i, [128, 512], F32) for i in range(3)])
            paccB = [[ps(st, "pbpacc%d_%d" % (k, i), [128, 4, 65], F32) for i in range(2)] for k in range(2)]
            paccB_s = [[slot(st, "pbpacc%d_%d" % (k, i)) for i in range(2)] for k in range(2)]
            ptr = Ring(P, "pbptr", [ps(st, "pbptr0", [128, 512], BF16)[:]])
            st.slots.extend(psbr.slots)
            st.slots.extend(ptr.slots)

            def pb_post(k, ysb, ysbs, t, flush):
                smb, smb_s = smbr.next()
                for g in range(2):
                    P.tt(DVE, smb[:, 4 * g:4 * g + 4], paccB[k][g][:, :, 64], esink[:, 4 * g:4 * g + 4],
                         ALU.add, reads=[paccB_s[k][g], esink_s], writes=[smb_s])
                P.recip(smb[:, 8:16], smb[:, 0:8], reads=[smb_s], writes=[smb_s])
                ybt, ybts = ybtr.next()
                for g in range(2):
                    P.tt(DVE, ybt[:, g * 256:(g + 1) * 256].rearrange("p (h d) -> p h d", d=64),
                         paccB[k][g][:, :, 0:64],
                         smb[:, 8 + 4 * g:12 + 4 * g].unsqueeze(2).to_broadcast([128, 4, 64]), ALU.mult,
                         reads=[paccB_s[k][g], smb_s], writes=[ybts])
                pt, pts = ptr.next()
                for fc in range(4):
                    P.tr(pt[:, fc * 128:(fc + 1) * 128], ybt[:, fc * 128:(fc + 1) * 128], idb[:],
                         reads=[ybts, idb_s], writes=[pts])
                P.cp(ACT, ysb[:, :, t * 128:(t + 1) * 128], pt.rearrange("p (f q) -> p f q", q=128),
                     reads=[pts], writes=[ysbs])
                if flush is not None:
                    P.dma(SP, flush, ysb[:], ysbs, reads=[ysbs])

            units = []
            tile_no = 0
            for si, S in enumerate(seqs):
                nk = S // 128
                for c in range(S // 512):
                    for t in range(4):
                        jq = 4 * c + t
                        rels = [r for r in range(3) if 0 <= jq + r - 1 < nk]
                        for g in range(2):
                            for ri, r in enumerate(rels):
                                units.append(dict(si=si, c=c, t=t, g=g, ri=ri, r=r, jk=jq + r - 1,
                                                  nrel=len(rels), k=tile_no % 2,
                                                  last=(g == 1 and ri == len(rels) - 1)))
                        tile_no += 1
            state = dict(si=None, c=None, qb=None, qbs=None, ysb=None, ysbs=None)

            def pb_qk(u):
                si, c = u["si"], u["c"]
                S = seqs[si]
                t0 = seq_off[si]
                nk = S // 128
                ktb, ktb_s, vb, vb_s = ktbs[si % 2], ktb_ss[si % 2], vbs[si % 2], vb_ss[si % 2]
                if state["si"] != si:
                    for g in range(2):
                        P.dma(SP, ktb[0:64, g, 0:S], qkB[8 + g, :, t0:t0 + S], ktb_s, writes=[ktb_s])
                        for j0 in range(0, nk, 8):
                            nj = min(8, nk - j0)
                            P.dma(SP, vb[:, j0:j0 + nj, g, 0:64],
                                  vAB[t0 + 128 * j0:t0 + 128 * (j0 + nj), 512 + 64 * g:576 + 64 * g].rearrange(
                                      "(j p) e -> p j e", p=128), vb_s, writes=[vb_s])
                    state["si"] = si
                    state["c"] = None
                if state["c"] != c:
                    qb, qbs = qbr.next()
                    for t in range(4):
                        P.dma(SP, qb[0:64, t, :, :],
                              qkB[0:8, :, t0 + 512 * c + 128 * t:t0 + 512 * c + 128 * (t + 1)].rearrange(
                                  "h d q -> d h q"), qbs, writes=[qbs])
                    state["qb"], state["qbs"] = qb, qbs
                    state["c"] = c
                qb, qbs = state["qb"], state["qbs"]
                g, r, jk, t = u["g"], u["r"], u["jk"], u["t"]
                pt, pts = psbr.next()
                P.mm(pt[:], ktb[:, g, jk * 128:(jk + 1) * 128], qb[:, t, 4 * g:4 * g + 4, :],
                     reads=[ktb_s, ktz_s, qbs, qbz_s], writes=[pts])
                P.tt(DVE, pt[:], pt[:], bb[:, (r * 8 + 4 * g) * 128:(r * 8 + 4 * g + 4) * 128],
                     ALU.add, reads=[pts, bb_s], writes=[pts])
                return pt, pts

            pending = None
            pq = [pb_qk(u) for u in units[0:2]]
            cur_ysb = {}
            for i, u in enumerate(units):
                pt, pts = pq.pop(0)
                et, ets = etbr.next()
                P.act(et[:], pt[:], AF.Exp, scale=0.125, reads=[pts], writes=[ets])
                if i + 2 < len(units):
                    pq.append(pb_qk(units[i + 2]))
                k, g, ri, jk = u["k"], u["g"], u["ri"], u["jk"]
                vb, vb_s = vbs[u["si"] % 2], vb_ss[u["si"] % 2]
                for hh in range(4):
                    P.mm(paccB[k][g][:, hh, :], et[:, hh * 128:(hh + 1) * 128], vb[:, jk, g, 0:65],
                         start=(ri == 0 and hh == 0), stop=(ri == u["nrel"] - 1), reads=[ets, vb_s, vb1_s],
                         writes=[paccB_s[k][g]], skip=True)
                if u["last"]:
                    key = (u["si"], u["c"])
                    if key not in cur_ysb:
                        cur_ysb.clear()
                        cur_ysb[key] = ysbr.next()
                    ysb, ysbs = cur_ysb[key]
                    if pending is not None:
                        pb_post(*pending)
                    flush = None
                    if u["t"] == 3:
                        tt0 = seq_off[u["si"]] + 512 * u["c"]
                        flush = yBT[:, :, tt0:tt0 + 512].rearrange("f p t -> p f t")
                    pending = (k, ysb, ysbs, u["t"], flush)
            pb_post(*pending)
            end_phase(st)

            st = phase_scope()
            idb, idb_s = load_ident(st, "pc")
            cst = mk_mhalf(st, "pc")
            stage = mk_stage(st, "pc", n=3)
            subc, subc_s = small(st, "pcsub", p_sub[l], [128, 1])
            wpa = sb(st, "pcwpa", [128, 4, D], BF16)
            wpa_s = slot(st, "pcwpa")
            wpb = sb(st, "pcwpb", [128, 4, D], BF16)
            wpb_s = slot(st, "pcwpb")
            wo = sb(st, "pcwo", [128, 8, D], BF16)
            wo_s = slot(st, "pcwo")
            subc4 = sb(st, "pcsub4", [128, 4], F32)
            subc4_s = slot(st, "pcsub4")
            for i in range(4):
                P.cp(DVE, subc4[:, i:i + 1], subc[:], reads=[subc_s], writes=[subc4_s])
            load_weight(st, "pc", wpa, wpa_s, w_pa[l], 4, D, subc4, subc4_s, s2=(1.0 - lam_init), stage=stage)
            load_weight(st, "pc", wpb, wpb_s, w_pb[l], 4, D, stage=stage)
            load_weight(st, "pc", wo, wo_s, w_out[l], 8, D, stage=stage)
            fcol, fcol_s = small(st, "pcfcol", p_fn[l], [128, 8])
            yar = Ring(P, "pcya", [sb(st, "pcya%d" % i, [128, 4, 512], BF16) for i in range(2)])
            ybr2 = Ring(P, "pcyb", [sb(st, "pcyb%d" % i, [128, 4, 512], BF16) for i in range(2)])
            ggr = Ring(P, "pcgg", [sb(st, "pcgg%d" % i, [128, 16, 512], BF16) for i in range(2)])
            xr = Ring(P, "pcx", [sb(st, "pcx%d" % i, [128, 4, D], F32) for i in range(2)])
            xts_all = [[slot(st, "pcx%d_%d" % (i, t)) for t in range(4)] for i in range(2)]
            t1r = Ring(P, "pct1", [sb(st, "pct1%d" % i, [128, 512], F32) for i in range(2)])
            t2r = Ring(P, "pct2", [sb(st, "pct2%d" % i, [128, 512], F32) for i in range(2)])
            for r in (yar, ybr2, ggr, xr, t1r, t2r):
                st.slots.extend(r.slots)
            mT = sb(st, "pcmT", [128, 8, 512], BF16)
            mT_s = slot(st, "pcmT")
            h2b = sb(st, "pch2", [128, 4, D], BF16)
            h2b_s = slot(st, "pch2")
            h2Ts = sb(st, "pch2T", [128, 8, 512], BF16)
            h2Ts_s = slot(st, "pch2T")
            junk = sb(st, "pcjunk", [128, D], BF16)
            junk_s = slot(st, "pcjunk", waw=True)
            ssq = sb(st, "pcssq", [128, 8], F32)
            ssq_s = slot(st, "pcssq")
            rstd = sb(st, "pcrstd", [128, 8], F32)
            rstd_s = slot(st, "pcrstd")
            ptr = Ring(P, "pcptr", [ps(st, "pcptr%d" % i, [128, 512], BF16)[:] for i in range(2)])
            pp = Ring(P, "pcpp", [ps(st, "pcpp%d" % i, [128, 512], F32) for i in range(6)])
            st.slots.extend(ptr.slots)
            st.slots.extend(pp.slots)

            def pc_load(c):
                t0 = 512 * c
                ya, yas = yar.next()
                yb_, ybs_ = ybr2.next()
                gg, ggs = ggr.next()
                xt, xs = xr.next()
                xs = xts_all[xr.i]
                P.dma(SP, ya[:], yAT[:, :, t0:t0 + 512].rearrange("f p t -> p f t"), yas, writes=[yas])
                P.dma(SP, yb_[:], yBT[:, :, t0:t0 + 512].rearrange("f p t -> p f t"), ybs_, writes=[ybs_])
                P.dma(SP, gg[:], gT[:, :, t0:t0 + 512].rearrange("c p t -> p c t"), ggs, writes=[ggs])
                P.dma(SP, xt[:], xsrc[t0:t0 + 512, :].rearrange("(t p) d -> p t d", p=128), xs[0], writes=xs)
                return ya, yas, yb_, ybs_, gg, ggs, xt, xs

            pc_pending = [None]

            def pc_flush():
                if pc_pending[0] is None:
                    return
                tp0 = pc_pending[0]
                transposes(h2b, h2b_s, h2Ts, h2Ts_s, ptr, idb, idb_s)
                P.dma(SP, h2T[:, :, tp0:tp0 + 512].rearrange("f p t -> p f t"), h2Ts[:], h2Ts_s, reads=[h2Ts_s])
                cb_ = tp0 // 512
                for e_, col_ in ((0, 0), (1, 511)):
                    P.dma(SP, hbnd[:, :, 2 * cb_ + e_:2 * cb_ + e_ + 1].rearrange("f p t -> p f t"),
                          h2Ts[:, :, col_:col_ + 1], h2Ts_s, reads=[h2Ts_s], allow_slow_non_contiguous=True)
                pc_pending[0] = None

            nxt = pc_load(0)
            for c in range(NCH):
                t0 = 512 * c
                ya, yas, yb_, ybs_, gg, ggs, xt, xs = nxt
                if c + 1 < NCH:
                    nxt = pc_load(c + 1)
                for cc in range(8):
                    pa_, pas_ = pp.next()
                    for fc in range(4):
                        P.mm(pa_[:], wpa[:, fc, cc * 128:(cc + 1) * 128], ya[:, fc, :], start=(fc == 0),
                             stop=(fc == 3), reads=[wpa_s, yas], writes=[pas_])
                    pb_, pbs_ = pp.next()
                    for fc in range(4):
                        P.mm(pb_[:], wpb[:, fc, cc * 128:(cc + 1) * 128], yb_[:, fc, :], start=(fc == 0),
                             stop=(fc == 3), reads=[wpb_s, ybs_], writes=[pbs_])
                    t1, t1s = t1r.next()
                    t2, t2s = t2r.next()
                    P.tt(DVE, t1[:], pa_[:], gg[:, cc, :], ALU.mult, reads=[pas_, ggs], writes=[t1s])
                    P.tt(DVE, t2[:], pb_[:], gg[:, 8 + cc, :], ALU.mult, reads=[pbs_, ggs], writes=[t2s])
                    P.tt(POOL, mT[:, cc, :], t1[:], t2[:], ALU.add, reads=[t1s, t2s], writes=[mT_s])
                    if cc == 7:
                        pc_flush()
                for t in range(4):
                    for hh in range(2):
                        pt, pts = pp.next()
                        for fc in range(8):
                            P.mm(pt[:], mT[:, fc, t * 128:(t + 1) * 128], wo[:, fc, hh * 512:(hh + 1) * 512],
                                 start=(fc == 0), stop=(fc == 7), reads=[mT_s, wo_s], writes=[pts])
                        P.tt(DVE, xt[:, t, hh * 512:(hh + 1) * 512], pt[:], xt[:, t, hh * 512:(hh + 1) * 512],
                             ALU.add, reads=[pts, xs[t]], writes=[xs[t]])
                    P.dma(SP, xmid[t0 + 128 * t:t0 + 128 * (t + 1), :], xt[:, t, :], xs[t], reads=[xs[t]])
                    rms_stats(cst, [xt[:, t, :]], xs[t], ssq[:, t:t + 1], ssq_s, rstd[:, t:t + 1], rstd_s, junk,
                              junk_s, 1, D)
                    P.act(h2b[:, t, :], xt[:, t, :], AF.Copy, scale=rstd[:, t:t + 1], reads=[xs[t], rstd_s],
                          writes=[h2b_s])
                pc_pending[0] = t0
            pc_flush()
            end_phase(st)

            st = phase_scope()
            cst = mk_mhalf(st, "pf")
            stage = mk_stage(st, "pf", n=3, width=512)
            fcol, fcol_s = small(st, "pffcol", p_fn[l], [128, 8])
            cw, cw_s = small(st, "pfcw", p_cw[l], [128, 3 * NFC])
            cb, cb_s = small(st, "pfcb", p_cb[l], [128, NFC])
            wu = sb(st, "pfwu", [128, 8, 2 * DFF], BF16)
            _r = []
            for b_ in range(6):
                _r.append((512 * b_, min(512 * b_ + 512, DFF)))
                _r.append((DFF + 512 * b_, min(DFF + 512 * b_ + 512, 2 * DFF)))
            wu_w = WSlots(st, "pfwu", _r)
            wd = sb(st, "pfwd", [128, NFC, D], BF16)
            wd_s = slot(st, "pfwd")
            if last:
                gfin, gfin_s = small(st, "pfgfin", p_fin, [128, D])
            hx = sb(st, "pfhx", [128, 8, 512], BF16)
            hx_s = slot(st, "pfhx")
            NB2 = 2 * NCH
            hbt = sb(st, "pfhbt", [128, 8, NB2], BF16)
            hbt_s = slot(st, "pfhbt")
            P.dma(SP, hbt[:], hbnd.rearrange("f p t -> p f t"), hbt_s, writes=[hbt_s])
            ahalo = sb(st, "pfahalo", [128, NFC, NB2], F32)
            ahalo_s = slot(st, "pfahalo")
            xq = [sb(st, "pfx%d" % i, [128, D], F32) for i in range(4)]
            xq_s = [slot(st, "pfx%d" % i) for i in range(4)]
            uT = sb(st, "pfuT", [128, NFC, 512], BF16)
            uT_s = slot(st, "pfuT")
            cr = Ring(P, "pfc", [sb(st, "pfc%d" % i, [128, 512], F32) for i in range(2)])
            grr = Ring(P, "pfg", [sb(st, "pfg%d" % i, [128, 512], F32) for i in range(2)])
            st.slots.extend(cr.slots)
            st.slots.extend(grr.slots)
            junk = sb(st, "pfjunk", [128, D], BF16)
            junk_s = slot(st, "pfjunk", waw=True)
            ssq = sb(st, "pfssq", [128, 8], F32)
            ssq_s = slot(st, "pfssq")
            rstd = sb(st, "pfrstd", [128, 8], F32)
            rstd_s = slot(st, "pfrstd")
            pp = Ring(P, "pfpp", [ps(st, "pfpp%d" % i, [128, 512], F32) for i in range(7)])
            st.slots.extend(pp.slots)
            phal = ps(st, "pfhal", [128, 512], F32)
            phal_s = slot(st, "pfhal")
            def load_hx(c):
                t0 = 512 * c
                P.dma(SP, hx[:], h2T[:, :, t0:t0 + 512].rearrange("f p t -> p f t"), hx_s, writes=[hx_s])

            load_hx(0)
            load_weight(st, "pf", wu, wu_w, w_up[l], 8, 2 * DFF, fcol, fcol_s, stage=stage)
            load_weight(st, "pf", wd, wd_s, w_dn[l], NFC, D, stage=stage)
            for c in range(NCH):
                t0 = 512 * c
                for t in range(4):
                    P.dma(dma_eng(), xq[t][:], xmid[t0 + 128 * t:t0 + 128 * (t + 1), :], xq_s[t], writes=[xq_s[t]])
                si, cs = chunk_seq[c]
                first_c = (cs == 0)
                last_c = (cs == seqs[si] // 512 - 1)
                for cc in range(NFC):
                    if c == 0:
                        for kc in range(8):
                            P.mm(phal[:, 0:NB2], wu[:, kc, cc * 128:(cc + 1) * 128], hbt[:, kc, :],
                                 start=(kc == 0), stop=(kc == 7), reads=[wu_w.s(cc * 128), hbt_s],
                                 writes=[phal_s])
                        P.cp(DVE, ahalo[:, cc, :], phal[:, 0:NB2], reads=[phal_s], writes=[ahalo_s])
                    pa_, pas_ = pp.next()
                    for kc in range(8):
                        P.mm(pa_[:], wu[:, kc, cc * 128:(cc + 1) * 128], hx[:, kc, :], start=(kc == 0),
                             stop=(kc == 7), reads=[wu_w.s(cc * 128), hx_s], writes=[pas_])
                    pv_, pvs_ = pp.next()
                    for kc in range(8):
                        P.mm(pv_[:], wu[:, kc, DFF + cc * 128:DFF + (cc + 1) * 128], hx[:, kc, :],
                             start=(kc == 0), stop=(kc == 7), reads=[wu_w.s(DFF + cc * 128), hx_s], writes=[pvs_])
                    ct, cs_ = cr.next()
                    w0 = cw[:, 0 * NFC + cc:0 * NFC + cc + 1]
                    w1 = cw[:, 1 * NFC + cc:1 * NFC + cc + 1]
                    w2 = cw[:, 2 * NFC + cc:2 * NFC + cc + 1]
                    P.act(ct[:], pa_[:], AF.Identity, bias=cb[:, cc:cc + 1], scale=w1, reads=[pas_, cb_s, cw_s],
                          writes=[cs_])
                    P.stt(ct[:, 1:512], pa_[:, 0:511], w0, ct[:, 1:512], ALU.mult, ALU.add,
                          reads=[pas_, cw_s, cs_], writes=[cs_])
                    P.stt(ct[:, 0:511], pa_[:, 1:512], w2, ct[:, 0:511], ALU.mult, ALU.add,
                          reads=[pas_, cw_s, cs_], writes=[cs_])
                    if not first_c:
                        P.stt(ct[:, 0:1], ahalo[:, cc, 2 * c - 1:2 * c], w0, ct[:, 0:1], ALU.mult, ALU.add,
                              reads=[ahalo_s, cw_s, cs_], writes=[cs_])
                    if not last_c:
                        P.stt(ct[:, 511:512], ahalo[:, cc, 2 * c + 2:2 * c + 3], w2, ct[:, 511:512], ALU.mult,
                              ALU.add, reads=[ahalo_s, cw_s, cs_], writes=[cs_])
                    gt_, gs_ = grr.next()
                    P.act(gt_[:], ct[:], AF.Gelu_apprx_tanh, reads=[cs_], writes=[gs_])
                    P.tt(DVE, uT[:, cc, :], pv_[:], gt_[:], ALU.mult, reads=[pvs_, gs_], writes=[uT_s])
                if c + 1 < NCH:
                    load_hx(c + 1)
                for t in range(4):
                    for hh in range(2):
                        pt, pts = pp.next()
                        for fc in range(NFC):
                            P.mm(pt[:], uT[:, fc, t * 128:(t + 1) * 128], wd[:, fc, hh * 512:(hh + 1) * 512],
                                 start=(fc == 0), stop=(fc == NFC - 1), reads=[uT_s, wd_s], writes=[pts])
                        P.tt(DVE, xq[t][:, hh * 512:(hh + 1) * 512], pt[:], xq[t][:, hh * 512:(hh + 1) * 512],
                             ALU.add, reads=[pts, xq_s[t]], writes=[xq_s[t]])
                    if not last:
                        P.dma(dma_eng(), xres[t0 + 128 * t:t0 + 128 * (t + 1), :], xq[t][:], xq_s[t],
                              reads=[xq_s[t]])
                    else:
                        rms_stats(cst, [xq[t][:]], xq_s[t], ssq[:, t:t + 1], ssq_s, rstd[:, t:t + 1], rstd_s,
                                  junk, junk_s, 1, D)
                        P.stt(xq[t][:], xq[t][:], rstd[:, t:t + 1], gfin[:], ALU.mult, ALU.mult,
                              reads=[xq_s[t], rstd_s, gfin_s], writes=[xq_s[t]])
                        P.dma(dma_eng(), yout[t0 + 128 * t:t0 + 128 * (t + 1), :], xq[t][:], xq_s[t],
                              reads=[xq_s[t]])
            end_phase(st)
    return nc


def _bf16_round(x):
    x = np.asarray(x, np.float32)
    u = x.view(np.uint32).astype(np.uint64)
    r = ((u + 0x7FFF + ((u >> 16) & 1)) & 0xFFFF0000).astype(np.uint32)
    return r.view(np.float32)


def make_consts():
    c = {}
    c["c_ident"] = np.eye(128, dtype=np.float32)
    qi = np.arange(512, dtype=np.float64)
    aug = np.zeros((4, 3, 3, 2, 512), np.float32)
    for h in range(4):
        for v, sgn in enumerate((-1.0, 1.0)):
            val = (sgn * 8.0 * SLOPES_A[h] * qi).astype(np.float32)
            hi = _bf16_round(val)
            mid = _bf16_round(val - hi)
            lo = _bf16_round(val - hi - mid)
            for r, part in enumerate((hi, mid, lo)):
                aug[h, r, v, :, :] = part[None, :]
    c["c_augq"] = aug
    ki = np.arange(128, dtype=np.float64)[:, None]
    m = np.arange(64, dtype=np.float64)[None, :]
    blr = np.zeros((128, 4, 2, 64), np.float32)
    for h in range(4):
        blr[:, h, 0, :] = SLOPES_A[h] * (ki - 128.0 * m)
        blr[:, h, 1, :] = -SLOPES_A[h] * (128.0 * m + ki)
    c["c_blr"] = blr.reshape(128, -1)
    xx = np.arange(896, dtype=np.float64)[None, :]
    toep = np.zeros((128, 4, 896), np.float32)
    for h in range(4):
        toep[:, h, :] = -8.0 * SLOPES_A[h] * np.abs(xx - 384.0 - ki)
    c["c_toep"] = toep.reshape(128, -1)
    qq = np.arange(128, dtype=np.float64)[None, :]
    bb = np.zeros((128, 3, 8, 128), np.float32)
    for r in range(3):
        dist = np.abs(128.0 * (r - 1) + ki - qq)
        for h in range(8):
            bb[:, r, h, :] = np.where(dist <= 128.0, -8.0 * SLOPES_B[h] * dist, 8.0 * NEGBIG)
    c["c_bb"] = bb.reshape(128, -1)
    return c


def layout_params(inp):
    f = lambda a: np.ascontiguousarray(np.asarray(a, np.float32))
    p = {}
    p["p_an"] = f(np.asarray(inp["attn_norm"]).reshape(DEPTH, 8, 128).transpose(0, 2, 1))
    p["p_fn"] = f(np.asarray(inp["ffn_norm"]).reshape(DEPTH, 8, 128).transpose(0, 2, 1))
    p["p_gb"] = f(np.asarray(inp["gate_bias"]).reshape(DEPTH, 16, 128).transpose(0, 2, 1))
    lam = np.concatenate([np.asarray(inp[k]) for k in ("lambda_q1", "lambda_k1", "lambda_q2", "lambda_k2")],
                         axis=1)
    p["p_lam"] = f(np.broadcast_to(lam[:, None, :], (DEPTH, 128, 256)))
    p["p_sub"] = f(np.asarray(inp["subln"]).reshape(DEPTH, 128, 1))
    p["p_sink"] = f(np.broadcast_to(np.asarray(inp["sink"])[:, None, :], (DEPTH, 128, 8)))
    p["p_cw"] = f(np.asarray(inp["conv_w"]).reshape(DEPTH, 3, NFC, 128).transpose(0, 3, 1, 2).reshape(
        DEPTH, 128, 3 * NFC))
    p["p_cb"] = f(np.asarray(inp["conv_b"]).reshape(DEPTH, NFC, 128).transpose(0, 2, 1))
    p["p_fin"] = f(np.broadcast_to(np.asarray(inp["final_norm"])[None, :], (128, D)))
    for k in ("w_in", "w_proj_a", "w_proj_b", "w_out", "w_up", "w_down"):
        p[k] = f(inp[k])
    return p


_NC_CACHE = {}


def run(core_x, inp, seqs, n_layers=DEPTH):
    key = (tuple(seqs), n_layers)
    if key not in _NC_CACHE:
        _NC_CACHE[key] = build(list(seqs), n_layers)
    nc = _NC_CACHE[key]
    shared = dict(make_consts())
    shared.update(layout_params(inp))
    in_maps = []
    for x in core_x:
        m = dict(shared)
        m["xin"] = np.ascontiguousarray(x, dtype=np.float32)
        in_maps.append(m)
    res = run_bass_kernel_spmd(nc, in_maps, core_ids=list(range(len(core_x))))
    return [r["yout"] for r in res.results]


def kernel(**inputs):
    xp = np.asarray(inputs["x_prompt"], np.float32)
    xs = np.asarray(inputs["x_sample"], np.float32)
    nb_p = xp.shape[0] // N_CORES
    nb_s = xs.shape[0] // N_CORES
    seqs = [xp.shape[1]] * nb_p + [xs.shape[1]] * nb_s
    core_x = []
    for i in range(N_CORES):
        parts = [xp[i * nb_p + b] for b in range(nb_p)] + [xs[i * nb_s + b] for b in range(nb_s)]
        core_x.append(np.concatenate(parts, axis=0))
    outs = run(core_x, inputs, seqs)
    yp = np.empty_like(xp)
    ys = np.empty_like(xs)
    for i in range(N_CORES):
        o = outs[i]
        off = 0
        for b in range(nb_p):
            yp[i * nb_p + b] = o[off:off + xp.shape[1]]
            off += xp.shape[1]
        for b in range(nb_s):
            ys[i * nb_s + b] = o[off:off + xs.shape[1]]
            off += xs.shape[1]
    return (yp, ys)
```

```python
import contextlib
import math
import numpy as np
import concourse.bass as bass
import concourse.mybir as mybir
from concourse.bass_utils import run_bass_kernel_spmd

F32 = mybir.dt.float32
BF16 = mybir.dt.bfloat16
AF = mybir.ActivationFunctionType
ALU = mybir.AluOpType

PE, ACT, DVE, POOL, SP = "tensor", "scalar", "vector", "gpsimd", "sync"
ENGS = [PE, ACT, DVE, POOL, SP]

D = 1024
DEPTH = 2
IN_COLS = 4352
DFF = 2816
NFC = 22
EPS = 1e-6
N_CORES = 8
STOP_AFTER = 10 ** 9
SLOPES_A = [2.0 ** (-8.0 * (h + 1) / 4) for h in range(4)]
SLOPES_B = [2.0 ** (-8.0 * (h + 1) / 8) for h in range(8)]
NEGBIG = -30000.0
SKIP_EXP = 88.0 + 92.3


class Slot:
    __slots__ = ("name", "writers", "readers", "prev_readers", "dsem", "waw")

    def __init__(self, name, waw=False):
        self.name = name
        self.waw = waw
        self.writers = []
        self.readers = []
        self.prev_readers = []
        self.dsem = None


class Op:
    __slots__ = ("eng", "fn", "deps", "signal", "count", "dma", "dtok")

    def __init__(self, eng, fn):
        self.eng = eng
        self.fn = fn
        self.deps = []
        self.signal = False
        self.count = None
        self.dma = False
        self.dtok = None


class Prog:
    def __init__(self, nc, stack, n_dma_sems=88):
        self.nc = nc
        self.ops = {e: [] for e in ENGS}
        self.esem = {e: stack.enter_context(nc.semaphore("S_" + e)) for e in ENGS}
        self.dsem = [stack.enter_context(nc.semaphore("D%d" % i)) for i in range(n_dma_sems)]
        self.dcount = [0] * n_dma_sems
        self.next_dsem = 0
        self.free_dsems = []
        self.ecount = {e: 0 for e in ENGS}
        self.waited_e = {e: {x: 0 for x in ENGS} for e in ENGS}
        self.waited_d = {e: [0] * n_dma_sems for e in ENGS}
        self.slots = []
        self.last_sig = {e: None for e in ENGS}

    def slot(self, name, waw=False):
        s = Slot(name, waw)
        self.slots.append(s)
        return s

    def release_slots(self, slots):
        for s in slots:
            if s.dsem is not None:
                self.free_dsems.append(s.dsem)
                s.dsem = None
        ids = set(id(s) for s in slots)
        self.slots = [s for s in self.slots if id(s) not in ids]

    def _mkdeps(self, op, reads, writes):
        deps = []
        for s in reads:
            deps.extend(s.writers)
        for s in writes:
            if s.readers:
                s.prev_readers = s.readers
                deps.extend(s.writers)
                s.readers = []
                s.writers = []
            deps.extend(s.prev_readers)
            if s.waw:
                deps.extend(s.writers)
        for s in reads:
            s.readers.append(op)
        for s in writes:
            s.writers.append(op)
        seen = set()
        out = []
        for d in deps:
            if d is op or id(d) in seen:
                continue
            seen.add(id(d))
            out.append(d)
        return out

    def op(self, eng, fn, reads=(), writes=(), extra=()):
        o = Op(eng, fn)
        o.deps = self._mkdeps(o, reads, writes) + list(extra)
        for d in o.deps:
            if not d.dma:
                d.signal = True
        self.ops[eng].append(o)
        return o

    def dma(self, eng, out, in_, sb, reads=(), writes=(), extra=(), **kw):
        o = Op(eng, lambda e: e.dma_start(out=out, in_=in_, **kw))
        o.dma = True
        o.deps = self._mkdeps(o, reads, writes) + list(extra)
        for d in o.deps:
            if not d.dma:
                d.signal = True
        if sb.dsem is None:
            if self.free_dsems:
                sb.dsem = self.free_dsems.pop()
            else:
                sb.dsem = self.next_dsem
                self.next_dsem += 1
                assert self.next_dsem <= len(self.dsem), "out of DMA semaphores"
        self.dcount[sb.dsem] += 16
        o.dtok = (sb.dsem, self.dcount[sb.dsem])
        self.ops[eng].append(o)
        return o

    def mm(self, out, lhsT, rhs, start=True, stop=True, reads=(), writes=(), skip=False):
        return self.op(PE, lambda e: e.matmul(out, lhsT=lhsT, rhs=rhs, start=start, stop=stop,
                                              skip_group_check=skip), reads, writes)

    def tr(self, out, in_, ident, reads=(), writes=()):
        return self.op(PE, lambda e: e.transpose(out=out, in_=in_, identity=ident), reads, writes)

    def act(self, out, in_, func, bias=None, scale=None, accum=None, reads=(), writes=()):
        kw = {}
        if bias is not None:
            kw["bias"] = bias
        if scale is not None:
            kw["scale"] = scale
        if accum is not None:
            kw["accum_out"] = accum
        return self.op(ACT, lambda e: e.activation(out=out, in_=in_, func=func, **kw), reads, writes)

    def ts(self, eng, out, in0, s1, s2, op0, op1=None, reads=(), writes=()):
        if op1 is None:
            return self.op(eng, lambda e: e.tensor_scalar(out=out, in0=in0, scalar1=s1, scalar2=None,
                                                          op0=op0), reads, writes)
        return self.op(eng, lambda e: e.tensor_scalar(out=out, in0=in0, scalar1=s1, scalar2=s2,
                                                      op0=op0, op1=op1), reads, writes)

    def tt(self, eng, out, in0, in1, op, reads=(), writes=()):
        return self.op(eng, lambda e: e.tensor_tensor(out=out, in0=in0, in1=in1, op=op), reads, writes)

    def stt(self, out, in0, scalar, in1, op0, op1, accum=None, reads=(), writes=()):
        if accum is None:
            return self.op(DVE, lambda e: e.scalar_tensor_tensor(out=out, in0=in0, scalar=scalar, in1=in1,
                                                                 op0=op0, op1=op1), reads, writes)
        return self.op(DVE, lambda e: e.scalar_tensor_tensor(out=out, in0=in0, scalar=scalar, in1=in1,
                                                             op0=op0, op1=op1, accum_out=accum),
                       reads, writes)

    def cp(self, eng, out, in_, reads=(), writes=()):
        if eng == ACT:
            return self.op(ACT, lambda e: e.copy(out=out, in_=in_), reads, writes)
        return self.op(eng, lambda e: e.tensor_copy(out=out, in_=in_), reads, writes)

    def memset(self, eng, ap, val, writes=()):
        return self.op(eng, lambda e: e.memset(ap, val), (), writes)

    def recip(self, out, in_, reads=(), writes=()):
        return self.op(DVE, lambda e: e.reciprocal(out=out, in_=in_), reads, writes)

    def barrier(self):
        lasts = []
        for e in ENGS:
            for o in reversed(self.ops[e]):
                if not o.dma:
                    lasts.append(o)
                    break
            else:
                if self.last_sig[e] is not None:
                    lasts.append(self.last_sig[e])
        dtoks = [(i, c) for i, c in enumerate(self.dcount) if c > 0]
        for e in ENGS:
            o = Op(e, ("barrier", dtoks))
            o.deps = [l for l in lasts if l.eng != e]
            for d in o.deps:
                d.signal = True
            self.ops[e].append(o)
        for s in self.slots:
            s.writers = []
            s.readers = []
            s.prev_readers = []

    def emit(self):
        nc = self.nc
        for e in ENGS:
            for o in self.ops[e]:
                if o.signal and not o.dma and o.count is None:
                    self.ecount[e] += 1
                    o.count = self.ecount[e]
                    self.last_sig[e] = o
        esem, dsem = self.esem, self.dsem

        def run(en, e):
            we = self.waited_e[en]
            wd = self.waited_d[en]
            for o in self.ops[en]:
                dmax = {}
                for d in o.deps:
                    if d.dma:
                        si, v = d.dtok
                        if dmax.get(si, 0) < v:
                            dmax[si] = v
                for si, v in dmax.items():
                    if wd[si] < v:
                        e.wait_ge(dsem[si], v)
                        wd[si] = v
                for d in o.deps:
                    if d.dma:
                        continue
                    else:
                        if d.eng == en and en == PE:
                            continue
                        if we[d.eng] < d.count:
                            e.wait_ge(esem[d.eng], d.count)
                            we[d.eng] = d.count
                if isinstance(o.fn, tuple):
                    for si, v in o.fn[1]:
                        if wd[si] < v:
                            e.wait_ge(dsem[si], v)
                            wd[si] = v
                    if o.signal:
                        e.nop().then_inc(esem[en], 1)
                    continue
                ins = o.fn(e)
                if o.dma:
                    ins.then_inc(dsem[o.dtok[0]], 16)
                elif o.signal:
                    ins.then_inc(esem[en], 1)

        with nc.Block() as block:
            @block.tensor
            def _(e):
                run(PE, e)

            @block.scalar
            def _(e):
                run(ACT, e)

            @block.vector
            def _(e):
                run(DVE, e)

            @block.gpsimd
            def _(e):
                run(POOL, e)

            @block.sync
            def _(e):
                run(SP, e)
        self.ops = {e: [] for e in ENGS}


class Ring:
    def __init__(self, P, name, tiles):
        self.tiles = tiles
        self.slots = [P.slot("%s%d" % (name, i)) for i in range(len(tiles))]
        self.i = -1

    def next(self):
        self.i = (self.i + 1) % len(self.tiles)
        return self.tiles[self.i], self.slots[self.i]

    def cur(self):
        return self.tiles[self.i], self.slots[self.i]


def build(seqs, n_layers=DEPTH):
    T = sum(seqs)
    SMAX = max(seqs)
    NKMAX = SMAX // 128
    nc = bass.Bass("TRN2", target_bir_lowering=False)

    def din(name, shape, dt=F32):
        return nc.dram_tensor(name, list(shape), dt, kind="ExternalInput").ap()

    def dscr(name, shape, dt):
        return nc.dram_tensor(name, list(shape), dt, kind="Internal").ap()

    xin = din("xin", [T, D])
    w_in = din("w_in", [DEPTH, D, IN_COLS])
    w_pa = din("w_proj_a", [DEPTH, 512, D])
    w_pb = din("w_proj_b", [DEPTH, 512, D])
    w_out = din("w_out", [DEPTH, D, D])
    w_up = din("w_up", [DEPTH, D, 2 * DFF])
    w_dn = din("w_down", [DEPTH, DFF, D])
    p_an = din("p_an", [DEPTH, 128, 8])
    p_fn = din("p_fn", [DEPTH, 128, 8])
    p_gb = din("p_gb", [DEPTH, 128, 16])
    p_lam = din("p_lam", [DEPTH, 128, 256])
    p_sub = din("p_sub", [DEPTH, 128, 1])
    p_sink = din("p_sink", [DEPTH, 128, 8])
    p_cw = din("p_cw", [DEPTH, 128, 3 * NFC])
    p_cb = din("p_cb", [DEPTH, 128, NFC])
    p_fin = din("p_fin", [128, D])
    c_ident = din("c_ident", [128, 128])
    c_augq = din("c_augq", [4, 3, 3, 2, 512])
    c_blr = din("c_blr", [128, 4 * 2 * 64])
    c_toep = din("c_toep", [128, 4 * 896])
    c_bb = din("c_bb", [128, 3 * 8 * 128])
    yout = nc.dram_tensor("yout", [T, D], F32, kind="ExternalOutput").ap()

    qkA = dscr("s_qkA", [16, 64, T], BF16)
    qkB = dscr("s_qkB", [10, 64, T], BF16)
    vAB = dscr("s_vAB", [T, 640], BF16)
    gT = dscr("s_gT", [16, 128, T], BF16)
    yAT = dscr("s_yAT", [4, 128, T], BF16)
    yBT = dscr("s_yBT", [4, 128, T], BF16)
    xmid = dscr("s_xmid", [T, D], F32)
    h2T = dscr("s_h2T", [8, 128, T], BF16)
    xres = dscr("s_xres", [T, D], F32)

    seq_off = [sum(seqs[:i]) for i in range(len(seqs))]
    NCH = T // 512
    chunk_seq = []
    for si, S in enumerate(seqs):
        for c in range(S // 512):
            chunk_seq.append((si, c))

    with contextlib.ExitStack() as top:
        P = Prog(nc, top)

        def phase_scope():
            st = contextlib.ExitStack()
            st.slots = []
            return st

        uniq = [0]

        def sb(st, name, shape, dt):
            uniq[0] += 1
            return st.enter_context(nc.sbuf_tensor("%s_%d" % (name, uniq[0]), list(shape), dt))

        def ps(st, name, shape, dt):
            uniq[0] += 1
            return st.enter_context(nc.psum_tensor("%s_%d" % (name, uniq[0]), list(shape), dt))

        def slot(st, name, waw=False):
            s = P.slot(name, waw)
            st.slots.append(s)
            return s

        nphase = [0]

        class StopBuild(Exception):
            pass

        def end_phase(st):
            P.barrier()
            P.emit()
            P.release_slots(st.slots)
            st.close()
            nphase[0] += 1
            if nphase[0] >= STOP_AFTER:
                raise StopBuild()

        dmaq = [SP, POOL]
        dq = [0]

        def dma_eng():
            return SP

        def load_ident(st, pfx):
            idf = sb(st, pfx + "idf", [128, 128], F32)
            idb = sb(st, pfx + "idb", [128, 128], BF16)
            s1, s2 = slot(st, pfx + "idf"), slot(st, pfx + "idb")
            P.dma(SP, idf[:], c_ident[:, :], s1, writes=[s1])
            P.cp(DVE, idb[:], idf[:], reads=[s1], writes=[s2])
            return idb, s2

        cast_rr = [0]

        class WSlots:
            def __init__(self, st, name, ranges):
                self.ranges = [(a_, b_, slot(st, "%s_%d" % (name, a_))) for (a_, b_) in ranges]

            def s(self, col):
                for a_, b_, sl in self.ranges:
                    if a_ <= col < b_:
                        return sl
                raise KeyError(col)

        def load_weight(st, pfx, dst, dst_slot, src, n_k, n_cols, scol=None, scol_slot=None, s2=None,
                        stage=None):
            stg_ring = stage
            cb = 1024
            if isinstance(dst_slot, WSlots):
                wsl = dst_slot
                order = [(kc, a_, b_ - a_, sl) for (a_, b_, sl) in wsl.ranges for kc in range(n_k)]
            else:
                order = [(kc, c0, min(cb, n_cols - c0), dst_slot) for kc in range(n_k)
                         for c0 in range(0, n_cols, cb)]
            if True:
                for (kc, c0, cw, dst_slot) in order:
                    stg, sslot = stg_ring.next()
                    P.dma(dma_eng(), stg[:, 0:cw], src[kc * 128:(kc + 1) * 128, c0:c0 + cw], sslot,
                          writes=[sslot])
                    o = dst[:, kc, c0:c0 + cw]
                    if scol is None:
                        P.cp(POOL, o, stg[:, 0:cw], reads=[sslot], writes=[dst_slot])
                    else:
                        sc = scol[:, kc:kc + 1]
                        P.ts(POOL, o, stg[:, 0:cw], sc, float(1.0 if s2 is None else s2), ALU.mult, ALU.mult,
                             reads=[sslot, scol_slot], writes=[dst_slot])

        def mk_stage(st, pfx, n=2):
            tiles = [sb(st, "%sstg%d" % (pfx, i), [128, 1024], F32) for i in range(n)]
            r = Ring(P, pfx + "stg", tiles)
            st.slots.extend(r.slots)
            return r

        def small(st, name, src, shape):
            t = sb(st, name, shape, F32)
            s = slot(st, name)
            P.dma(SP, t[:], src, s, writes=[s])
            return t, s

        def rms_stats(st_tiles, x_ap_list, x_slot, ssq, ssq_slot, rstd, rstd_slot, junk, junk_slot, n, dim):
            for i in range(n):
                P.act(junk[:, 0:dim], x_ap_list[i], AF.Square, accum=ssq[:, i:i + 1], reads=[x_slot],
                      writes=[junk_slot, ssq_slot])
            P.ts(POOL, rstd[:, 0:n], ssq[:, 0:n], 1.0 / dim, EPS, ALU.mult, ALU.add, reads=[ssq_slot],
                 writes=[rstd_slot])
            P.tt(POOL, rstd[:, 0:n], rstd[:, 0:n], st_tiles["mhalf"][:, 0:n], ALU.pow,
                 reads=[rstd_slot, st_tiles["mhalf_s"]], writes=[rstd_slot])

        def mk_mhalf(st, pfx):
            t = sb(st, pfx + "mhalf", [128, 8], F32)
            s = slot(st, pfx + "mhalf")
            P.memset(POOL, t[:], -0.5, writes=[s])
            return {"mhalf": t, "mhalf_s": s}

        top.push(lambda et, ev, tb: et is StopBuild)
        for l in range(n_layers):
            lam_init = 0.8 - 0.6 * math.exp(-0.3 * l)
            xsrc = xin if l == 0 else xres
            last = (l == n_layers - 1)

            st = phase_scope()
            idb, idb_s = load_ident(st, "p1")
            cst = mk_mhalf(st, "p1")
            wi = sb(st, "p1wi", [128, 8, IN_COLS], BF16)
            wi_w = WSlots(st, "p1wi", [(0, 512), (512, 1024), (1536, 2176), (2304, 2816), (2816, 3328),
                                       (3328, 3840), (3840, 4352), (1024, 1536), (2176, 2304)])
            stage = mk_stage(st, "p1", n=3)
            gcol, gcol_s = small(st, "p1gcol", p_an[l], [128, 8])
            gbias, gbias_s = small(st, "p1gbias", p_gb[l], [128, 16])

            xr = Ring(P, "p1x", [sb(st, "p1x%d" % i, [128, 4, D], F32) for i in range(2)])
            st.slots.extend(xr.slots)
            hb = sb(st, "p1h", [128, 4, D], BF16)
            hb_s = slot(st, "p1h")
            hT = sb(st, "p1hT", [128, 8, 512], BF16)
            hT_s = slot(st, "p1hT")
            stA = sb(st, "p1stA", [128, 8, 512], BF16)
            stA_s = slot(st, "p1stA")
            stB = sb(st, "p1stB", [128, 5, 512], BF16)
            stB_s = slot(st, "p1stB")
            gr = Ring(P, "p1g", [sb(st, "p1g%d" % i, [128, 8, 512], BF16) for i in range(2)])
            st.slots.extend(gr.slots)
            vr = Ring(P, "p1v", [sb(st, "p1v%d" % i, [128, 4, 640], BF16) for i in range(2)])
            st.slots.extend(vr.slots)
            junk = sb(st, "p1junk", [128, D], BF16)
            junk_s = slot(st, "p1junk", waw=True)
            ssq = sb(st, "p1ssq", [128, 8], F32)
            ssq_s = slot(st, "p1ssq")
            rstd = sb(st, "p1rstd", [128, 8], F32)
            rstd_s = slot(st, "p1rstd")
            ptr = Ring(P, "p1ptr", [ps(st, "p1ptr%d" % i, [128, 512], BF16)[:] for i in range(2)])
            st.slots.extend(ptr.slots)
            pp = Ring(P, "p1pp", [ps(st, "p1pp%d" % i, [128, 512], F32) for i in range(6)])
            st.slots.extend(pp.slots)
            evac_rr = [0]

            def evac_eng():
                evac_rr[0] += 1
                return DVE if evac_rr[0] % 3 else ACT

            def p1_load(c):
                xt, xs = xr.next()
                t0 = 512 * c
                P.dma(SP, xt[:], xsrc[t0:t0 + 512, :].rearrange("(t p) d -> p t d", p=128), xs, writes=[xs])
                return xt, xs

            def p1_norm(xt, xs):
                rms_stats(cst, [xt[:, t, :] for t in range(4)], xs, ssq, ssq_s, rstd, rstd_s, junk, junk_s,
                          4, D)
                for t in range(4):
                    P.ts(DVE, hb[:, t, :], xt[:, t, :], rstd[:, t:t + 1], None, ALU.mult,
                         reads=[xs, rstd_s], writes=[hb_s])

            def transposes(src, src_s, dstT, dstT_s, ptr_ring, idb, idb_s, nfc=8):
                for fc in range(nfc):
                    pt, pts = ptr_ring.next()
                    for t in range(4):
                        P.tr(pt[:, t * 128:(t + 1) * 128], src[:, t, fc * 128:(fc + 1) * 128], idb[:],
                             reads=[src_s, idb_s], writes=[pts])
                    P.cp(evac_eng(), dstT[:, fc, :], pt, reads=[pts], writes=[dstT_s])

            nxt = p1_load(0)
            p1_norm(*nxt)
            load_weight(st, "p1", wi, wi_w, w_in[l], 8, IN_COLS, gcol, gcol_s, stage=stage)
            for c in range(NCH):
                t0 = 512 * c
                transposes(hb, hb_s, hT, hT_s, ptr, idb, idb_s)
                if c + 1 < NCH:
                    nxt = p1_load(c + 1)
                    p1_norm(*nxt)
                for i in range(8):
                    pt, pts = pp.next()
                    for kc in range(8):
                        P.mm(pt[:], wi[:, kc, i * 128:(i + 1) * 128], hT[:, kc, :], start=(kc == 0),
                             stop=(kc == 7), reads=[wi_w.s(i * 128), hT_s], writes=[pts])
                    P.cp(evac_eng(), stA[:, i, :], pt[:], reads=[pts], writes=[stA_s])
                P.dma(SP, qkA[:, :, t0:t0 + 512].rearrange("(i two) d t -> (two d) i t", two=2), stA[:], stA_s,
                      reads=[stA_s])
                for i in range(5):
                    pt, pts = pp.next()
                    c0 = 1536 + i * 128
                    for kc in range(8):
                        P.mm(pt[:], wi[:, kc, c0:c0 + 128], hT[:, kc, :], start=(kc == 0),
                             stop=(kc == 7), reads=[wi_w.s(c0), hT_s], writes=[pts])
                    P.cp(evac_eng(), stB[:, i, :], pt[:], reads=[pts], writes=[stB_s])
                P.dma(SP, qkB[:, :, t0:t0 + 512].rearrange("(i two) d t -> (two d) i t", two=2), stB[:], stB_s,
                      reads=[stB_s])
                for half in range(2):
                    gt, gs = gr.next()
                    for i in range(8):
                        gi = half * 8 + i
                        c0 = 2304 + gi * 128
                        pt, pts = pp.next()
                        for kc in range(8):
                            P.mm(pt[:], wi[:, kc, c0:c0 + 128], hT[:, kc, :], start=(kc == 0), stop=(kc == 7),
                                 reads=[wi_w.s(c0), hT_s], writes=[pts])
                        P.act(gt[:, i, :], pt[:], AF.Sigmoid, bias=gbias[:, gi:gi + 1], reads=[pts, gbias_s],
                              writes=[gs])
                    P.dma(SP, gT[half * 8:half * 8 + 8, :, t0:t0 + 512].rearrange("c p t -> p c t"), gt[:], gs,
                          reads=[gs])
                vt, vs = vr.next()
                for t in range(4):
                    pt, pts = pp.next()
                    for kc in range(8):
                        P.mm(pt[:], hT[:, kc, t * 128:(t + 1) * 128], wi[:, kc, 1024:1536], start=(kc == 0),
                             stop=(kc == 7), reads=[wi_w.s(1024), hT_s], writes=[pts])
                    P.cp(evac_eng(), vt[:, t, 0:512], pt[:], reads=[pts], writes=[vs])
                    pt, pts = pp.next()
                    for kc in range(8):
                        P.mm(pt[:, 0:128], hT[:, kc, t * 128:(t + 1) * 128], wi[:, kc, 2176:2304],
                             start=(kc == 0), stop=(kc == 7), reads=[wi_w.s(2176), hT_s], writes=[pts])
                    P.cp(evac_eng(), vt[:, t, 512:640], pt[:, 0:128], reads=[pts], writes=[vs])
                P.dma(SP, vAB[t0:t0 + 512, :].rearrange("(t p) d -> p t d", p=128), vt[:], vs, reads=[vs])
            end_phase(st)

            st = phase_scope()
            idb, idb_s = load_ident(st, "pa")
            cst = mk_mhalf(st, "pa")
            blr, blr_s = small(st, "pablr", c_blr, [128, 4 * 2 * 64])
            toep, toep_s = small(st, "patoep", c_toep, [128, 4 * 896])
            lamv, lamv_s = small(st, "palam", p_lam[l], [128, 256])
            lt = sb(st, "palt", [128, 8], F32)
            lt_s = slot(st, "palt")
            ljunk = sb(st, "paljunk", [128, 64], F32)
            ljunk_s = slot(st, "paljunk", waw=True)
            P.stt(ljunk[:], lamv[:, 0:64], 1.0, lamv[:, 64:128], ALU.mult, ALU.mult, accum=lt[:, 0:1],
                  reads=[lamv_s], writes=[ljunk_s, lt_s])
            P.stt(ljunk[:], lamv[:, 128:192], 1.0, lamv[:, 192:256], ALU.mult, ALU.mult, accum=lt[:, 1:2],
                  reads=[lamv_s], writes=[ljunk_s, lt_s])
            P.act(lt[:, 2:4], lt[:, 0:2], AF.Exp, reads=[lt_s], writes=[lt_s])
            P.tt(DVE, lt[:, 4:5], lt[:, 3:4], lt[:, 2:3], ALU.subtract, reads=[lt_s], writes=[lt_s])
            P.ts(DVE, lt[:, 5:6], lt[:, 4:5], -lam_init, None, ALU.add, reads=[lt_s], writes=[lt_s])
            neglam = lt[:, 5:6]

            ktr = Ring(P, "pakt", [sb(st, "pakt%d" % i, [67, 2, SMAX], BF16) for i in range(2)])
            var = Ring(P, "pava", [sb(st, "pava%d" % i, [128, NKMAX, 130], BF16) for i in range(2)])
            qvr = Ring(P, "paqv", [sb(st, "paqv%d" % i, [67, 3, 2, 512], BF16) for i in range(2)])
            qaug_s = [slot(st, "paqaug%d" % i) for i in range(2)]
            augst = sb(st, "paaugst", [67, 3, 2, 512], F32)
            augst_s = slot(st, "paaugst")
            etr = Ring(P, "paet", [sb(st, "paet%d" % i, [128, 1024], BF16) for i in range(3)])
            orr = Ring(P, "paor", [sb(st, "paor%d" % i, [128, 8, 129], F32) for i in range(2)])
            obr = Ring(P, "paob", [sb(st, "paob%d" % i, [128, 4, 128], F32) for i in range(2)])
            ybr = Ring(P, "payb", [sb(st, "payb%d" % i, [128, 4, 128], BF16) for i in range(2)])
            ysr = Ring(P, "pays", [sb(st, "pays%d" % i, [128, 512], BF16) for i in range(2)])
            for r in (ktr, var, qvr, etr, orr, obr, ybr, ysr):
                st.slots.extend(r.slots)
            pjunk = sb(st, "pajunk", [128, 128], F32)
            pjunk_s = slot(st, "pajunk", waw=True)
            smlr = Ring(P, "pasml", [sb(st, "pasml%d" % i, [128, 32], F32) for i in range(2)])
            st.slots.extend(smlr.slots)
            pscr = Ring(P, "papsc", [ps(st, "papsc%d" % i, [128, 1024], F32) for i in range(2)])
            paccT = [ps(st, "papacc%d" % i, [128, 512], F32) for i in range(3)]
            pacc_s = [slot(st, "papacc%d" % i) for i in range(3)]
            ptr = Ring(P, "paptr", [ps(st, "paptr%d" % i, [128, 512], BF16)[:] for i in range(1)])
            st.slots.extend(pscr.slots)
            st.slots.extend(ptr.slots)
            for i in range(2):
                P.memset(POOL, ktr.tiles[i][64:67, :, :], 1.0, writes=[ktr.slots[i]])
                P.memset(POOL, var.tiles[i][:, :, 128:130], 1.0, writes=[var.slots[i]])

            def acc_ap(a):
                return paccT[a // 3][:, (a % 3) * 129:(a % 3) * 129 + 129], pacc_s[a // 3]

            def pa_load_kv(si, h):
                S = seqs[si]
                t0 = seq_off[si]
                kt, kts = ktr.next()
                va, vas = var.next()
                for m in range(2):
                    P.dma(SP, kt[0:64, m, 0:S], qkA[8 + 2 * h + m, :, t0:t0 + S], kts, writes=[kts])
                nk = S // 128
                for j0 in range(0, nk, 8):
                    nj = min(8, nk - j0)
                    P.dma(dma_eng(), va[:, j0:j0 + nj, 0:128],
                          vAB[t0 + 128 * j0:t0 + 128 * (j0 + nj), 128 * h:128 * h + 128].rearrange(
                              "(j p) e -> p j e", p=128), vas, writes=[vas])
                return kt, kts, va, vas

            def keep(h, c, j):
                jj = j - 4 * c
                if jj < 0:
                    dmin = 128 * (-jj) - 127
                elif jj >= 4:
                    dmin = 128 * jj - 511
                else:
                    return True
                return SLOPES_A[h] * dmin <= SKIP_EXP

            work = [(si, h) for si in range(len(seqs)) for h in range(4)]
            W = []
            flat = []
            for wi_, (si, h) in enumerate(work):
                S = seqs[si]
                w = dict(si=si, h=h, S=S, t0=seq_off[si], nk=S // 128, nq=S // 512, kv=None, first={}, last={})
                for c in range(w["nq"]):
                    for j in range(w["nk"]):
                        if keep(h, c, j):
                            w["first"].setdefault(c, j)
                            w["last"][c] = j
                            flat.append((wi_, c, j))
                W.append(w)
            chunk_order = []
            for (wi_, c, j) in flat:
                if not chunk_order or chunk_order[-1] != (wi_, c):
                    chunk_order.append((wi_, c))
            chunk_pos = {k: i for i, k in enumerate(chunk_order)}
            cur_h = [None, None]
            qbufs = {}

            def load_q(wi_, c):
                w = W[wi_]
                h, t0 = w["h"], w["t0"]
                qv, qs = qvr.next()
                bi = qvr.i
                if cur_h[bi] != h:
                    P.dma(SP, augst[64:67, :, :, :], c_augq[h], augst_s, writes=[augst_s])
                    P.cp(DVE, qv[64:67, :, :, :], augst[64:67, :, :, :], reads=[augst_s],
                         writes=[qaug_s[bi], qs])
                    cur_h[bi] = h
                for v in range(3):
                    P.dma(SP, qv[0:64, v, :, :],
                          qkA[2 * h:2 * h + 2, :, t0 + 512 * c:t0 + 512 * c + 512].rearrange(
                              "m d t -> d m t"), qs, writes=[qs])
                return qv, qs

            def ensure_q(wi_, c, j):
                if (wi_, c) not in qbufs:
                    qbufs[(wi_, c)] = load_q(wi_, c)
                w = W[wi_]
                if j == min(w["first"][c] + 1, w["last"][c]):
                    p = chunk_pos[(wi_, c)] + 1
                    if p < len(chunk_order) and chunk_order[p] not in qbufs:
                        qbufs[chunk_order[p]] = load_q(*chunk_order[p])

            def ensure_kv(wi_):
                if W[wi_]["kv"] is None:
                    W[wi_]["kv"] = pa_load_kv(*work[wi_])

            def qk(wi_, c, j):
                w = W[wi_]
                h = w["h"]
                ensure_kv(wi_)
                kt, kts, va, vas = w["kv"]
                qv, qs = qbufs[(wi_, c)]
                pscT, pscS = pscr.next()
                jj = j - 4 * c
                if jj < 0:
                    kind, v, kk = "L", 0, 67
                elif jj >= 4:
                    kind, v, kk = "R", 1, 67
                else:
                    kind, v, kk = "D", 2, 67
                for m in range(2):
                    P.mm(pscT[:, m * 512:(m + 1) * 512], kt[0:kk, m, j * 128:(j + 1) * 128],
                         qv[0:kk, v, m, :], reads=[kts, qs], writes=[pscS])
                if kind == "D":
                    off = 384 - 128 * jj
                    for m in range(2):
                        P.tt(DVE, pscT[:, m * 512:(m + 1) * 512], pscT[:, m * 512:(m + 1) * 512],
                             toep[:, h * 896 + off:h * 896 + off + 512], ALU.add, reads=[pscS, toep_s],
                             writes=[pscS])
                    bias = None
                elif kind == "L":
                    bias = blr[:, (h * 2 + 0) * 64 + (-jj):(h * 2 + 0) * 64 + (-jj) + 1]
                else:
                    bias = blr[:, (h * 2 + 1) * 64 + jj:(h * 2 + 1) * 64 + jj + 1]
                return pscT, pscS, bias

            def post(wi_, c):
                orw, ors = orr.next()
                sml, sml_s = smlr.next()
                for b_ in range(3):
                    wd_ = 387 if b_ < 2 else 258
                    P.cp(DVE, orw[:, 3 * b_:3 * b_ + wd_ // 129, :], paccT[b_][:, 0:wd_].rearrange(
                        "p (a e) -> p a e", e=129), reads=[pacc_s[b_]], writes=[ors])
                yield
                P.recip(sml[:, 0:8], orw[:, :, 128], reads=[ors], writes=[sml_s])
                P.ts(DVE, sml[:, 8:12], sml[:, 4:8], neglam, None, ALU.mult, reads=[sml_s, lt_s],
                     writes=[sml_s])
                ob, obs = obr.next()
                for qs_ in range(4):
                    yield
                    P.ts(DVE, ob[:, qs_, :], orw[:, qs_, 0:128], sml[:, qs_:qs_ + 1], None, ALU.mult,
                         reads=[ors, sml_s], writes=[obs])
                    P.stt(ob[:, qs_, :], orw[:, 4 + qs_, 0:128], sml[:, 8 + qs_:9 + qs_], ob[:, qs_, :],
                          ALU.mult, ALU.add, reads=[ors, sml_s, obs], writes=[obs])
                    P.stt(pjunk[:], ob[:, qs_, :], 1.0, ob[:, qs_, :], ALU.mult, ALU.mult,
                          accum=sml[:, 12 + qs_:13 + qs_], reads=[obs], writes=[pjunk_s, sml_s])
                P.ts(POOL, sml[:, 16:20], sml[:, 12:16], 1.0 / 128, EPS, ALU.mult, ALU.add, reads=[sml_s],
                     writes=[sml_s])
                P.tt(POOL, sml[:, 16:20], sml[:, 16:20], cst["mhalf"][:, 0:4], ALU.pow,
                     reads=[sml_s, cst["mhalf_s"]], writes=[sml_s])
                yield
                yield
                yb, ybs = ybr.next()
                for qs_ in range(4):
                    if qs_ == 2:
                        yield
                    P.ts(DVE, yb[:, qs_, :], ob[:, qs_, :], sml[:, 16 + qs_:17 + qs_], None, ALU.mult,
                         reads=[obs, sml_s], writes=[ybs])
                yield
                post_b(yb, ybs, wi_, c)

            def post_b(yb, ybs, wi_, c):
                w = W[wi_]
                pt, pts = ptr.next()
                for qs_ in range(4):
                    P.tr(pt[:, qs_ * 128:(qs_ + 1) * 128], yb[:, qs_, :], idb[:], reads=[ybs, idb_s],
                         writes=[pts])
                ys, yss = ysr.next()
                P.cp(DVE, ys[:], pt, reads=[pts], writes=[yss])
                tt0 = w["t0"] + 512 * c
                P.dma(SP, yAT[w["h"], :, tt0:tt0 + 512], ys[:], yss, reads=[yss])

            ensure_kv(0)
            pend_post = None
            since_post = 0
            pendq = []
            for it in flat[0:2]:
                ensure_q(*it)
                pendq.append(qk(*it))
            for ii, (wi_, c, j) in enumerate(flat):
                w = W[wi_]
                kt, kts, va, vas = w["kv"]
                pscT, pscS, bias = pendq.pop(0)
                et, ets = etr.next()
                if bias is None:
                    P.act(et[:], pscT[:], AF.Exp, scale=0.125, reads=[pscS], writes=[ets])
                else:
                    P.act(et[:], pscT[:], AF.Exp, bias=bias, scale=0.125, reads=[pscS, blr_s], writes=[ets])
                if ii + 2 < len(flat):
                    ensure_q(*flat[ii + 2])
                    pendq.append(qk(*flat[ii + 2]))
                first, last_ = w["first"][c], w["last"][c]
                for a_ in range(8):
                    m, qs_ = a_ // 4, a_ % 4
                    ap_, as_ = acc_ap(a_)
                    P.mm(ap_, et[:, m * 512 + qs_ * 128:m * 512 + (qs_ + 1) * 128], va[:, j, 0:129],
                         start=(j == first and a_ % 3 == 0), stop=(j == last_), reads=[ets, vas],
                         writes=[as_], skip=True)
                if pend_post is not None:
                    if next(pend_post, "done") == "done":
                        pend_post = None
                if j == last_:
                    if pend_post is not None:
                        for _ in pend_post:
                            pass
                    pend_post = post(wi_, c)
                    next(pend_post)
                    qbufs.pop((wi_, c), None)
                if wi_ + 1 < len(work) and W[wi_ + 1]["kv"] is None and c == min(1, w["nq"] - 1) \
                        and j == min(first + 6, last_):
                    ensure_kv(wi_ + 1)
            if pend_post is not None:
                for _ in pend_post:
                    pass
            end_phase(st)

            st = phase_scope()
            idb, idb_s = load_ident(st, "pb")
            bb, bb_s = small(st, "pbbb", c_bb, [128, 3 * 8 * 128])
            skt, skt_s = small(st, "pbsink", p_sink[l], [128, 8])
            esink = sb(st, "pbesink", [128, 8], F32)
            esink_s = slot(st, "pbesink")
            P.act(esink[:], skt[:], AF.Exp, reads=[skt_s], writes=[esink_s])
            ktbs = [sb(st, "pbkt%d" % i, [128, 2, SMAX], BF16) for i in range(2)]
            ktb_ss = [slot(st, "pbkt%d" % i) for i in range(2)]
            ktz_s = slot(st, "pbktz")
            vbs = [sb(st, "pbvb%d" % i, [128, NKMAX, 2, 66], BF16) for i in range(2)]
            vb_ss = [slot(st, "pbvb%d" % i) for i in range(2)]
            vb1_s = slot(st, "pbvb1")
            for i in range(2):
                P.memset(POOL, ktbs[i][64:128, :, :], 0.0, writes=[ktz_s])
                P.memset(POOL, vbs[i][:, :, :, 64:66], 1.0, writes=[vb1_s])
            qbr = Ring(P, "pbqb", [sb(st, "pbqb%d" % i, [128, 4, 8, 128], BF16) for i in range(2)])
            qbz_s = slot(st, "pbqbz")
            for i in range(2):
                P.memset(POOL, qbr.tiles[i][64:128, :, :, :], 0.0, writes=[qbz_s])
            etbr = Ring(P, "pbet", [sb(st, "pbet%d" % i, [128, 512], BF16) for i in range(4)])
            ybtr = Ring(P, "pbyb", [sb(st, "pbyb%d" % i, [128, 512], BF16) for i in range(2)])
            ysbr = Ring(P, "pbys", [sb(st, "pbys%d" % i, [128, 4, 512], BF16) for i in range(2)])
            smbr = Ring(P, "pbsm", [sb(st, "pbsm%d" % i, [128, 16], F32) for i in range(2)])
            for r in (qbr, etbr, ybtr, ysbr, smbr):
                st.slots.extend(r.slots)
            psbr = Ring(P, "pbpsb", [ps(st, "pbpsb%d" % i, [128, 512], F32) for i in range(3)])
            paccB = [[ps(st, "pbpacc%d_%d" % (k, i), [128, 4, 65], F32) for i in range(2)] for k in range(2)]
            paccB_s = [[slot(st, "pbpacc%d_%d" % (k, i)) for i in range(2)] for k in range(2)]
            ptr = Ring(P, "pbptr", [ps(st, "pbptr0", [128, 512], BF16)[:]])
            st.slots.extend(psbr.slots)
            st.slots.extend(ptr.slots)

            def pb_post(k, ysb, ysbs, t, flush):
                smb, smb_s = smbr.next()
                for g in range(2):
                    P.tt(DVE, smb[:, 4 * g:4 * g + 4], paccB[k][g][:, :, 64], esink[:, 4 * g:4 * g + 4],
                         ALU.add, reads=[paccB_s[k][g], esink_s], writes=[smb_s])
                P.recip(smb[:, 8:16], smb[:, 0:8], reads=[smb_s], writes=[smb_s])
                ybt, ybts = ybtr.next()
                for g in range(2):
                    P.tt(DVE, ybt[:, g * 256:(g + 1) * 256].rearrange("p (h d) -> p h d", d=64),
                         paccB[k][g][:, :, 0:64],
                         smb[:, 8 + 4 * g:12 + 4 * g].unsqueeze(2).to_broadcast([128, 4, 64]), ALU.mult,
                         reads=[paccB_s[k][g], smb_s], writes=[ybts])
                pt, pts = ptr.next()
                for fc in range(4):
                    P.tr(pt[:, fc * 128:(fc + 1) * 128], ybt[:, fc * 128:(fc + 1) * 128], idb[:],
                         reads=[ybts, idb_s], writes=[pts])
                P.cp(ACT, ysb[:, :, t * 128:(t + 1) * 128], pt.rearrange("p (f q) -> p f q", q=128),
                     reads=[pts], writes=[ysbs])
                if flush is not None:
                    P.dma(SP, flush, ysb[:], ysbs, reads=[ysbs])

            units = []
            tile_no = 0
            for si, S in enumerate(seqs):
                nk = S // 128
                for c in range(S // 512):
                    for t in range(4):
                        jq = 4 * c + t
                        rels = [r for r in range(3) if 0 <= jq + r - 1 < nk]
                        for g in range(2):
                            for ri, r in enumerate(rels):
                                units.append(dict(si=si, c=c, t=t, g=g, ri=ri, r=r, jk=jq + r - 1,
                                                  nrel=len(rels), k=tile_no % 2,
                                                  last=(g == 1 and ri == len(rels) - 1)))
                        tile_no += 1
            state = dict(si=None, c=None, qb=None, qbs=None, ysb=None, ysbs=None)

            def pb_qk(u):
                si, c = u["si"], u["c"]
                S = seqs[si]
                t0 = seq_off[si]
                nk = S // 128
                ktb, ktb_s, vb, vb_s = ktbs[si % 2], ktb_ss[si % 2], vbs[si % 2], vb_ss[si % 2]
                if state["si"] != si:
                    for g in range(2):
                        P.dma(SP, ktb[0:64, g, 0:S], qkB[8 + g, :, t0:t0 + S], ktb_s, writes=[ktb_s])
                        for j0 in range(0, nk, 8):
                            nj = min(8, nk - j0)
                            P.dma(SP, vb[:, j0:j0 + nj, g, 0:64],
                                  vAB[t0 + 128 * j0:t0 + 128 * (j0 + nj), 512 + 64 * g:576 + 64 * g].rearrange(
                                      "(j p) e -> p j e", p=128), vb_s, writes=[vb_s])
                    state["si"] = si
                    state["c"] = None
                if state["c"] != c:
                    qb, qbs = qbr.next()
                    for t in range(4):
                        P.dma(SP, qb[0:64, t, :, :],
                              qkB[0:8, :, t0 + 512 * c + 128 * t:t0 + 512 * c + 128 * (t + 1)].rearrange(
                                  "h d q -> d h q"), qbs, writes=[qbs])
                    state["qb"], state["qbs"] = qb, qbs
                    state["c"] = c
                qb, qbs = state["qb"], state["qbs"]
                g, r, jk, t = u["g"], u["r"], u["jk"], u["t"]
                pt, pts = psbr.next()
                P.mm(pt[:], ktb[:, g, jk * 128:(jk + 1) * 128], qb[:, t, 4 * g:4 * g + 4, :],
                     reads=[ktb_s, ktz_s, qbs, qbz_s], writes=[pts])
                P.tt(DVE, pt[:], pt[:], bb[:, (r * 8 + 4 * g) * 128:(r * 8 + 4 * g + 4) * 128],
                     ALU.add, reads=[pts, bb_s], writes=[pts])
                return pt, pts

            pending = None
            pq = [pb_qk(u) for u in units[0:2]]
            cur_ysb = {}
            for i, u in enumerate(units):
                pt, pts = pq.pop(0)
                et, ets = etbr.next()
                P.act(et[:], pt[:], AF.Exp, scale=0.125, reads=[pts], writes=[ets])
                if i + 2 < len(units):
                    pq.append(pb_qk(units[i + 2]))
                k, g, ri, jk = u["k"], u["g"], u["ri"], u["jk"]
                vb, vb_s = vbs[u["si"] % 2], vb_ss[u["si"] % 2]
                for hh in range(4):
                    P.mm(paccB[k][g][:, hh, :], et[:, hh * 128:(hh + 1) * 128], vb[:, jk, g, 0:65],
                         start=(ri == 0 and hh == 0), stop=(ri == u["nrel"] - 1), reads=[ets, vb_s, vb1_s],
                         writes=[paccB_s[k][g]], skip=True)
                if u["last"]:
                    key = (u["si"], u["c"])
                    if key not in cur_ysb:
                        cur_ysb.clear()
                        cur_ysb[key] = ysbr.next()
                    ysb, ysbs = cur_ysb[key]
                    if pending is not None:
                        pb_post(*pending)
                    flush = None
                    if u["t"] == 3:
                        tt0 = seq_off[u["si"]] + 512 * u["c"]
                        flush = yBT[:, :, tt0:tt0 + 512].rearrange("f p t -> p f t")
                    pending = (k, ysb, ysbs, u["t"], flush)
            pb_post(*pending)
            end_phase(st)

            st = phase_scope()
            idb, idb_s = load_ident(st, "pc")
            cst = mk_mhalf(st, "pc")
            stage = mk_stage(st, "pc", n=3)
            subc, subc_s = small(st, "pcsub", p_sub[l], [128, 1])
            wpa = sb(st, "pcwpa", [128, 4, D], BF16)
            wpa_s = slot(st, "pcwpa")
            wpb = sb(st, "pcwpb", [128, 4, D], BF16)
            wpb_s = slot(st, "pcwpb")
            wo = sb(st, "pcwo", [128, 8, D], BF16)
            wo_s = slot(st, "pcwo")
            subc4 = sb(st, "pcsub4", [128, 4], F32)
            subc4_s = slot(st, "pcsub4")
            for i in range(4):
                P.cp(DVE, subc4[:, i:i + 1], subc[:], reads=[subc_s], writes=[subc4_s])
            load_weight(st, "pc", wpa, wpa_s, w_pa[l], 4, D, subc4, subc4_s, s2=(1.0 - lam_init), stage=stage)
            load_weight(st, "pc", wpb, wpb_s, w_pb[l], 4, D, stage=stage)
            load_weight(st, "pc", wo, wo_s, w_out[l], 8, D, stage=stage)
            fcol, fcol_s = small(st, "pcfcol", p_fn[l], [128, 8])
            yar = Ring(P, "pcya", [sb(st, "pcya%d" % i, [128, 4, 512], BF16) for i in range(2)])
            ybr2 = Ring(P, "pcyb", [sb(st, "pcyb%d" % i, [128, 4, 512], BF16) for i in range(2)])
            ggr = Ring(P, "pcgg", [sb(st, "pcgg%d" % i, [128, 16, 512], BF16) for i in range(2)])
            xr = Ring(P, "pcx", [sb(st, "pcx%d" % i, [128, 4, D], F32) for i in range(2)])
            xts_all = [[slot(st, "pcx%d_%d" % (i, t)) for t in range(4)] for i in range(2)]
            t1r = Ring(P, "pct1", [sb(st, "pct1%d" % i, [128, 512], F32) for i in range(2)])
            t2r = Ring(P, "pct2", [sb(st, "pct2%d" % i, [128, 512], F32) for i in range(2)])
            for r in (yar, ybr2, ggr, xr, t1r, t2r):
                st.slots.extend(r.slots)
            mT = sb(st, "pcmT", [128, 8, 512], BF16)
            mT_ss = [slot(st, "pcmT%d" % i) for i in range(8)]
            h2b = sb(st, "pch2", [128, 4, D], BF16)
            h2b_s = slot(st, "pch2")
            h2Ts = sb(st, "pch2T", [128, 8, 512], BF16)
            h2Ts_s = slot(st, "pch2T")
            junk = sb(st, "pcjunk", [128, D], BF16)
            junk_s = slot(st, "pcjunk", waw=True)
            ssq = sb(st, "pcssq", [128, 8], F32)
            ssq_s = slot(st, "pcssq")
            rstd = sb(st, "pcrstd", [128, 8], F32)
            rstd_s = slot(st, "pcrstd")
            ptr = Ring(P, "pcptr", [ps(st, "pcptr%d" % i, [128, 512], BF16)[:] for i in range(2)])
            pp = Ring(P, "pcpp", [ps(st, "pcpp%d" % i, [128, 512], F32) for i in range(6)])
            st.slots.extend(ptr.slots)
            st.slots.extend(pp.slots)

            def pc_load(c):
                t0 = 512 * c
                ya, yas = yar.next()
                yb_, ybs_ = ybr2.next()
                gg, ggs = ggr.next()
                xt, xs = xr.next()
                xs = xts_all[xr.i]
                P.dma(SP, ya[:], yAT[:, :, t0:t0 + 512].rearrange("f p t -> p f t"), yas, writes=[yas])
                P.dma(SP, yb_[:], yBT[:, :, t0:t0 + 512].rearrange("f p t -> p f t"), ybs_, writes=[ybs_])
                P.dma(SP, gg[:], gT[:, :, t0:t0 + 512].rearrange("c p t -> p c t"), ggs, writes=[ggs])
                P.dma(SP, xt[:], xsrc[t0:t0 + 512, :].rearrange("(t p) d -> p t d", p=128), xs[0], writes=xs)
                return ya, yas, yb_, ybs_, gg, ggs, xt, xs

            pc_pending = [None]

            def pc_flush():
                if pc_pending[0] is None:
                    return
                tp0 = pc_pending[0]
                transposes(h2b, h2b_s, h2Ts, h2Ts_s, ptr, idb, idb_s)
                P.dma(SP, h2T[:, :, tp0:tp0 + 512].rearrange("f p t -> p f t"), h2Ts[:], h2Ts_s, reads=[h2Ts_s])
                pc_pending[0] = None

            nxt = pc_load(0)
            for c in range(NCH):
                t0 = 512 * c
                ya, yas, yb_, ybs_, gg, ggs, xt, xs = nxt
                if c + 1 < NCH:
                    nxt = pc_load(c + 1)
                for cc in range(8):
                    pa_, pas_ = pp.next()
                    for fc in range(4):
                        P.mm(pa_[:], wpa[:, fc, cc * 128:(cc + 1) * 128], ya[:, fc, :], start=(fc == 0),
                             stop=(fc == 3), reads=[wpa_s, yas], writes=[pas_])
                    pb_, pbs_ = pp.next()
                    for fc in range(4):
                        P.mm(pb_[:], wpb[:, fc, cc * 128:(cc + 1) * 128], yb_[:, fc, :], start=(fc == 0),
                             stop=(fc == 3), reads=[wpb_s, ybs_], writes=[pbs_])
                    t1, t1s = t1r.next()
                    t2, t2s = t2r.next()
                    P.tt(DVE, t1[:], pa_[:], gg[:, cc, :], ALU.mult, reads=[pas_, ggs], writes=[t1s])
                    P.tt(DVE, t2[:], pb_[:], gg[:, 8 + cc, :], ALU.mult, reads=[pbs_, ggs], writes=[t2s])
                    P.tt(POOL, mT[:, cc, :], t1[:], t2[:], ALU.add, reads=[t1s, t2s], writes=[mT_ss[cc]])
                    if cc == 7:
                        pc_flush()
                for t in range(4):
                    for hh in range(2):
                        pt, pts = pp.next()
                        for fc in range(8):
                            P.mm(pt[:], mT[:, fc, t * 128:(t + 1) * 128], wo[:, fc, hh * 512:(hh + 1) * 512],
                                 start=(fc == 0), stop=(fc == 7), reads=[mT_ss[fc], wo_s], writes=[pts])
                        P.tt(DVE, xt[:, t, hh * 512:(hh + 1) * 512], pt[:], xt[:, t, hh * 512:(hh + 1) * 512],
                             ALU.add, reads=[pts, xs[t]], writes=[xs[t]])
                    P.dma(SP, xmid[t0 + 128 * t:t0 + 128 * (t + 1), :], xt[:, t, :], xs[t], reads=[xs[t]])
                    rms_stats(cst, [xt[:, t, :]], xs[t], ssq[:, t:t + 1], ssq_s, rstd[:, t:t + 1], rstd_s, junk,
                              junk_s, 1, D)
                    P.act(h2b[:, t, :], xt[:, t, :], AF.Copy, scale=rstd[:, t:t + 1], reads=[xs[t], rstd_s],
                          writes=[h2b_s])
                pc_pending[0] = t0
            pc_flush()
            end_phase(st)

            st = phase_scope()
            cst = mk_mhalf(st, "pf")
            stage = mk_stage(st, "pf", n=3)
            fcol, fcol_s = small(st, "pffcol", p_fn[l], [128, 8])
            cw, cw_s = small(st, "pfcw", p_cw[l], [128, 3 * NFC])
            cb, cb_s = small(st, "pfcb", p_cb[l], [128, NFC])
            wu = sb(st, "pfwu", [128, 8, 2 * DFF], BF16)
            _r = []
            for b_ in range(6):
                _r.append((512 * b_, min(512 * b_ + 512, DFF)))
                _r.append((DFF + 512 * b_, min(DFF + 512 * b_ + 512, 2 * DFF)))
            wu_w = WSlots(st, "pfwu", _r)
            wd = sb(st, "pfwd", [128, NFC, D], BF16)
            wd_s = slot(st, "pfwd")
            if last:
                gfin, gfin_s = small(st, "pfgfin", p_fin, [128, D])
            hx = sb(st, "pfhx", [128, 8, 514], BF16)
            hx_s = slot(st, "pfhx")
            xq = [sb(st, "pfx%d" % i, [128, D], F32) for i in range(4)]
            xq_s = [slot(st, "pfx%d" % i) for i in range(4)]
            uT = sb(st, "pfuT", [128, NFC, 512], BF16)
            uT_ss = [slot(st, "pfuT%d" % i) for i in range(NFC)]
            cr = Ring(P, "pfc", [sb(st, "pfc%d" % i, [128, 512], F32) for i in range(2)])
            grr = Ring(P, "pfg", [sb(st, "pfg%d" % i, [128, 512], F32) for i in range(2)])
            st.slots.extend(cr.slots)
            st.slots.extend(grr.slots)
            junk = sb(st, "pfjunk", [128, D], BF16)
            junk_s = slot(st, "pfjunk", waw=True)
            ssq = sb(st, "pfssq", [128, 8], F32)
            ssq_s = slot(st, "pfssq")
            rstd = sb(st, "pfrstd", [128, 8], F32)
            rstd_s = slot(st, "pfrstd")
            pp = Ring(P, "pfpp", [ps(st, "pfpp%d" % i, [128, 512], F32) for i in range(7)])
            st.slots.extend(pp.slots)
            phal = ps(st, "pfhal", [128, 512], F32)
            phr = Ring(P, "pfhal", [phal[:, 0:2]])
            st.slots.extend(phr.slots)
            def load_hx(c):
                t0 = 512 * c
                si, cs = chunk_seq[c]
                first_c = (cs == 0)
                last_c = (cs == seqs[si] // 512 - 1)
                lo = 0 if first_c else 1
                hi = 0 if last_c else 1
                if first_c:
                    P.memset(POOL, hx[:, :, 0:1], 0.0, writes=[hx_s])
                if last_c:
                    P.memset(POOL, hx[:, :, 513:514], 0.0, writes=[hx_s])
                P.dma(SP, hx[:, :, 1 - lo:513 + hi], h2T[:, :, t0 - lo:t0 + 512 + hi].rearrange("f p t -> p f t"),
                      hx_s, writes=[hx_s])

            load_hx(0)
            load_weight(st, "pf", wu, wu_w, w_up[l], 8, 2 * DFF, fcol, fcol_s, stage=stage)
            load_weight(st, "pf", wd, wd_s, w_dn[l], NFC, D, stage=stage)
            for c in range(NCH):
                t0 = 512 * c
                for t in range(4):
                    P.dma(dma_eng(), xq[t][:], xmid[t0 + 128 * t:t0 + 128 * (t + 1), :], xq_s[t], writes=[xq_s[t]])
                for cc in range(NFC):
                    pa_, pas_ = pp.next()
                    ph_, phs_ = phr.next()
                    for kc in range(8):
                        P.mm(pa_[:], wu[:, kc, cc * 128:(cc + 1) * 128], hx[:, kc, 1:513], start=(kc == 0),
                             stop=(kc == 7), reads=[wu_w.s(cc * 128), hx_s], writes=[pas_])
                    pv_, pvs_ = pp.next()
                    for kc in range(8):
                        P.mm(pv_[:], wu[:, kc, DFF + cc * 128:DFF + (cc + 1) * 128], hx[:, kc, 1:513],
                             start=(kc == 0), stop=(kc == 7), reads=[wu_w.s(DFF + cc * 128), hx_s], writes=[pvs_])
                    for kc in range(8):
                        P.mm(ph_, wu[:, kc, cc * 128:(cc + 1) * 128], hx[:, kc, 0:514:513], start=(kc == 0),
                             stop=(kc == 7), reads=[wu_w.s(cc * 128), hx_s], writes=[phs_])
                    ct, cs_ = cr.next()
                    w0 = cw[:, 0 * NFC + cc:0 * NFC + cc + 1]
                    w1 = cw[:, 1 * NFC + cc:1 * NFC + cc + 1]
                    w2 = cw[:, 2 * NFC + cc:2 * NFC + cc + 1]
                    P.act(ct[:], pa_[:], AF.Identity, bias=cb[:, cc:cc + 1], scale=w1, reads=[pas_, cb_s, cw_s],
                          writes=[cs_])
                    P.stt(ct[:, 1:512], pa_[:, 0:511], w0, ct[:, 1:512], ALU.mult, ALU.add,
                          reads=[pas_, cw_s, cs_], writes=[cs_])
                    P.stt(ct[:, 0:511], pa_[:, 1:512], w2, ct[:, 0:511], ALU.mult, ALU.add,
                          reads=[pas_, cw_s, cs_], writes=[cs_])
                    P.stt(ct[:, 0:1], ph_[:, 0:1], w0, ct[:, 0:1], ALU.mult, ALU.add, reads=[phs_, cw_s, cs_],
                          writes=[cs_])
                    P.stt(ct[:, 511:512], ph_[:, 1:2], w2, ct[:, 511:512], ALU.mult, ALU.add,
                          reads=[phs_, cw_s, cs_], writes=[cs_])
                    gt_, gs_ = grr.next()
                    P.act(gt_[:], ct[:], AF.Gelu_apprx_tanh, reads=[cs_], writes=[gs_])
                    P.tt(DVE, uT[:, cc, :], pv_[:], gt_[:], ALU.mult, reads=[pvs_, gs_], writes=[uT_ss[cc]])
                if c + 1 < NCH:
                    load_hx(c + 1)
                for t in range(4):
                    for hh in range(2):
                        pt, pts = pp.next()
                        for fc in range(NFC):
                            P.mm(pt[:], uT[:, fc, t * 128:(t + 1) * 128], wd[:, fc, hh * 512:(hh + 1) * 512],
                                 start=(fc == 0), stop=(fc == NFC - 1), reads=[uT_ss[fc], wd_s], writes=[pts])
                        P.tt(DVE, xq[t][:, hh * 512:(hh + 1) * 512], pt[:], xq[t][:, hh * 512:(hh + 1) * 512],
                             ALU.add, reads=[pts, xq_s[t]], writes=[xq_s[t]])
                    if not last:
                        P.dma(dma_eng(), xres[t0 + 128 * t:t0 + 128 * (t + 1), :], xq[t][:], xq_s[t],
                              reads=[xq_s[t]])
                    else:
                        rms_stats(cst, [xq[t][:]], xq_s[t], ssq[:, t:t + 1], ssq_s, rstd[:, t:t + 1], rstd_s,
                                  junk, junk_s, 1, D)
                        P.stt(xq[t][:], xq[t][:], rstd[:, t:t + 1], gfin[:], ALU.mult, ALU.mult,
                              reads=[xq_s[t], rstd_s, gfin_s], writes=[xq_s[t]])
                        P.dma(dma_eng(), yout[t0 + 128 * t:t0 + 128 * (t + 1), :], xq[t][:], xq_s[t],
                              reads=[xq_s[t]])
            end_phase(st)
    return nc


def _bf16_round(x):
    x = np.asarray(x, np.float32)
    u = x.view(np.uint32).astype(np.uint64)
    r = ((u + 0x7FFF + ((u >> 16) & 1)) & 0xFFFF0000).astype(np.uint32)
    return r.view(np.float32)


def make_consts():
    c = {}
    c["c_ident"] = np.eye(128, dtype=np.float32)
    qi = np.arange(512, dtype=np.float64)
    aug = np.zeros((4, 3, 3, 2, 512), np.float32)
    for h in range(4):
        for v, sgn in enumerate((-1.0, 1.0)):
            val = (sgn * 8.0 * SLOPES_A[h] * qi).astype(np.float32)
            hi = _bf16_round(val)
            mid = _bf16_round(val - hi)
            lo = _bf16_round(val - hi - mid)
            for r, part in enumerate((hi, mid, lo)):
                aug[h, r, v, :, :] = part[None, :]
    c["c_augq"] = aug
    ki = np.arange(128, dtype=np.float64)[:, None]
    m = np.arange(64, dtype=np.float64)[None, :]
    blr = np.zeros((128, 4, 2, 64), np.float32)
    for h in range(4):
        blr[:, h, 0, :] = SLOPES_A[h] * (ki - 128.0 * m)
        blr[:, h, 1, :] = -SLOPES_A[h] * (128.0 * m + ki)
    c["c_blr"] = blr.reshape(128, -1)
    xx = np.arange(896, dtype=np.float64)[None, :]
    toep = np.zeros((128, 4, 896), np.float32)
    for h in range(4):
        toep[:, h, :] = -8.0 * SLOPES_A[h] * np.abs(xx - 384.0 - ki)
    c["c_toep"] = toep.reshape(128, -1)
    qq = np.arange(128, dtype=np.float64)[None, :]
    bb = np.zeros((128, 3, 8, 128), np.float32)
    for r in range(3):
        dist = np.abs(128.0 * (r - 1) + ki - qq)
        for h in range(8):
            bb[:, r, h, :] = np.where(dist <= 128.0, -8.0 * SLOPES_B[h] * dist, 8.0 * NEGBIG)
    c["c_bb"] = bb.reshape(128, -1)
    return c


def layout_params(inp):
    f = lambda a: np.ascontiguousarray(np.asarray(a, np.float32))
    p = {}
    p["p_an"] = f(np.asarray(inp["attn_norm"]).reshape(DEPTH, 8, 128).transpose(0, 2, 1))
    p["p_fn"] = f(np.asarray(inp["ffn_norm"]).reshape(DEPTH, 8, 128).transpose(0, 2, 1))
    p["p_gb"] = f(np.asarray(inp["gate_bias"]).reshape(DEPTH, 16, 128).transpose(0, 2, 1))
    lam = np.concatenate([np.asarray(inp[k]) for k in ("lambda_q1", "lambda_k1", "lambda_q2", "lambda_k2")],
                         axis=1)
    p["p_lam"] = f(np.broadcast_to(lam[:, None, :], (DEPTH, 128, 256)))
    p["p_sub"] = f(np.asarray(inp["subln"]).reshape(DEPTH, 128, 1))
    p["p_sink"] = f(np.broadcast_to(np.asarray(inp["sink"])[:, None, :], (DEPTH, 128, 8)))
    p["p_cw"] = f(np.asarray(inp["conv_w"]).reshape(DEPTH, 3, NFC, 128).transpose(0, 3, 1, 2).reshape(
        DEPTH, 128, 3 * NFC))
    p["p_cb"] = f(np.asarray(inp["conv_b"]).reshape(DEPTH, NFC, 128).transpose(0, 2, 1))
    p["p_fin"] = f(np.broadcast_to(np.asarray(inp["final_norm"])[None, :], (128, D)))
    for k in ("w_in", "w_proj_a", "w_proj_b", "w_out", "w_up", "w_down"):
        p[k] = f(inp[k])
    return p


_NC_CACHE = {}


def run(core_x, inp, seqs, n_layers=DEPTH):
    key = (tuple(seqs), n_layers)
    if key not in _NC_CACHE:
        _NC_CACHE[key] = build(list(seqs), n_layers)
    nc = _NC_CACHE[key]
    shared = dict(make_consts())
    shared.update(layout_params(inp))
    in_maps = []
    for x in core_x:
        m = dict(shared)
        m["xin"] = np.ascontiguousarray(x, dtype=np.float32)
        in_maps.append(m)
    res = run_bass_kernel_spmd(nc, in_maps, core_ids=list(range(len(core_x))))
    return [r["yout"] for r in res.results]


def kernel(**inputs):
    xp = np.asarray(inputs["x_prompt"], np.float32)
    xs = np.asarray(inputs["x_sample"], np.float32)
    nb_p = xp.shape[0] // N_CORES
    nb_s = xs.shape[0] // N_CORES
    seqs = [xp.shape[1]] * nb_p + [xs.shape[1]] * nb_s
    core_x = []
    for i in range(N_CORES):
        parts = [xp[i * nb_p + b] for b in range(nb_p)] + [xs[i * nb_s + b] for b in range(nb_s)]
        core_x.append(np.concatenate(parts, axis=0))
    outs = run(core_x, inputs, seqs)
    yp = np.empty_like(xp)
    ys = np.empty_like(xs)
    for i in range(N_CORES):
        o = outs[i]
        off = 0
        for b in range(nb_p):
            yp[i * nb_p + b] = o[off:off + xp.shape[1]]
            off += xp.shape[1]
        for b in range(nb_s):
            ys[i * nb_s + b] = o[off:off + xs.shape[1]]
            off += xs.shape[1]
    return (yp, ys)
```
